# Optimizing a Trainium2 kernel written in Bass

```python
import math
import jax
import jax.numpy as jnp
from jax import lax
import numpy as np

D_MODEL = 1024
BATCH = 8
SEQ = 2048
DEPTH = 2
DEC_BATCH = 128
DEC_SEQ = 8
PAST_LEN = 16384
PAGE_SIZE = 128

N_META = 16
POOL_GROUPS = 4
POOL_GROUP_W = D_MODEL // 8
POOL_WIDTH = POOL_GROUPS * POOL_GROUP_W
POOL_WINDOWS = (2, 4, 8, 16)
POOL_STATE = 15
HEAD_K = 128
HEAD_V = 128
N_HEADS = D_MODEL // HEAD_K
KEY_W = N_HEADS * HEAD_K
VAL_W = N_HEADS * HEAD_V
QKV_W = 2 * KEY_W + VAL_W
CONV_W = 4
CHUNK = 64
D_FF = 4 * D_MODEL
N_BRANCH = 2
IN_SPLITS = (POOL_WIDTH, POOL_WIDTH + QKV_W, POOL_WIDTH + QKV_W + N_HEADS, POOL_WIDTH + QKV_W + 2 * N_HEADS, POOL_WIDTH + QKV_W + 2 * N_HEADS + VAL_W)
IN_W = POOL_WIDTH + QKV_W + 2 * N_HEADS + VAL_W + N_BRANCH * D_MODEL
EPS = 1e-6

kernel_name = 'pool_gated_deltanet_hybrid_step'


def rms_norm(x, g):
    xf = x.astype(jnp.float32)
    y = xf * lax.rsqrt(jnp.mean(xf * xf, axis=-1, keepdims=True) + EPS)
    return (y * g.astype(jnp.float32)).astype(x.dtype)


def l2_normalize(x):
    return x * lax.rsqrt(jnp.sum(x * x, axis=-1, keepdims=True) + EPS)


def causal_dwconv(x_ext, w):
    return lax.conv_general_dilated(x_ext, w.astype(x_ext.dtype)[:, None, :], window_strides=(1,), padding='VALID', dimension_numbers=('NWC', 'WIO', 'NWC'), feature_group_count=x_ext.shape[-1])


def pool_mix(ext, pos0, pool_w, pool_scale):
    bsz, lx, _ = ext.shape
    t_len = lx - POOL_STATE
    ef = ext.astype(jnp.float32)
    cs = jnp.concatenate([jnp.zeros((bsz, 1, POOL_WIDTH), jnp.float32), jnp.cumsum(ef, axis=1)], axis=1)
    pos_count = jnp.arange(1, t_len + 1) + pos0
    means = []
    for gi, w in enumerate(POOL_WINDOWS):
        lo, hi = gi * POOL_GROUP_W, (gi + 1) * POOL_GROUP_W
        s = cs[:, POOL_STATE + 1:POOL_STATE + 1 + t_len, lo:hi] - cs[:, POOL_STATE + 1 - w:POOL_STATE + 1 - w + t_len, lo:hi]
        cnt = jnp.minimum(pos_count, w).astype(jnp.float32)
        means.append(s / cnt[None, :, None])
    mean = jnp.stack(means, axis=2)
    tok = ef[:, POOL_STATE:].reshape(bsz, t_len, POOL_GROUPS, POOL_GROUP_W)
    y = jnp.einsum('btgc,gcd->btgd', mean - tok, pool_w.astype(jnp.float32)) * pool_scale.astype(jnp.float32).reshape(POOL_GROUPS, POOL_GROUP_W)
    return y.reshape(bsz, t_len, POOL_WIDTH).astype(ext.dtype)


def gdn_chunked(q, k, v, g, beta, s0, chunk):
    bsz, t_len, nh, dk = q.shape
    dv = v.shape[-1]
    n = -(-t_len // chunk)
    pad = n * chunk - t_len

    def blk(a):
        a = jnp.pad(a, [(0, 0), (0, pad)] + [(0, 0)] * (a.ndim - 2))
        a = a.reshape((bsz, n, chunk) + a.shape[2:])
        return jnp.moveaxis(a, (1, 3), (0, 2))

    qc, kc, vc, gc, bc = blk(q), blk(k), blk(v), blk(g), blk(beta)
    gc = jnp.cumsum(gc, axis=-1)
    tril = jnp.tril(jnp.ones((chunk, chunk), bool))
    strict = jnp.tril(jnp.ones((chunk, chunk), bool), -1)
    diff = gc[..., :, None] - gc[..., None, :]
    decay = jnp.where(tril, jnp.exp(jnp.where(tril, diff, 0.0)), 0.0)
    kb = kc * bc[..., None]
    lmat = jnp.where(strict, jnp.einsum('nbhik,nbhjk->nbhij', kb, kc) * decay, 0.0)
    amat = lmat + jnp.eye(chunk, dtype=jnp.float32)
    rhs = jnp.concatenate([vc * bc[..., None], kb * jnp.exp(gc)[..., None]], axis=-1)
    sol = lax.linalg.triangular_solve(amat, rhs, left_side=True, lower=True, unit_diagonal=True)
    u_c, w_c = sol[..., :dv], sol[..., dv:]

    def step(s, xs):
        q_i, k_i, u_i, w_i, g_i, d_i = xs
        attn = jnp.einsum('bhik,bhjk->bhij', q_i, k_i) * d_i
        v_new = u_i - jnp.einsum('bhck,bhkv->bhcv', w_i, s)
        o = jnp.einsum('bhck,bhkv->bhcv', q_i * jnp.exp(g_i)[..., None], s) + jnp.einsum('bhij,bhjv->bhiv', attn, v_new)
        g_last = g_i[..., -1:]
        s = s * jnp.exp(g_last)[..., None] + jnp.einsum('bhck,bhcv->bhkv', k_i * jnp.exp(g_last - g_i)[..., None], v_new)
        return s, o

    s_fin, o = lax.scan(step, s0, (qc, kc, u_c, w_c, gc, decay))
    o = jnp.moveaxis(o, (0, 2), (1, 3)).reshape(bsz, n * chunk, nh, dv)[:, :t_len]
    return o, s_fin


def delta_mixer(qkv_ext, b, a, z, s0, n_meta, conv_w, a_log, dt_bias, o_norm_g):
    bsz, t_len = b.shape[0], b.shape[1]
    c = jax.nn.silu(causal_dwconv(qkv_ext, conv_w).astype(jnp.float32))
    q = l2_normalize(c[..., :KEY_W].reshape(bsz, t_len, N_HEADS, HEAD_K)) * (HEAD_K ** -0.5)
    k = l2_normalize(c[..., KEY_W:2 * KEY_W].reshape(bsz, t_len, N_HEADS, HEAD_K))
    v = c[..., 2 * KEY_W:].reshape(bsz, t_len, N_HEADS, HEAD_V)
    beta = jax.nn.sigmoid(b.astype(jnp.float32))
    g = -jnp.exp(a_log.astype(jnp.float32)) * jax.nn.softplus(a.astype(jnp.float32) + dt_bias.astype(jnp.float32))
    if n_meta > 0:
        o_m, s_m = gdn_chunked(q[:, :n_meta], k[:, :n_meta], v[:, :n_meta], g[:, :n_meta], beta[:, :n_meta], s0, n_meta)
        o_r, s_new = gdn_chunked(q[:, n_meta:], k[:, n_meta:], v[:, n_meta:], g[:, n_meta:], beta[:, n_meta:], s_m, CHUNK)
        o = jnp.concatenate([o_m, o_r], axis=1)
    else:
        o, s_new = gdn_chunked(q, k, v, g, beta, s0, min(CHUNK, t_len))
    o = o * lax.rsqrt(jnp.mean(o * o, axis=-1, keepdims=True) + EPS) * o_norm_g.astype(jnp.float32)
    o = o * jax.nn.silu(z.astype(jnp.float32).reshape(bsz, t_len, N_HEADS, HEAD_V))
    return o.reshape(bsz, t_len, VAL_W).astype(z.dtype), s_new


def trunk_layer(x, conv_prefix, s0, pool_prefix, pos0, n_meta, lp):
    dt = x.dtype
    h = rms_norm(x, lp['g_pre_mix'])
    proj = h @ lp['w_in'].astype(dt)
    u, qkv, b, a, z, gates = jnp.split(proj, list(IN_SPLITS), axis=-1)
    pool_ext = jnp.concatenate([pool_prefix.astype(dt), u], axis=1)
    a_out = pool_mix(pool_ext, pos0, lp['pool_w'], lp['pool_scale'])
    qkv_ext = jnp.concatenate([conv_prefix.astype(dt), qkv], axis=1)
    d_out, s_new = delta_mixer(qkv_ext, b, a, z, s0, n_meta, lp['conv_w'], lp['a_log'], lp['dt_bias'], lp['o_norm_g'])
    br_pool = (a_out @ lp['w_branch_pool'].astype(dt)).astype(jnp.float32)
    br_delta = (d_out @ lp['w_branch_delta'].astype(dt)).astype(jnp.float32)
    gate_pool, gate_delta = jnp.split(jax.nn.sigmoid(gates.astype(jnp.float32)), N_BRANCH, axis=-1)
    m = (gate_pool * br_pool + gate_delta * br_delta).astype(dt) @ lp['w_out'].astype(dt)
    x = x + rms_norm(m, lp['g_post_mix'])
    h2 = rms_norm(x, lp['g_pre_ffn'])
    f = jnp.square(jax.nn.relu(h2 @ lp['w_up'].astype(dt))) @ lp['w_down'].astype(dt)
    x = x + rms_norm(f, lp['g_post_ffn'])
    return x, qkv_ext[:, -(CONV_W - 1):], s_new.astype(dt), pool_ext[:, -POOL_STATE:]


def setup_inputs(seed: int = 0) -> dict:
    key = jax.random.key(seed)
    ks = jax.random.split(key, 24)
    nrm = jax.random.normal
    f32 = jnp.float32
    dt_init = jnp.exp(jax.random.uniform(ks[10], (DEPTH, N_HEADS), f32, math.log(1e-3), math.log(0.1)))
    return {
        'x_prompt': nrm(ks[0], (BATCH, SEQ, D_MODEL), f32),
        'x_sample': nrm(ks[1], (DEC_BATCH, DEC_SEQ, D_MODEL), f32),
        'state_conv': nrm(ks[2], (DEPTH, DEC_BATCH, CONV_W - 1, QKV_W), f32),
        'state_ssm': nrm(ks[3], (DEPTH, DEC_BATCH, N_HEADS, HEAD_K, HEAD_V), f32) * HEAD_K ** -0.5,
        'state_pool': nrm(ks[4], (DEPTH, DEC_BATCH, POOL_STATE, POOL_WIDTH), f32),
        'meta_tokens': nrm(ks[5], (N_META, D_MODEL), f32),
        'g_pre_mix': 1.0 + 0.02 * nrm(ks[6], (DEPTH, D_MODEL), f32),
        'w_in': nrm(ks[7], (DEPTH, D_MODEL, IN_W), f32) * D_MODEL ** -0.5,
        'conv_w': nrm(ks[8], (DEPTH, CONV_W, QKV_W), f32) * CONV_W ** -0.5,
        'a_log': jnp.log(jax.random.uniform(ks[9], (DEPTH, N_HEADS), f32, 1.0, 16.0)),
        'dt_bias': dt_init + jnp.log(-jnp.expm1(-dt_init)),
        'o_norm_g': 1.0 + 0.02 * nrm(ks[11], (DEPTH, HEAD_V), f32),
        'pool_w': nrm(ks[12], (DEPTH, POOL_GROUPS, POOL_GROUP_W, POOL_GROUP_W), f32) * POOL_GROUP_W ** -0.5,
        'pool_scale': 1.0 + 0.1 * nrm(ks[13], (DEPTH, POOL_WIDTH), f32),
        'w_branch_pool': nrm(ks[14], (DEPTH, POOL_WIDTH, D_MODEL), f32) * POOL_WIDTH ** -0.5,
        'w_branch_delta': nrm(ks[15], (DEPTH, VAL_W, D_MODEL), f32) * VAL_W ** -0.5,
        'w_out': nrm(ks[16], (DEPTH, D_MODEL, D_MODEL), f32) * D_MODEL ** -0.5,
        'g_post_mix': 1.0 + 0.02 * nrm(ks[17], (DEPTH, D_MODEL), f32),
        'g_pre_ffn': 1.0 + 0.02 * nrm(ks[18], (DEPTH, D_MODEL), f32),
        'w_up': nrm(ks[19], (DEPTH, D_MODEL, D_FF), f32) * D_MODEL ** -0.5,
        'w_down': nrm(ks[20], (DEPTH, D_FF, D_MODEL), f32) * D_FF ** -0.5,
        'g_post_ffn': 1.0 + 0.02 * nrm(ks[21], (DEPTH, D_MODEL), f32),
    }


def reference(x_prompt, x_sample, state_conv, state_ssm, state_pool, meta_tokens, g_pre_mix, w_in, conv_w, a_log, dt_bias, o_norm_g, pool_w, pool_scale, w_branch_pool, w_branch_delta, w_out, g_post_mix, g_pre_ffn, w_up, w_down, g_post_ffn):
    dt = x_prompt.dtype
    bp = x_prompt.shape[0]
    xp = jnp.concatenate([jnp.broadcast_to(meta_tokens.astype(dt)[None], (bp, N_META, D_MODEL)), x_prompt], axis=1)
    xs = x_sample
    zero_conv = jnp.zeros((bp, CONV_W - 1, QKV_W), dt)
    zero_pool = jnp.zeros((bp, POOL_STATE, POOL_WIDTH), dt)
    zero_ssm = jnp.zeros((bp, N_HEADS, HEAD_K, HEAD_V), jnp.float32)
    conv_p, ssm_p, pool_p, conv_s, ssm_s, pool_s = [], [], [], [], [], []
    for l in range(DEPTH):
        lp = {'g_pre_mix': g_pre_mix[l], 'w_in': w_in[l], 'conv_w': conv_w[l], 'a_log': a_log[l], 'dt_bias': dt_bias[l], 'o_norm_g': o_norm_g[l], 'pool_w': pool_w[l], 'pool_scale': pool_scale[l], 'w_branch_pool': w_branch_pool[l], 'w_branch_delta': w_branch_delta[l], 'w_out': w_out[l], 'g_post_mix': g_post_mix[l], 'g_pre_ffn': g_pre_ffn[l], 'w_up': w_up[l], 'w_down': w_down[l], 'g_post_ffn': g_post_ffn[l]}
        xp, c_p, s_p, p_p = trunk_layer(xp, zero_conv, zero_ssm, zero_pool, 0, N_META, lp)
        xs, c_s, s_s, p_s = trunk_layer(xs, state_conv[l], state_ssm[l].astype(jnp.float32), state_pool[l], PAST_LEN, 0, lp)
        conv_p.append(c_p)
        ssm_p.append(s_p)
        pool_p.append(p_p)
        conv_s.append(c_s)
        ssm_s.append(s_s)
        pool_s.append(p_s)
    y_prompt = xp[:, N_META:]
    return (y_prompt, xs, jnp.stack(conv_p), jnp.stack(ssm_p), jnp.stack(pool_p), jnp.stack(conv_s), jnp.stack(ssm_s), jnp.stack(pool_s))
```

```python
import numpy as np
import concourse.bass as bass
import concourse.mybir as mybir
from concourse.bass_utils import run_bass_kernel_spmd

F32 = mybir.dt.float32
BF16 = mybir.dt.bfloat16
AF = mybir.ActivationFunctionType
ALU = mybir.AluOpType

ENGS = ('pe', 'act', 'dve', 'pool', 'sp')
MAXOPS = None


class _Op:
    __slots__ = ('eng', 'fn', 'waits', 'signal', 'dma', 'idx')

    def __init__(self, eng, fn, dma=None):
        self.eng = eng
        self.fn = fn
        self.waits = []
        self.signal = False
        self.dma = dma
        self.idx = 0


class Sched:
    NDMA = 24

    def __init__(self, nc):
        self.nc = nc
        self.ops = {e: [] for e in ENGS}
        self.recs = {}
        self.water = {e: {} for e in ENGS}
        self.dma_tot = [0] * self.NDMA
        self.dma_last = [None] * self.NDMA
        self.dma_rr = 0
        self.dma_rr_sw = 0
        self.tensors = []
        self.same_eng_dist = 1 << 30

    def sbuf(self, name, shape, dtype):
        t = self.nc.alloc_sbuf_tensor(name, list(shape), dtype)
        return t

    def psum(self, name, shape, dtype):
        return self.nc.alloc_psum_tensor(name, list(shape), dtype)

    @staticmethod
    def _box(ap):
        t = ap.tensor
        shp = list(t.shape)
        row = 1
        for s in shp[1:]:
            row *= s
        dsz = mybir.dt.size(ap.dtype)
        pat = list(ap.ap)
        off = ap.offset
        p0 = off // row
        f0 = (off % row) * dsz
        pstep, pcnt = pat[0]
        if pstep == 0:
            np_ = 1
        else:
            np_ = (pcnt - 1) * (pstep // row) + 1
        ext = 0
        for st, cnt in pat[1:]:
            ext += (cnt - 1) * abs(st)
        if type(t).__name__ == 'PSumTensorHandle':
            return (p0, p0 + np_, 0, 1 << 20)
        return (p0, p0 + np_, f0, f0 + (ext + 1) * dsz)

    def _track(self, op, aps_r, aps_w):
        deps = set()
        for kind, aps in (('r', aps_r), ('w', aps_w)):
            for ap in aps:
                name = ap.tensor.name
                box = self._box(ap)
                if type(ap.tensor).__name__ == 'PSumTensorHandle':
                    kind = 'w'
                recs = self.recs.setdefault(name, {})
                dead = []
                for (b, k, key), tok in recs.items():
                    if b[0] < box[1] and box[0] < b[1] and b[2] < box[3] and box[2] < b[3]:
                        if kind == 'w' or k == 'w':
                            deps.add(tok)
                        if kind == 'w' and box[0] <= b[0] and b[1] <= box[1] and box[2] <= b[2] and b[3] <= box[3]:
                            dead.append((b, k, key))
                for d in dead:
                    del recs[d]
        return deps

    def _record(self, op, tok, aps_r, aps_w):
        for kind, aps in (('r', aps_r), ('w', aps_w)):
            for ap in aps:
                name = ap.tensor.name
                box = self._box(ap)
                if type(ap.tensor).__name__ == 'PSumTensorHandle':
                    kind = 'w'
                key = tok[0] if tok[0] != 'dma' else ('dma', tok[1])
                self.recs.setdefault(name, {})[(box, kind, key)] = tok

    def _add_waits(self, op, deps):
        eng = op.eng
        my_idx = len(self.ops[eng])
        for tok in deps:
            if tok[0] == 'dma':
                _, s, val = tok
                key = ('dma', s)
                if self.water[eng].get(key, 0) >= val:
                    continue
                self.water[eng][key] = val
                op.waits.append(tok)
            else:
                f, i = tok
                if f == eng:
                    if eng in ('pe', 'sp'):
                        continue
                    if my_idx - i > self.same_eng_dist:
                        continue
                if self.water[eng].get(f, -1) >= i:
                    continue
                self.water[eng][f] = i
                self.ops[f][i].signal = True
                op.waits.append(tok)

    @staticmethod
    def _is_onchip(ap):
        return type(ap.tensor).__name__ in ('SBTensorHandle', 'PSumTensorHandle')

    def op(self, eng, fn, r, w):
        self.nrec = getattr(self, 'nrec', 0) + 1
        if MAXOPS is not None and self.nrec > MAXOPS:
            return None
        o = _Op(eng, fn)
        r = [a for a in r if self._is_onchip(a)]
        w = [a for a in w if self._is_onchip(a)]
        deps = self._track(o, r, w)
        self._add_waits(o, deps)
        o.idx = len(self.ops[eng])
        self.ops[eng].append(o)
        self._record(o, (eng, o.idx), r, w)
        return o

    def dma(self, queue, out, in_):
        self.nrec = getattr(self, 'nrec', 0) + 1
        if MAXOPS is not None and self.nrec > MAXOPS:
            return None
        if queue == 'pool':
            s = 16 + self.dma_rr_sw
            self.dma_rr_sw = (self.dma_rr_sw + 1) % 8
        else:
            s = self.dma_rr
            self.dma_rr = (self.dma_rr + 1) % 16
        o = _Op(queue, None, dma=(s, out, in_))
        r = [in_] if self._is_onchip(in_) else []
        w = [out] if self._is_onchip(out) else []
        deps = self._track(o, r, w)
        if self.dma_last[s] is not None:
            deps.add(self.dma_last[s])
        self._add_waits(o, deps)
        self.dma_tot[s] += 16
        tok = ('dma', s, self.dma_tot[s])
        self.dma_last[s] = tok
        o.idx = len(self.ops[queue])
        self.ops[queue].append(o)
        self._record(o, tok, r, w)
        return tok

    def mm(self, out, lhsT, rhs, start=True, stop=True):
        return self.op('pe', lambda e: e.matmul(out, lhsT, rhs, start=start, stop=stop), [lhsT, rhs], [out])

    def tr(self, out, in_, ident):
        return self.op('pe', lambda e: e.transpose(out, in_, ident), [in_, ident], [out])

    def trf(self, out, in_, ident):
        return self.op('pe', lambda e: e.matmul(out, in_, ident, start=True, stop=True), [in_, ident], [out])

    def act(self, out, in_, func, bias=None, scale=None, accum_out=None, eng='act'):
        kw = {}
        r = [in_]
        w = [out]
        if bias is not None:
            kw['bias'] = bias
            if not isinstance(bias, (int, float)):
                r.append(bias)
        if scale is not None:
            kw['scale'] = scale
            if not isinstance(scale, (int, float)):
                r.append(scale)
        if accum_out is not None:
            kw['accum_out'] = accum_out
            w.append(accum_out)
        return self.op('act', lambda e: e.activation(out, in_, func, **kw), r, w)

    def tt(self, eng, out, in0, in1, op):
        return self.op(eng, lambda e: e.tensor_tensor(out, in0, in1, op), [in0, in1], [out])

    def ts(self, eng, out, in0, s1, s2, op0, op1=None, accum_out=None):
        r = [in0]
        if not isinstance(s1, (int, float)) and s1 is not None:
            r.append(s1)
        if not isinstance(s2, (int, float)) and s2 is not None:
            r.append(s2)
        w = [out]
        kw = {}
        if op1 is not None:
            kw['op1'] = op1
        if accum_out is not None:
            kw['accum_out'] = accum_out
            w.append(accum_out)
        return self.op(eng, lambda e: e.tensor_scalar(out, in0, s1, s2, op0, **kw), r, w)

    def stt(self, out, in0, scalar, in1, op0, op1, eng='dve'):
        r = [in0, in1]
        if not isinstance(scalar, (int, float)):
            r.append(scalar)
        return self.op(eng, lambda e: e.scalar_tensor_tensor(out, in0, scalar, in1, op0, op1), r, [out])

    def cp(self, eng, out, in_):
        if eng == 'act':
            return self.op('act', lambda e: e.copy(out, in_), [in_], [out])
        return self.op(eng, lambda e: e.tensor_copy(out, in_), [in_], [out])

    def memset(self, eng, ap, val):
        return self.op(eng, lambda e: e.memset(ap, val), [], [ap])

    def emit(self, final_wait=True):
        nc = self.nc
        engobj = {'pe': nc.tensor, 'act': nc.scalar, 'dve': nc.vector, 'pool': nc.gpsimd, 'sp': nc.sync}
        sems = {e: nc.alloc_semaphore("prog_" + e) for e in ENGS}
        dsems = [nc.alloc_semaphore("dma_%d" % i) for i in range(self.NDMA)]
        cnt = {}
        for e in ENGS:
            c = 0
            arr = []
            for o in self.ops[e]:
                if o.signal and o.dma is None:
                    c += 1
                arr.append(c)
            cnt[e] = arr
        blockattr = {'pe': 'tensor', 'act': 'scalar', 'dve': 'vector', 'pool': 'gpsimd', 'sp': 'sync'}
        with nc.Block() as block:
            for e in ENGS:
                ops = self.ops[e]

                def body(eng, e=e, ops=ops):
                    for o in ops:
                        for tok in o.waits:
                            if tok[0] == 'dma':
                                eng.wait_ge(dsems[tok[1]], tok[2])
                            else:
                                eng.wait_ge(sems[tok[0]], cnt[tok[0]][tok[1]])
                        if o.dma is not None:
                            s, out, in_ = o.dma
                            eng.dma_start(out=out, in_=in_).then_inc(dsems[s], 16)
                        else:
                            ins = o.fn(eng)
                            if o.signal:
                                ins.then_inc(sems[e], 1)
                    if e == 'sp' and final_wait:
                        for s in range(self.NDMA):
                            if self.dma_tot[s] > 0:
                                eng.wait_ge(dsems[s], self.dma_tot[s])
                getattr(block, blockattr[e])(body)


D = 1024
NH = 8
IN_W = 6672
DFF = 4096
NL = 2
TPG = 6
NT = TPG * 128
CH = 384
NCH = NT // CH
EPS = 1e-6
POOL_W = (2, 4, 8, 16)
C_ID, C_UP, C_US, C_SLP, C_SLS, C_SAMES, C_ONES, C_NEGP, C_NEGS, C_BLK, C_INVC = (
    0, 128, 256, 384, 512, 640, 768, 896, 1024, 1152, 1168)
NCONST = 1232
GROUPS = [[('P', i) for i in range(0, 6)], [('P', i) for i in range(6, 12)],
          [('P', i) for i in range(12, 17)] + [('S', 0)]]
CHAIN_F32 = True
DEBUG = None


def make_consts():
    c = np.zeros((128, NCONST), np.float32)
    p = np.arange(128)
    same = (p[:, None] // 8) == (p[None, :] // 8)
    c[:, C_ID:C_ID + 128] = np.eye(128)
    c[:, C_UP:C_UP + 128] = (p[:, None] <= p[None, :])
    c[:, C_US:C_US + 128] = (p[:, None] <= p[None, :]) & same
    c[:, C_SLP:C_SLP + 128] = (p[:, None] > p[None, :])
    c[:, C_SLS:C_SLS + 128] = (p[:, None] > p[None, :]) & same
    c[:, C_SAMES:C_SAMES + 128] = same
    c[:, C_ONES:C_ONES + 128] = 1.0
    c[:, C_NEGP:C_NEGP + 128] = np.where(p[:, None] < p[None, :], 0.0, -30000.0)
    c[:, C_NEGS:C_NEGS + 128] = np.where((p[:, None] < p[None, :]) & same, 0.0, -30000.0)
    c[:, C_BLK:C_BLK + 16] = (p[:, None] // 8) == np.arange(16)[None, :]
    for gi, w in enumerate(POOL_W):
        for pos in range(16):
            c[:, C_INVC + gi * 16 + pos] = 1.0 / min(pos + 1, w)
    return c


def build_program():
    nc = bass.Bass("TRN2", target_bir_lowering=False)

    def din(name, shape):
        return nc.dram_tensor(name, list(shape), F32, kind="ExternalInput").ap()

    def dout(name, shape):
        return nc.dram_tensor(name, list(shape), F32, kind="ExternalOutput").ap()

    xp = din("xp", [2048, D]); xs = din("xs", [128, D]); meta = din("meta", [16, D])
    sconv = din("sconv", [NL, 16, 3, 3072]); sssm = din("sssm", [NL, 16, NH, 128, 128])
    spool = din("spool", [NL, 16, 15, 512])
    spoolT = din("spoolT", [NL, 128, 4, 16, 15]); sconvT = din("sconvT", [NL, 128, 24, 16, 3])
    w_in = din("w_in", [NL, D, IN_W]); pool_w = din("pool_w", [NL, 4, 128, 128])
    wbp = din("wbp", [NL, 512, D]); wbd = din("wbd", [NL, D, D]); wo = din("wo", [NL, D, D])
    wup = din("wup", [NL, D, DFF]); wdn = din("wdn", [NL, DFF, D])
    consts = din("consts", [128, NCONST]); gpre_d = din("gpre", [128, NL, 2, 8])
    cw_d = din("cw", [128, NL, 24, 4]); pscale_d = din("pscale", [128, NL, 4])
    gpost_d = din("gpost", [128, NL, 2, D]); ong_d = din("ong", [128, NL, 128])
    alog_d = din("alog", [128, NL, 8]); dtb_d = din("dtb", [128, NL, 8])
    y_p = dout("y_p", [2048, D]); y_s = dout("y_s", [128, D])
    ncp = dout("ncp", [NL, 3, 3072]); nsp = dout("nsp", [NL, NH, 128, 128]); npp = dout("npp", [NL, 15, 512])
    ncs = dout("ncs", [NL, 16, 3, 3072]); nss = dout("nss", [NL, 16, NH, 128, 128]); nps = dout("nps", [NL, 16, 15, 512])

    S = Sched(nc)
    CD = F32 if CHAIN_F32 else BF16

    X = S.sbuf("X", [128, TPG, D], F32)
    HT = S.sbuf("HT", [128, 8, NT], BF16)
    BIG = S.sbuf("BIG", [128, 16 * NT], BF16)
    UT = BIG[:, :].rearrange("p (a b) -> p a b", b=NT)
    QT4 = BIG[:, 0:4 * NT].rearrange("p (a b) -> p a b", b=NT)
    KT4 = BIG[:, 4 * NT:8 * NT].rearrange("p (a b) -> p a b", b=NT)
    VT4 = BIG[:, 8 * NT:12 * NT].rearrange("p (a b) -> p a b", b=NT)
    ZS4 = BIG[:, 12 * NT:16 * NT].rearrange("p (t c) -> p t c", c=512)
    MT = BIG[:, 0:8 * NT].rearrange("p (a b) -> p a b", b=NT)
    DM = S.sbuf("DM", [128, 16 * NT], BF16)
    DOT = DM[:, 0:8 * NT].rearrange("p (a b) -> p a b", b=NT)
    AOT = DM[:, 8 * NT:12 * NT].rearrange("p (a b) -> p a b", b=NT)
    FF = DM[:, :].bitcast(F32).rearrange("p (t c) -> p t c", c=D)
    NSLOT = 5
    PREF = 3
    WR = [S.sbuf("WR%d" % i, [128, 8, 512], BF16) for i in range(NSLOT)]
    CST = S.sbuf("CST", [128, NCONST], F32)
    IDB = S.sbuf("IDB", [128, 128], BF16)
    GPRE = S.sbuf("GPRE", [128, NL, 2, 8], F32)
    CW = S.sbuf("CW", [128, NL, 24, 4], F32)
    PSC = S.sbuf("PSC", [128, NL, 4], F32)
    ONG = S.sbuf("ONG", [128, NL, 128], F32)
    NEGA = S.sbuf("NEGA", [128, NL, 8], F32)
    DTB = S.sbuf("DTB", [128, NL, 8], F32)
    EPSC = S.sbuf("EPSC", [128, 2], F32)
    S_ST = S.sbuf("S_ST", [128, NL, NH, 128], F32)
    CPRE = S.sbuf("CPRE", [128, NL, 24, 3], F32)
    PPRE = S.sbuf("PPRE", [128, NL, 4, 15], F32)
    PWL = S.sbuf("PWL", [128, 4, 128], BF16)
    WBA = S.sbuf("WBA", [128, 8, 16], BF16)
    STAT = S.sbuf("STAT", [128, 64], F32)
    BETA = S.sbuf("BETA", [128, TPG, 8], F32)
    NBETA = S.sbuf("NBETA", [128, TPG, 8], F32)
    GG = S.sbuf("GG", [128, TPG, 8], F32)
    TA = S.sbuf("TA", [128, 8], F32)
    EGC = S.sbuf("EGC", [128, 8], F32)
    EGD = S.sbuf("EGD", [128, 8], F32)
    EGL = S.sbuf("EGL", [128, 8], F32)
    GCs = S.sbuf("GCs", [128, 8], F32)
    GBS = S.sbuf("GBS", [128, 8, 16], F32)
    EGLS = S.sbuf("EGLS", [128, 8, 16], F32)
    SPOOLT = S.sbuf("SPOOLT", [128, 4, 16, 15], F32)
    SCONVT = S.sbuf("SCONVT", [128, 24, 16, 3], F32)

    SCR_BYTES = 52 * 1024
    SCR = S.sbuf("SCR", [128, SCR_BYTES // 2], BF16)

    class Arena:
        def __init__(self):
            self.off = 0

        def take(self, free, dtype):
            n = 1
            for f in free:
                n *= f
            nb = n * mybir.dt.size(dtype)
            nb_al = (nb + 31) // 32 * 32
            assert self.off + nb_al <= SCR_BYTES, ("arena overflow", self.off, nb_al)
            ap = SCR[:, self.off // 2:(self.off + nb) // 2]
            self.off += nb_al
            if dtype != BF16:
                ap = ap.bitcast(dtype)
            if len(free) == 2:
                ap = ap.rearrange("p (a b) -> p a b", b=free[1])
            elif len(free) == 3:
                ap = ap.rearrange("p (a b c) -> p a b c", b=free[1], c=free[2])
            return ap

    EXTW = 16 + NT
    A = Arena()
    JUNK = A.take([D], BF16)
    HB = [A.take([D], BF16) for _ in range(2)]
    offA0 = A.off
    UEXT = [A.take([EXTW], F32) for _ in range(2)]
    SCEXT = A.take([16, 11], F32)
    TMROW = A.take([512], F32)
    offA1 = A.off
    PA = A.take([EXTW], F32)
    PBf = A.take([EXTW], F32)
    MTP = A.take([NT], BF16)
    SEXT = A.take([16, 23], F32)
    SPA = A.take([16, 23], F32)
    SPB = A.take([16, 23], F32)
    A.off = offA1
    CQ2 = [A.take([NT], F32) for _ in range(2)]
    SQ2_2 = [A.take([NT], F32) for _ in range(2)]
    SQ_4 = [A.take([NT], F32) for _ in range(4)]
    RN_4 = [A.take([NT], F32) for _ in range(4)]
    A.off = offA0
    GPOSTb = A.take([D], F32)
    TMPX = [A.take([512], F32) for _ in range(2)]
    SGT = [A.take([CH], F32) for _ in range(4)]
    T12 = [A.take([CH], F32) for _ in range(4)]
    RL = [A.take([CH], F32) for _ in range(2)]
    G = Arena()
    NB = 4
    Bm = [G.take([128], F32) for _ in range(NB)]
    DTS = [G.take([128], F32) for _ in range(NB)]
    DTM = [G.take([128], F32) for _ in range(NB)]
    CA = [[G.take([128], CD) for _ in range(2)] for _ in range(NB)]
    CAT = [[G.take([128], CD) for _ in range(2)] for _ in range(NB)]
    CR = [[G.take([128], CD) for _ in range(2)] for _ in range(NB)]
    KE = [G.take([128], CD) for _ in range(NB)]
    KDEC = [G.take([128], BF16) for _ in range(NB)]
    VTM = [G.take([128], CD) for _ in range(NB)]
    WTt = [G.take([128], BF16) for _ in range(NB)]
    UB = [G.take([128], F32) for _ in range(NB)]
    VN = [G.take([128], BF16) for _ in range(NB)]
    ATt = [G.take([128], BF16) for _ in range(NB)]
    O1 = [G.take([128], F32) for _ in range(NB)]
    OO = [G.take([128], F32) for _ in range(NB)]
    DD = [G.take([128], BF16) for _ in range(NB)]
    SBF = [G.take([128], BF16) for _ in range(NB)]
    JG = [G.take([128], BF16) for _ in range(NB)]
    FMS = [G.take([128], F32) for _ in range(2)]
    S0F = G.take([16, 128], F32)
    S0B = G.take([16, 128], BF16)
    VBLK = G.take([16, 128], BF16)

    PB = [S.psum("PB%d" % i, [128, 512], F32) for i in range(6)]
    PH = [S.psum("PH%d" % i, [128, 1024], BF16) for i in range(2)]
    qctr = [0]

    def pq_align():
        qctr[0] = (qctr[0] + 3) // 4 * 4 % 8

    def pq():
        i = qctr[0]
        qctr[0] = (i + 1) % 8
        return PB[4 + i // 4][:, (i % 4) * 128:(i % 4 + 1) * 128]
    bctr = [0]

    def pbank():
        i = bctr[0]
        bctr[0] = (i + 1) % 6
        return PB[i]
    hctr = [0]
    hq = [0, 0, 0, 0]

    def pqb_align():
        hctr[0] = (hctr[0] + 3) // 4 * 4 % 16

    def pqb():
        i = hctr[0]
        hctr[0] = (i + 1) % 16
        slot = (i % 4) + 4 * ((i // 8) % 2)
        return PH[(i // 4) % 2][:, slot * 128:(slot + 1) * 128]

    ident = CST[:, C_ID:C_ID + 128]
    ident_cd = ident if CHAIN_F32 else IDB[:, :]
    ones_f = CST[:, C_ONES:C_ONES + 128]
    blk16 = CST[:, C_BLK:C_BLK + 16]
    eps_c = EPSC[:, 0:1]
    one_c = EPSC[:, 1:2]

    wlist = []

    def wsrc_k8(wt, l, c0, ncols=512):
        return wt[l, :, c0:c0 + ncols].rearrange("(dc p) c -> p dc c", p=128)

    wstate = {'issued': 0}

    def wissue(upto):
        while wstate['issued'] <= upto and wstate['issued'] < len(wlist):
            n = wstate['issued']
            src, kc, ncol = wlist[n]
            S.dma('pool', WR[n % NSLOT][:, 0:kc, 0:ncol], src)
            wstate['issued'] += 1

    wuse = [0]

    def wget(n, pref=None):
        pref = PREF if pref is None else pref
        if DEBUG is None:
            assert n == wuse[0], ("weight order mismatch", n, wuse[0])
        wuse[0] += 1
        if DEBUG is None:
            wissue(n + pref)
        else:
            src, kc, ncol = wlist[n]
            S.dma('pool', WR[n % NSLOT][:, 0:kc, 0:ncol], src)
        return WR[n % NSLOT]

    widx = {}
    for g in range(len(GROUPS)):
        for l in range(NL):
            def add(key, src, kc=8, ncol=512):
                widx[(g, l) + key] = len(wlist)
                wlist.append((src, kc, ncol))
            add(('u',), wsrc_k8(w_in, l, 0))
            for hb in range(2):
                for comp in range(3):
                    add(('qkv', comp, hb), wsrc_k8(w_in, l, 512 + comp * 1024 + hb * 512))
                add(('z', hb), wsrc_k8(w_in, l, 3600 + hb * 512))
            for half in range(2):
                add(('wbp', half), wbp[l, :, half * 512:(half + 1) * 512].rearrange("(dc p) c -> p dc c", p=128), kc=4)
                add(('wbd', half), wsrc_k8(wbd, l, half * 512))
                add(('gp', half), wsrc_k8(w_in, l, 4624 + half * 512))
                add(('gd', half), wsrc_k8(w_in, l, 5648 + half * 512))
            for half in range(2):
                add(('wo', half), wsrc_k8(wo, l, half * 512))
            for fh in range(2):
                for f4 in range(4):
                    add(('up', fh * 4 + f4), wsrc_k8(wup, l, (fh * 4 + f4) * 512))
                for half in range(2):
                    for kl in range(2):
                        kcg = fh * 2 + kl
                        add(('dn', half, kcg),
                            wdn[l, kcg * 1024:(kcg + 1) * 1024, half * 512:(half + 1) * 512].rearrange("(dc p) c -> p dc c", p=128))

    S.dma('sp', CST[:], consts)
    S.dma('sp', GPRE[:], gpre_d)
    S.dma('sp', CW[:], cw_d)
    S.dma('sp', PSC[:], pscale_d)
    S.dma('sp', ONG[:], ong_d)
    S.dma('sp', NEGA[:], alog_d)
    S.dma('sp', DTB[:], dtb_d)
    S.cp('dve', IDB[:], ident)
    S.memset('dve', EPSC[:, 0:1], EPS)
    S.memset('dve', EPSC[:, 1:2], 1.0)
    S.act(NEGA[:], NEGA[:], AF.Exp)
    S.ts('pool', NEGA[:], NEGA[:], -1.0, None, ALU.mult)
    S.memset('pool', S_ST[:], 0.0)
    S.memset('pool', CPRE[:], 0.0)
    S.memset('pool', PPRE[:], 0.0)

    def rstd_from(out, ssq, n):
        S.act(out, ssq, AF.Ln, bias=eps_c, scale=1.0 / n)
        S.act(out, out, AF.Exp, scale=-0.5)

    def group_info(g):
        tiles = GROUPS[g]
        npt = sum(1 for t in tiles if t[0] == 'P')
        has_s = any(t[0] == 'S' for t in tiles)
        return tiles, npt, npt * 128, has_s

    def chunk_ranges(c, NW, has_s):
        a, b = c * CH, min((c + 1) * CH, NW)
        pr = (a, b) if b > a else None
        sr = has_s and (c + 1) * CH == NT
        return pr, sr

    def load_x(g):
        tiles, npt, NW, has_s = group_info(g)
        ti = 0
        if tiles[0] == ('P', 0):
            S.memset('pool', X[:, 0, :], 0.0)
            S.dma('sp', X[112:128, 0, :], meta)
            ti = 1
        if npt > ti:
            pt0 = tiles[ti][1]
            S.dma('sp', X[:, ti:npt, :], xp[(pt0 - 1) * 128:(pt0 - 1 + npt - ti) * 128, :].rearrange("(t p) c -> p t c", p=128))
        if has_s:
            S.dma('sp', X[:, TPG - 1, :], xs)

    def store_x(g):
        tiles, npt, NW, has_s = group_info(g)
        ti = 1 if tiles[0] == ('P', 0) else 0
        if npt > ti:
            pt0 = tiles[ti][1]
            S.dma('sp', y_p[(pt0 - 1) * 128:(pt0 - 1 + npt - ti) * 128, :].rearrange("(t p) c -> p t c", p=128), X[:, ti:npt, :])
        if has_s:
            S.dma('sp', y_s, X[:, TPG - 1, :])

    def norm_to_HT(l, which):
        for ti in range(TPG):
            ssq = STAT[:, ti:ti + 1]
            rs = STAT[:, 8 + ti:9 + ti]
            S.act(JUNK, X[:, ti, :], AF.Square, accum_out=ssq)
            rstd_from(rs, ssq, D)
            hb = HB[ti % 2]
            S.ts('dve', hb, X[:, ti, :], rs, None, ALU.mult)
            ph = PH[ti % 2]
            for dc in range(8):
                S.tr(ph[:, dc * 128:(dc + 1) * 128], hb[:, dc * 128:(dc + 1) * 128], IDB[:])
            S.tt('dve', HT[:, :, ti * 128:(ti + 1) * 128],
                 ph[:, :].rearrange("p (a b) -> p a b", b=128),
                 GPRE[:, l, which, :].to_broadcast([128, 8, 128]), ALU.mult)

    def fm_proj(wtile, c0, rhs_buf, nk, cidx):
        acc = pbank()[:, 0:CH]
        for kc in range(nk):
            S.mm(acc, wtile[:, kc, c0:c0 + 128], rhs_buf[:, kc, cidx * CH:(cidx + 1) * CH],
                 start=(kc == 0), stop=(kc == nk - 1))
        return acc

    def tm_proj(wtile, ti, src_buf, nk=8, ncol=512):
        acc = pbank()[:, 0:ncol]
        for kc in range(nk):
            S.mm(acc, src_buf[:, kc, ti * 128:(ti + 1) * 128], wtile[:, kc, 0:ncol],
                 start=(kc == 0), stop=(kc == nk - 1))
        return acc

    def stage_pool(g, l):
        tiles, npt, NW, has_s = group_info(g)
        W0 = wget(widx[(g, l, 'u')])
        S.dma('pool', PWL[:], pool_w[l].rearrange("g c d -> c g d"))
        if has_s:
            S.dma('sp', SPOOLT[:], spoolT[l])
            S.dma('sp', nps[l, :, 0:7, :], spool[l, :, 8:15, :])
            acc = tm_proj(W0, npt - 1, HT)
            S.cp('act', TMROW, acc)
            S.dma('sp', npp[l], TMROW[113:128, :])
            acc = tm_proj(W0, TPG - 1, HT)
            S.cp('act', TMROW, acc)
            for t in range(8):
                S.dma('sp', nps[l, :, 7 + t, :], TMROW[t::8, :])
        for gi in range(4):
            w = POOL_W[gi]
            ue = UEXT[gi % 2]
            S.cp('pool', ue[:, 1:16], PPRE[:, l, gi, :])
            for c in range(NCH):
                acc = fm_proj(W0, gi * 128, HT, 8, c)
                pr, sr = chunk_ranges(c, NW, has_s)
                if pr:
                    S.cp('act', ue[:, 16 + pr[0]:16 + pr[1]], acc[:, pr[0] - c * CH:pr[1] - c * CH])
                if sr:
                    S.cp('act', SEXT[:, :, 15:23], acc[:, CH - 128:CH].rearrange("p (s t) -> p s t", t=8))
            Wd = 16 + NW
            S.cp('pool', PPRE[:, l, gi, :], ue[:, Wd - 15:Wd])
            src = ue
            for j in range(gi + 1):
                k = 1 << j
                sh = (1 << (j + 1))
                dst = PA if j % 2 == 0 else PBf
                S.tt('pool', dst[:, sh:Wd], src[:, sh:Wd], src[:, sh - k:Wd - k], ALU.add)
                src = dst
            S.stt(MTP[:, 0:NW], src[:, 16:Wd], 1.0 / w, ue[:, 16:Wd], ALU.mult, ALU.subtract)
            if g == 0:
                tmp = STAT[:, 32:48]
                S.tt('dve', tmp, src[:, 16 + 112:16 + 128], CST[:, C_INVC + gi * 16:C_INVC + gi * 16 + 16], ALU.mult)
                S.tt('dve', MTP[:, 112:128], tmp, ue[:, 16 + 112:16 + 128], ALU.subtract)
            if has_s:
                S.cp('pool', SEXT[:, :, 0:15], SPOOLT[:, gi, :, :])
                ssrc = SEXT
                for j in range(gi + 1):
                    k = 1 << j
                    sh = (1 << (j + 1)) - 1
                    dst = SPA if j % 2 == 0 else SPB
                    S.tt('pool', dst[:, :, sh:23], ssrc[:, :, sh:23], ssrc[:, :, sh - k:23 - k], ALU.add)
                    ssrc = dst
                S.stt(MTP[:, NW:NT].rearrange("p (s t) -> p s t", t=8), ssrc[:, :, 15:23], 1.0 / w,
                      SEXT[:, :, 15:23], ALU.mult, ALU.subtract)
            for c in range(NCH):
                acc = pbank()[:, 0:CH]
                S.mm(acc, PWL[:, gi, :], MTP[:, c * CH:(c + 1) * CH])
                S.act(AOT[:, gi, c * CH:(c + 1) * CH], acc, AF.Copy, scale=PSC[:, l, gi:gi + 1])

    def stage_sconv_prep(l):
        S.dma('sp', SCONVT[:], sconvT[l])

    def stage_qkv(g, l, hb):
        tiles, npt, NW, has_s = group_info(g)
        for comp in range(3):
            Wt = wget(widx[(g, l, 'qkv', comp, hb)])
            if has_s:
                colbase = comp * 1024 + hb * 512
                acc = tm_proj(Wt, npt - 1, HT)
                S.cp('act', TMROW, acc)
                S.dma('sp', ncp[l, :, colbase:colbase + 512], TMROW[125:128, :])
                acc = tm_proj(Wt, TPG - 1, HT)
                S.cp('act', TMROW, acc)
                for t in range(3):
                    S.dma('sp', ncs[l, :, t, colbase:colbase + 512], TMROW[5 + t::8, :])
            def a1(hl):
                h = hb * 4 + hl
                ch = comp * 8 + h
                c0 = hl * 128
                ext = UEXT[ch % 2]
                S.cp('pool', ext[:, 0:3], CPRE[:, l, ch, :])
                for c in range(NCH):
                    acc = fm_proj(Wt, c0, HT, 8, c)
                    pr, sr = chunk_ranges(c, NW, has_s)
                    if pr:
                        S.cp('act', ext[:, 3 + pr[0]:3 + pr[1]], acc[:, pr[0] - c * CH:pr[1] - c * CH])
                    if sr:
                        S.cp('act', SCEXT[:, :, 3:11], acc[:, CH - 128:CH].rearrange("p (s t) -> p s t", t=8))
                S.cp('pool', CPRE[:, l, ch, :], ext[:, NW:NW + 3])
                if has_s:
                    CQ = CQ2[ch % 2]
                    S.cp('pool', SCEXT[:, :, 0:3], SCONVT[:, ch, :, :])
                    cqs = CQ[:, NW:NT].rearrange("p (s t) -> p s t", t=8)
                    S.ts('dve', cqs, SCEXT[:, :, 0:8], CW[:, l, ch, 0:1], None, ALU.mult)
                    for j in range(1, 4):
                        S.stt(cqs, SCEXT[:, :, j:j + 8], CW[:, l, ch, j:j + 1], cqs, ALU.mult, ALU.add)

            def a2(hl):
                h = hb * 4 + hl
                ch = comp * 8 + h
                ext = UEXT[ch % 2]
                CQ, SQ2, SQ, RN = CQ2[ch % 2], SQ2_2[ch % 2], SQ_4[hl], RN_4[hl]
                S.act(CQ[:, 0:NW], ext[:, 0:NW], AF.Copy, scale=CW[:, l, ch, 0:1])
                for j in range(1, 4):
                    S.stt(CQ[:, 0:NW], ext[:, j:NW + j], CW[:, l, ch, j:j + 1], CQ[:, 0:NW], ALU.mult, ALU.add)
                if comp == 2:
                    S.act(VT4[:, hl, :], CQ, AF.Silu)
                else:
                    S.act(SQ, CQ, AF.Silu)
                    S.tt('pool', SQ2, SQ, SQ, ALU.mult)
                    for c in range(NCH):
                        acc = pbank()[:, 0:CH]
                        S.mm(acc, ones_f, SQ2[:, c * CH:(c + 1) * CH])
                        S.cp('act', RN[:, c * CH:(c + 1) * CH], acc)

            def phase_b(hl):
                SQ, RN = SQ_4[hl], RN_4[hl]
                S.act(RN, RN, AF.Ln, bias=eps_c, scale=1.0)
                S.act(RN, RN, AF.Exp, scale=-0.5)
                dst = QT4 if comp == 0 else KT4
                sc = (128.0 ** -0.5) if comp == 0 else 1.0
                S.stt(dst[:, hl, :], SQ, sc, RN, ALU.mult, ALU.mult)

            a1(0)
            for hl in range(4):
                if hl + 1 < 4:
                    a1(hl + 1)
                a2(hl)
            if comp != 2:
                for hl in range(4):
                    phase_b(hl)

    def stage_ba(g, l):
        S.dma('pool', WBA[:], w_in[l, :, 3584:3600].rearrange("(dc p) c -> p dc c", p=128))
        for ti in range(TPG):
            acc = pq()[:, 0:16]
            for kc in range(8):
                S.mm(acc, HT[:, kc, ti * 128:(ti + 1) * 128], WBA[:, kc, :], start=(kc == 0), stop=(kc == 7))
            S.act(BETA[:, ti, :], acc[:, 0:8], AF.Sigmoid)
            S.cp('act', TA[:], acc[:, 8:16])
            S.tt('dve', TA[:], TA[:], DTB[:, l, :], ALU.add)
            S.act(TA[:], TA[:], AF.Exp)
            S.act(TA[:], TA[:], AF.Ln, bias=one_c, scale=1.0)
            S.tt('dve', GG[:, ti, :], TA[:], NEGA[:, l, :], ALU.mult)
        S.ts('pool', NBETA[:], BETA[:], -1.0, None, ALU.mult)

    def stage_z(g, l, hb):
        Wt = wget(widx[(g, l, 'z', hb)])
        for ti in range(TPG):
            acc = tm_proj(Wt, ti, HT)
            S.act(ZS4[:, ti, :], acc, AF.Silu)
            zv = ZS4[:, ti, :].rearrange("p (h v) -> p h v", v=128)
            S.tt('pool', zv, zv, ONG[:, l, :].unsqueeze(1).broadcast_to([128, 4, 128]), ALU.mult)

    def stage_gdn(g, l, hb):
        tiles, npt, NW, has_s = group_info(g)
        for ti, (kind, pidx) in enumerate(tiles):
            isS = (kind == 'S')
            Umat = CST[:, C_US:C_US + 128] if isS else CST[:, C_UP:C_UP + 128]
            SLmat = CST[:, C_SLS:C_SLS + 128] if isS else CST[:, C_SLP:C_SLP + 128]
            SAMEmat = CST[:, C_SAMES:C_SAMES + 128] if isS else ones_f
            NEGmat = CST[:, C_NEGS:C_NEGS + 128] if isS else CST[:, C_NEGP:C_NEGP + 128]
            L = 3 if isS else 6
            tsl = slice(ti * 128, (ti + 1) * 128)
            gc_ps = pq()[:, 0:8]
            S.mm(gc_ps, Umat, GG[:, ti, :])
            gl_ps = pq()[:, 0:8]
            S.mm(gl_ps, SAMEmat, GG[:, ti, :])
            S.act(EGC[:], gc_ps, AF.Exp)
            S.cp('act', GCs[:], gc_ps)
            S.cp('act', EGD[:], gl_ps)
            S.act(EGL[:], gl_ps, AF.Exp)
            S.tt('dve', EGD[:], EGD[:], GCs[:], ALU.subtract)
            S.act(EGD[:], EGD[:], AF.Exp)
            if isS:
                S.tt('pool', GBS[:], GG[:, ti, :].to_broadcast([128, 8, 16]),
                     blk16.unsqueeze(1).broadcast_to([128, 8, 16]), ALU.mult)
                egp = pq()
                S.mm(egp, ones_f, GBS[:, :, :].rearrange("p a b -> p (a b)"))
                S.act(EGLS[:, :, :].rearrange("p a b -> p (a b)"), egp, AF.Exp)
            head_sets = [[0], [1], [2], [3]] if isS else [[0, 1, 2, 3]]
            for hls in head_sets:
                gdn_heads(g, l, hb, ti, isS, hls, Umat, SLmat, NEGmat, L, tsl)
        if g == len(GROUPS) - 1 and hb == 1:
            S.dma('sp', nsp[l].rearrange("h k v -> k h v"), S_ST[:, l, :, :])

    def gdn_heads(g, l, hb, ti, isS, hls, Umat, SLmat, NEGmat, L, tsl):
        H = [(hl, hb * 4 + hl) for hl in hls]
        lock4 = (len(hls) == 4)

        def pqx(hl):
            if not lock4:
                return pq()
            q = hq[hl]
            hq[hl] = (q + 1) % 4
            return PB[hl][:, q * 128:(q + 1) * 128]
        dps = {}
        for hl, h in H:
            S.ts('pool', Bm[hl], SLmat, GG[:, ti, h:h + 1], None, ALU.mult)
        pq_align()
        for hl, h in H:
            d = pqx(hl)
            S.mm(d, Bm[hl], Umat, start=True, stop=False)
            S.mm(d, ident, NEGmat, start=False, stop=True)
            dps[hl] = d
        for hl, h in H:
            S.act(DTS[hl], dps[hl], AF.Exp)
        for hl, h in H:
            S.tt('pool', DTM[hl], DTS[hl], ident, ALU.add)
        gks = {}
        pq_align()
        for hl, h in H:
            gk = pqx(hl)
            S.mm(gk, KT4[:, hl, tsl], KT4[:, hl, tsl])
            gks[hl] = gk
        for hl, h in H:
            S.stt(CA[hl][0], gks[hl], BETA[:, ti, h:h + 1], DTS[hl], ALU.mult, ALU.mult)
        pts = {}
        pq_align()
        pqb_align()
        for hl, h in H:
            p_ = pqx(hl) if CHAIN_F32 else pqb()
            if CHAIN_F32:
                S.trf(p_, CA[hl][0], ident_cd)
            else:
                S.tr(p_, CA[hl][0], ident_cd)
            pts[hl] = p_
        for hl, h in H:
            S.cp('act', CAT[hl][0], pts[hl])
            S.tt('pool', CR[hl][0], ident_cd, CA[hl][0], ALU.subtract)
        kts, vts = {}, {}
        pqb_align()
        for hl, h in H:
            kt_ = pqb()
            S.tr(kt_, KT4[:, hl, tsl], IDB[:])
            kts[hl] = kt_
        pqb_align()
        for hl, h in H:
            vt_ = pqb()
            S.tr(vt_, VT4[:, hl, tsl], IDB[:])
            vts[hl] = vt_
        for hl, h in H:
            S.ts('dve', KE[hl], kts[hl], EGC[:, h:h + 1], None, ALU.mult)
            S.ts('dve', KDEC[hl], kts[hl], EGD[:, h:h + 1], None, ALU.mult)
            S.cp('act', VTM[hl], vts[hl])
        for k in range(1, L + 1):
            a_ps, at_ps = {}, {}
            pq_align()
            if k < L:
                for hl, h in H:
                    a = pqx(hl)
                    S.mm(a, CAT[hl][(k - 1) % 2], CA[hl][(k - 1) % 2])
                    a_ps[hl] = a
                pq_align()
            for hl, h in H:
                at = pqx(hl)
                S.mm(at, CA[hl][(k - 1) % 2], CAT[hl][(k - 1) % 2])
                at_ps[hl] = at
            for hl, h in H:
                if k < L:
                    S.cp('act', CA[hl][k % 2], a_ps[hl])
                S.cp('dve', CAT[hl][k % 2], at_ps[hl])
            r_ps = {}
            pq_align()
            for hl, h in H:
                r = pqx(hl)
                S.mm(r, CAT[hl][k % 2], CR[hl][(k - 1) % 2])
                r_ps[hl] = r
            for hl, h in H:
                S.tt('dve', CR[hl][k % 2], r_ps[hl], CR[hl][(k - 1) % 2], ALU.add)
        XT = {hl: CR[hl][L % 2] for hl, h in H}
        wps, ups = {}, {}
        pq_align()
        for hl, h in H:
            w_ = pqx(hl)
            S.mm(w_, KE[hl], XT[hl])
            wps[hl] = w_
            u_ = pqx(hl)
            S.mm(u_, XT[hl], VTM[hl])
            ups[hl] = u_
        for hl, h in H:
            S.cp('act', WTt[hl], wps[hl])
            S.act(UB[hl], ups[hl], AF.Copy, scale=BETA[:, ti, h:h + 1])
        wss = {}
        if not isS:
            for hl, h in H:
                S.cp('pool', SBF[hl], S_ST[:, l, h, :])
            pq_align()
            for hl, h in H:
                ws = pqx(hl)
                S.mm(ws, WTt[hl], SBF[hl])
                wss[hl] = ws
        else:
            for hl, h in H:
                S.dma('sp', S0F, sssm[l, :, h].rearrange("s k v -> k s v"))
                S.dma('pool', S0B, sssm[l, :, h].rearrange("s k v -> k s v"))
                wsT = pqx(hl)
                for s in range(16):
                    S.mm(wsT[:, 8 * s:8 * s + 8], S0B[:, s, :], WTt[hl][:, 8 * s:8 * s + 8])
                S.cp('act', FMS[0], wsT)
                ws = pqx(hl)
                S.trf(ws, FMS[0], ident)
                wss[hl] = ws
        for hl, h in H:
            S.stt(VN[hl], wss[hl], NBETA[:, ti, h:h + 1], UB[hl], ALU.mult, ALU.add)
        aps = {}
        pq_align()
        for hl, h in H:
            a_ = pqx(hl)
            S.mm(a_, KT4[:, hl, tsl], QT4[:, hl, tsl])
            aps[hl] = a_
        for hl, h in H:
            S.tt('dve', ATt[hl], aps[hl], DTM[hl], ALU.mult)
        avs, qss = {}, {}
        pq_align()
        for hl, h in H:
            av = pqx(hl)
            S.mm(av, ATt[hl], VN[hl])
            avs[hl] = av
        pq_align()
        for hl, h in H:
            if not isS:
                qs = pqx(hl)
                S.mm(qs, QT4[:, hl, tsl], SBF[hl])
            else:
                qsT = pqx(hl)
                for s in range(16):
                    S.mm(qsT[:, 8 * s:8 * s + 8], S0B[:, s, :], QT4[:, hl, ti * 128 + 8 * s:ti * 128 + 8 * s + 8])
                S.cp('act', FMS[1], qsT)
                qs = pqx(hl)
                S.trf(qs, FMS[1], ident)
            qss[hl] = qs
        for hl, h in H:
            S.act(O1[hl], qss[hl], AF.Copy, scale=EGC[:, h:h + 1])
            S.tt('dve', OO[hl], avs[hl], O1[hl], ALU.add)
        for hl, h in H:
            S.act(JG[hl], OO[hl], AF.Square, accum_out=STAT[:, 16 + hl:17 + hl])
        for hl, h in H:
            rstd_from(STAT[:, 24 + hl:25 + hl], STAT[:, 16 + hl:17 + hl], 128)
        dts = {}
        pqb_align()
        for hl, h in H:
            S.stt(DD[hl], OO[hl], STAT[:, 24 + hl:25 + hl], ZS4[:, ti, hl * 128:(hl + 1) * 128], ALU.mult, ALU.mult)
            d_ = pqb()
            S.tr(d_, DD[hl], IDB[:])
            dts[hl] = d_
        for hl, h in H:
            S.cp('act', DOT[:, h, tsl], dts[hl])
        if not isS:
            dss = {}
            pq_align()
            for hl, h in H:
                ds = pqx(hl)
                S.mm(ds, KDEC[hl], VN[hl])
                dss[hl] = ds
            for hl, h in H:
                S.stt(S_ST[:, l, h, :], S_ST[:, l, h, :], EGL[:, h:h + 1], dss[hl], ALU.mult, ALU.add)
        else:
            for hl, h in H:
                S.tt('pool', VBLK, VN[hl].unsqueeze(1).broadcast_to([128, 16, 128]),
                     blk16.to_broadcast([128, 16, 128]), ALU.mult)
                S.tt('pool', S0F, S0F, EGLS[:, h, :].to_broadcast([128, 16, 128]), ALU.mult)
                for q4 in range(4):
                    bk = pbank()
                    S.mm(bk[:, :], KDEC[hl], VBLK[:, 4 * q4:4 * q4 + 4, :].rearrange("p a b -> p (a b)"))
                    sl = S0F[:, 4 * q4:4 * q4 + 4, :].rearrange("p a b -> p (a b)")
                    S.tt('dve', sl, bk[:, :], sl, ALU.add)
                S.dma('sp', nss[l, :, h].rearrange("s k v -> k s v"), S0F)

    def stage_merge(g, l):
        for half in range(2):
            Wbp = wget(widx[(g, l, 'wbp', half)], min(PREF, 4))
            Wbd = wget(widx[(g, l, 'wbd', half)], min(PREF, 3))
            Wgp = wget(widx[(g, l, 'gp', half)], min(PREF, 2))
            Wgd = wget(widx[(g, l, 'gd', half)], 1)
            for dl in range(4):
                dcn = half * 4 + dl
                for c in range(NCH):
                    i0 = (dl * NCH + c) % 2
                    gp = fm_proj(Wgp, dl * 128, HT, 8, c)
                    gd = fm_proj(Wgd, dl * 128, HT, 8, c)
                    brp = fm_proj(Wbp, dl * 128, AOT, 4, c)
                    brd = fm_proj(Wbd, dl * 128, DOT, 8, c)
                    S.act(SGT[2 * i0], gp, AF.Sigmoid)
                    S.act(SGT[2 * i0 + 1], gd, AF.Sigmoid)
                    S.tt('dve', T12[2 * i0], brp, SGT[2 * i0], ALU.mult)
                    S.tt('dve', T12[2 * i0 + 1], brd, SGT[2 * i0 + 1], ALU.mult)
                    S.tt('pool', MT[:, dcn, c * CH:(c + 1) * CH], T12[2 * i0], T12[2 * i0 + 1], ALU.add)

    def post_tile(l, which, ti):
        ssq = STAT[:, 48 + ti:49 + ti]
        rs = STAT[:, 56 + ti:57 + ti]
        S.act(JUNK, FF[:, ti, :], AF.Square, accum_out=ssq)
        rstd_from(rs, ssq, D)
        for half in range(2):
            cs = slice(half * 512, (half + 1) * 512)
            S.stt(TMPX[half], FF[:, ti, cs], rs, GPOSTb[:, cs], ALU.mult, ALU.mult)
            S.tt('pool', X[:, ti, cs], X[:, ti, cs], TMPX[half], ALU.add)

    def stage_out(g, l):
        S.dma('sp', GPOSTb, gpost_d[:, l, 0, :])
        for half in range(2):
            Wt = wget(widx[(g, l, 'wo', half)])
            for ti in range(TPG):
                acc = tm_proj(Wt, ti, MT)
                S.cp('act', FF[:, ti, half * 512:(half + 1) * 512], acc)
                if half == 1:
                    post_tile(l, 0, ti)

    def stage_ffn(g, l):
        norm_to_HT(l, 1)
        S.dma('sp', GPOSTb, gpost_d[:, l, 1, :])
        for fh in range(2):
            for f4 in range(4):
                Wt = wget(widx[(g, l, 'up', fh * 4 + f4)])
                for fl in range(4):
                    fi = f4 * 4 + fl
                    for c in range(NCH):
                        acc = fm_proj(Wt, fl * 128, HT, 8, c)
                        rl = RL[(fl * NCH + c) % 2]
                        S.act(rl, acc, AF.Relu)
                        S.tt('pool', UT[:, fi, c * CH:(c + 1) * CH], rl, rl, ALU.mult)
            for half in range(2):
                for kl in range(2):
                    Wt = wget(widx[(g, l, 'dn', half, fh * 2 + kl)])
                    for ti in range(TPG):
                        for kc in range(8):
                            S.mm(PB[ti][:, :], UT[:, kl * 8 + kc, ti * 128:(ti + 1) * 128], Wt[:, kc, :],
                                 start=(kl == 0 and kc == 0), stop=(kl == 1 and kc == 7))
                for ti in range(TPG):
                    dst = FF[:, ti, half * 512:(half + 1) * 512]
                    if fh == 0:
                        S.cp('act', dst, PB[ti][:, :])
                    else:
                        S.tt('dve', dst, PB[ti][:, :], dst, ALU.add)
                        if half == 1:
                            post_tile(l, 1, ti)

    dbg = DEBUG
    def on(name):
        return dbg is None or name in dbg['stages']
    if dbg is None:
        wissue(PREF)
    for g in range(len(GROUPS)):
        if dbg is not None and g not in dbg['groups']:
            continue
        tiles, npt, NW, has_s = group_info(g)
        load_x(g)
        for l in range(NL):
            if dbg is not None and l not in dbg['layers']:
                continue
            if on('norm'):
                norm_to_HT(l, 0)
            if on('pool'):
                stage_pool(g, l)
            if on('ba'):
                stage_ba(g, l)
            if has_s and on('qkv'):
                stage_sconv_prep(l)
            for hb in range(2):
                if on('qkv'):
                    stage_qkv(g, l, hb)
                if on('z'):
                    stage_z(g, l, hb)
                if on('gdn'):
                    stage_gdn(g, l, hb)
            if on('merge'):
                stage_merge(g, l)
            if on('out'):
                stage_out(g, l)
            if on('ffn'):
                stage_ffn(g, l)
        store_x(g)
    if dbg is None:
        assert wuse[0] == len(wlist)
    S.emit()
    return nc, S


_CACHE = {}


def kernel(x_prompt, x_sample, state_conv, state_ssm, state_pool, meta_tokens, g_pre_mix, w_in, conv_w, a_log,
           dt_bias, o_norm_g, pool_w, pool_scale, w_branch_pool, w_branch_delta, w_out, g_post_mix, g_pre_ffn,
           w_up, w_down, g_post_ffn):
    f = lambda a: np.ascontiguousarray(np.asarray(a, dtype=np.float32))
    if 'nc' not in _CACHE:
        _CACHE['nc'] = build_program()[0]
    nc = _CACHE['nc']
    x_prompt, x_sample, state_conv, state_ssm, state_pool = map(f, (x_prompt, x_sample, state_conv, state_ssm, state_pool))
    gpre = np.stack([f(g_pre_mix), f(g_pre_ffn)], axis=1)
    gpre = f(gpre.reshape(NL, 2, 8, 128).transpose(3, 0, 1, 2))
    cw = f(f(conv_w).reshape(NL, 4, 24, 128).transpose(3, 0, 2, 1))
    psc = f(f(pool_scale).reshape(NL, 4, 128).transpose(2, 0, 1))
    gpost = np.stack([f(g_post_mix), f(g_post_ffn)], axis=1)
    gpost = f(np.broadcast_to(gpost[None], (128, NL, 2, D)))
    ong = f(np.broadcast_to(f(o_norm_g)[None], (128, NL, 128)))
    alog = f(np.broadcast_to(f(a_log)[None], (128, NL, 8)))
    dtb = f(np.broadcast_to(f(dt_bias)[None], (128, NL, 8)))
    shared = {
        "meta": f(meta_tokens), "w_in": f(w_in), "pool_w": f(pool_w), "wbp": f(w_branch_pool),
        "wbd": f(w_branch_delta), "wo": f(w_out), "wup": f(w_up), "wdn": f(w_down),
        "consts": make_consts(), "gpre": gpre, "cw": cw, "pscale": psc, "gpost": gpost, "ong": ong,
        "alog": alog, "dtb": dtb,
    }
    in_maps = []
    for c in range(8):
        m = dict(shared)
        m["xp"] = x_prompt[c]
        m["xs"] = f(x_sample[16 * c:16 * c + 16].reshape(128, D))
        m["sconv"] = f(state_conv[:, 16 * c:16 * c + 16])
        m["sssm"] = f(state_ssm[:, 16 * c:16 * c + 16])
        m["spool"] = f(state_pool[:, 16 * c:16 * c + 16])
        m["spoolT"] = f(m["spool"].reshape(NL, 16, 15, 4, 128).transpose(0, 4, 3, 1, 2))
        m["sconvT"] = f(m["sconv"].reshape(NL, 16, 3, 24, 128).transpose(0, 4, 3, 1, 2))
        in_maps.append(m)
    res = run_bass_kernel_spmd(nc, in_maps, core_ids=list(range(8)))
    R = res.results
    y_prompt = np.stack([R[c]["y_p"] for c in range(8)], axis=0)
    y_sample = np.concatenate([R[c]["y_s"].reshape(16, 8, D) for c in range(8)], axis=0)
    ncp = np.stack([R[c]["ncp"] for c in range(8)], axis=1)
    nsp = np.stack([R[c]["nsp"] for c in range(8)], axis=1)
    npp = np.stack([R[c]["npp"] for c in range(8)], axis=1)
    ncs = np.concatenate([R[c]["ncs"] for c in range(8)], axis=1)
    nss = np.concatenate([R[c]["nss"] for c in range(8)], axis=1)
    nps = np.concatenate([R[c]["nps"] for c in range(8)], axis=1)
    return tuple(np.ascontiguousarray(a, dtype=np.float32) for a in (y_prompt, y_sample, ncp, nsp, npp, ncs, nss, nps))
```

```python
import numpy as np
import concourse.bass as bass
import concourse.mybir as mybir
from concourse.bass_utils import run_bass_kernel_spmd

F32 = mybir.dt.float32
BF16 = mybir.dt.bfloat16
AF = mybir.ActivationFunctionType
ALU = mybir.AluOpType

ENGS = ('pe', 'act', 'dve', 'pool', 'sp')
MAXOPS = None


class _Op:
    __slots__ = ('eng', 'fn', 'waits', 'signal', 'dma', 'idx')

    def __init__(self, eng, fn, dma=None):
        self.eng = eng
        self.fn = fn
        self.waits = []
        self.signal = False
        self.dma = dma
        self.idx = 0


class Sched:
    NDMA = 24

    def __init__(self, nc):
        self.nc = nc
        self.ops = {e: [] for e in ENGS}
        self.recs = {}
        self.water = {e: {} for e in ENGS}
        self.dma_tot = [0] * self.NDMA
        self.dma_last = [None] * self.NDMA
        self.dma_rr = 0
        self.dma_rr_sw = 0
        self.tensors = []
        self.same_eng_dist = 1 << 30

    def sbuf(self, name, shape, dtype):
        t = self.nc.alloc_sbuf_tensor(name, list(shape), dtype)
        return t

    def psum(self, name, shape, dtype):
        return self.nc.alloc_psum_tensor(name, list(shape), dtype)

    @staticmethod
    def _box(ap):
        t = ap.tensor
        shp = list(t.shape)
        row = 1
        for s in shp[1:]:
            row *= s
        dsz = mybir.dt.size(ap.dtype)
        pat = list(ap.ap)
        off = ap.offset
        p0 = off // row
        f0 = (off % row) * dsz
        pstep, pcnt = pat[0]
        if pstep == 0:
            np_ = 1
        else:
            np_ = (pcnt - 1) * (pstep // row) + 1
        ext = 0
        for st, cnt in pat[1:]:
            ext += (cnt - 1) * abs(st)
        if type(t).__name__ == 'PSumTensorHandle':
            return (p0, p0 + np_, 0, 1 << 20)
        return (p0, p0 + np_, f0, f0 + (ext + 1) * dsz)

    def _track(self, op, aps_r, aps_w):
        deps = set()
        for kind, aps in (('r', aps_r), ('w', aps_w)):
            for ap in aps:
                name = ap.tensor.name
                box = self._box(ap)
                if type(ap.tensor).__name__ == 'PSumTensorHandle':
                    kind = 'w'
                recs = self.recs.setdefault(name, {})
                dead = []
                for (b, k, key), tok in recs.items():
                    if b[0] < box[1] and box[0] < b[1] and b[2] < box[3] and box[2] < b[3]:
                        if kind == 'w' or k == 'w':
                            deps.add(tok)
                        if kind == 'w' and box[0] <= b[0] and b[1] <= box[1] and box[2] <= b[2] and b[3] <= box[3]:
                            dead.append((b, k, key))
                for d in dead:
                    del recs[d]
        return deps

    def _record(self, op, tok, aps_r, aps_w):
        for kind, aps in (('r', aps_r), ('w', aps_w)):
            for ap in aps:
                name = ap.tensor.name
                box = self._box(ap)
                if type(ap.tensor).__name__ == 'PSumTensorHandle':
                    kind = 'w'
                key = tok[0] if tok[0] != 'dma' else ('dma', tok[1])
                self.recs.setdefault(name, {})[(box, kind, key)] = tok

    def _add_waits(self, op, deps):
        eng = op.eng
        my_idx = len(self.ops[eng])
        for tok in deps:
            if tok[0] == 'dma':
                _, s, val = tok
                key = ('dma', s)
                if self.water[eng].get(key, 0) >= val:
                    continue
                self.water[eng][key] = val
                op.waits.append(tok)
            else:
                f, i = tok
                if f == eng:
                    if eng in ('pe', 'sp'):
                        continue
                    if my_idx - i > self.same_eng_dist:
                        continue
                if self.water[eng].get(f, -1) >= i:
                    continue
                self.water[eng][f] = i
                self.ops[f][i].signal = True
                op.waits.append(tok)

    @staticmethod
    def _is_onchip(ap):
        return type(ap.tensor).__name__ in ('SBTensorHandle', 'PSumTensorHandle')

    def op(self, eng, fn, r, w):
        self.nrec = getattr(self, 'nrec', 0) + 1
        if MAXOPS is not None and self.nrec > MAXOPS:
            return None
        o = _Op(eng, fn)
        r = [a for a in r if self._is_onchip(a)]
        w = [a for a in w if self._is_onchip(a)]
        deps = self._track(o, r, w)
        self._add_waits(o, deps)
        o.idx = len(self.ops[eng])
        self.ops[eng].append(o)
        self._record(o, (eng, o.idx), r, w)
        return o

    def dma(self, queue, out, in_):
        self.nrec = getattr(self, 'nrec', 0) + 1
        if MAXOPS is not None and self.nrec > MAXOPS:
            return None
        if queue == 'pool':
            s = 16 + self.dma_rr_sw
            self.dma_rr_sw = (self.dma_rr_sw + 1) % 8
        else:
            s = self.dma_rr
            self.dma_rr = (self.dma_rr + 1) % 16
        o = _Op(queue, None, dma=(s, out, in_))
        r = [in_] if self._is_onchip(in_) else []
        w = [out] if self._is_onchip(out) else []
        deps = self._track(o, r, w)
        if self.dma_last[s] is not None:
            deps.add(self.dma_last[s])
        self._add_waits(o, deps)
        self.dma_tot[s] += 16
        tok = ('dma', s, self.dma_tot[s])
        self.dma_last[s] = tok
        o.idx = len(self.ops[queue])
        self.ops[queue].append(o)
        self._record(o, tok, r, w)
        return tok

    def mm(self, out, lhsT, rhs, start=True, stop=True):
        return self.op('pe', lambda e: e.matmul(out, lhsT, rhs, start=start, stop=stop), [lhsT, rhs], [out])

    def tr(self, out, in_, ident):
        return self.op('pe', lambda e: e.transpose(out, in_, ident), [in_, ident], [out])

    def trf(self, out, in_, ident):
        return self.op('pe', lambda e: e.matmul(out, in_, ident, start=True, stop=True), [in_, ident], [out])

    def act(self, out, in_, func, bias=None, scale=None, accum_out=None, eng='act'):
        kw = {}
        r = [in_]
        w = [out]
        if bias is not None:
            kw['bias'] = bias
            if not isinstance(bias, (int, float)):
                r.append(bias)
        if scale is not None:
            kw['scale'] = scale
            if not isinstance(scale, (int, float)):
                r.append(scale)
        if accum_out is not None:
            kw['accum_out'] = accum_out
            w.append(accum_out)
        return self.op('act', lambda e: e.activation(out, in_, func, **kw), r, w)

    def tt(self, eng, out, in0, in1, op):
        return self.op(eng, lambda e: e.tensor_tensor(out, in0, in1, op), [in0, in1], [out])

    def ts(self, eng, out, in0, s1, s2, op0, op1=None, accum_out=None):
        r = [in0]
        if not isinstance(s1, (int, float)) and s1 is not None:
            r.append(s1)
        if not isinstance(s2, (int, float)) and s2 is not None:
            r.append(s2)
        w = [out]
        kw = {}
        if op1 is not None:
            kw['op1'] = op1
        if accum_out is not None:
            kw['accum_out'] = accum_out
            w.append(accum_out)
        return self.op(eng, lambda e: e.tensor_scalar(out, in0, s1, s2, op0, **kw), r, w)

    def stt(self, out, in0, scalar, in1, op0, op1, eng='dve'):
        r = [in0, in1]
        if not isinstance(scalar, (int, float)):
            r.append(scalar)
        return self.op(eng, lambda e: e.scalar_tensor_tensor(out, in0, scalar, in1, op0, op1), r, [out])

    def cp(self, eng, out, in_):
        if eng == 'act':
            return self.op('act', lambda e: e.copy(out, in_), [in_], [out])
        return self.op(eng, lambda e: e.tensor_copy(out, in_), [in_], [out])

    def memset(self, eng, ap, val):
        return self.op(eng, lambda e: e.memset(ap, val), [], [ap])

    def emit(self, final_wait=True):
        nc = self.nc
        engobj = {'pe': nc.tensor, 'act': nc.scalar, 'dve': nc.vector, 'pool': nc.gpsimd, 'sp': nc.sync}
        sems = {e: nc.alloc_semaphore("prog_" + e) for e in ENGS}
        dsems = [nc.alloc_semaphore("dma_%d" % i) for i in range(self.NDMA)]
        cnt = {}
        for e in ENGS:
            c = 0
            arr = []
            for o in self.ops[e]:
                if o.signal and o.dma is None:
                    c += 1
                arr.append(c)
            cnt[e] = arr
        blockattr = {'pe': 'tensor', 'act': 'scalar', 'dve': 'vector', 'pool': 'gpsimd', 'sp': 'sync'}
        with nc.Block() as block:
            for e in ENGS:
                ops = self.ops[e]

                def body(eng, e=e, ops=ops):
                    for o in ops:
                        for tok in o.waits:
                            if tok[0] == 'dma':
                                eng.wait_ge(dsems[tok[1]], tok[2])
                            else:
                                eng.wait_ge(sems[tok[0]], cnt[tok[0]][tok[1]])
                        if o.dma is not None:
                            s, out, in_ = o.dma
                            eng.dma_start(out=out, in_=in_).then_inc(dsems[s], 16)
                        else:
                            ins = o.fn(eng)
                            if o.signal:
                                ins.then_inc(sems[e], 1)
                    if e == 'sp' and final_wait:
                        for s in range(self.NDMA):
                            if self.dma_tot[s] > 0:
                                eng.wait_ge(dsems[s], self.dma_tot[s])
                getattr(block, blockattr[e])(body)


D = 1024
NH = 8
IN_W = 6672
DFF = 4096
NL = 2
TPG = 6
NT = TPG * 128
CH = 384
NCH = NT // CH
EPS = 1e-6
POOL_W = (2, 4, 8, 16)
C_ID, C_UP, C_US, C_SLP, C_SLS, C_SAMES, C_ONES, C_NEGP, C_NEGS, C_BLK, C_INVC = (
    0, 128, 256, 384, 512, 640, 768, 896, 1024, 1152, 1168)
NCONST = 1232
GROUPS = [[('P', i) for i in range(0, 6)], [('P', i) for i in range(6, 12)],
          [('P', i) for i in range(12, 17)] + [('S', 0)]]
CHAIN_F32 = True
DEBUG = None


def make_consts():
    c = np.zeros((128, NCONST), np.float32)
    p = np.arange(128)
    same = (p[:, None] // 8) == (p[None, :] // 8)
    c[:, C_ID:C_ID + 128] = np.eye(128)
    c[:, C_UP:C_UP + 128] = (p[:, None] <= p[None, :])
    c[:, C_US:C_US + 128] = (p[:, None] <= p[None, :]) & same
    c[:, C_SLP:C_SLP + 128] = (p[:, None] > p[None, :])
    c[:, C_SLS:C_SLS + 128] = (p[:, None] > p[None, :]) & same
    c[:, C_SAMES:C_SAMES + 128] = same
    c[:, C_ONES:C_ONES + 128] = 1.0
    c[:, C_NEGP:C_NEGP + 128] = np.where(p[:, None] < p[None, :], 0.0, -30000.0)
    c[:, C_NEGS:C_NEGS + 128] = np.where((p[:, None] < p[None, :]) & same, 0.0, -30000.0)
    c[:, C_BLK:C_BLK + 16] = (p[:, None] // 8) == np.arange(16)[None, :]
    for gi, w in enumerate(POOL_W):
        for pos in range(16):
            c[:, C_INVC + gi * 16 + pos] = 1.0 / min(pos + 1, w)
    return c


def build_program():
    nc = bass.Bass("TRN2", target_bir_lowering=False)

    def din(name, shape):
        return nc.dram_tensor(name, list(shape), F32, kind="ExternalInput").ap()

    def dout(name, shape):
        return nc.dram_tensor(name, list(shape), F32, kind="ExternalOutput").ap()

    xp = din("xp", [2048, D]); xs = din("xs", [128, D]); meta = din("meta", [16, D])
    sconv = din("sconv", [NL, 16, 3, 3072]); sssm = din("sssm", [NL, 16, NH, 128, 128])
    spool = din("spool", [NL, 16, 15, 512])
    spoolT = din("spoolT", [NL, 128, 4, 16, 15]); sconvT = din("sconvT", [NL, 128, 24, 16, 3])
    w_in = din("w_in", [NL, D, IN_W]); pool_w = din("pool_w", [NL, 4, 128, 128])
    wbp = din("wbp", [NL, 512, D]); wbd = din("wbd", [NL, D, D]); wo = din("wo", [NL, D, D])
    wup = din("wup", [NL, D, DFF]); wdn = din("wdn", [NL, DFF, D])
    consts = din("consts", [128, NCONST]); gpre_d = din("gpre", [128, NL, 2, 8])
    cw_d = din("cw", [128, NL, 24, 4]); pscale_d = din("pscale", [128, NL, 4])
    gpost_d = din("gpost", [128, NL, 2, D]); ong_d = din("ong", [128, NL, 128])
    alog_d = din("alog", [128, NL, 8]); dtb_d = din("dtb", [128, NL, 8])
    y_p = dout("y_p", [2048, D]); y_s = dout("y_s", [128, D])
    ncp = dout("ncp", [NL, 3, 3072]); nsp = dout("nsp", [NL, NH, 128, 128]); npp = dout("npp", [NL, 15, 512])
    ncs = dout("ncs", [NL, 16, 3, 3072]); nss = dout("nss", [NL, 16, NH, 128, 128]); nps = dout("nps", [NL, 16, 15, 512])

    S = Sched(nc)
    CD = F32 if CHAIN_F32 else BF16

    X = S.sbuf("X", [128, TPG, D], F32)
    HT = S.sbuf("HT", [128, 8, NT], BF16)
    BIG = S.sbuf("BIG", [128, 16 * NT], BF16)
    UT = BIG[:, :].rearrange("p (a b) -> p a b", b=NT)
    QT4 = BIG[:, 0:4 * NT].rearrange("p (a b) -> p a b", b=NT)
    KT4 = BIG[:, 4 * NT:8 * NT].rearrange("p (a b) -> p a b", b=NT)
    VT4 = BIG[:, 8 * NT:12 * NT].rearrange("p (a b) -> p a b", b=NT)
    ZS4 = BIG[:, 12 * NT:16 * NT].rearrange("p (t c) -> p t c", c=512)
    MT = BIG[:, 0:8 * NT].rearrange("p (a b) -> p a b", b=NT)
    DM = S.sbuf("DM", [128, 16 * NT], BF16)
    DOT = DM[:, 0:8 * NT].rearrange("p (a b) -> p a b", b=NT)
    AOT = DM[:, 8 * NT:12 * NT].rearrange("p (a b) -> p a b", b=NT)
    FF = DM[:, :].bitcast(F32).rearrange("p (t c) -> p t c", c=D)
    NSLOT = 5
    PREF = 4
    WR = [S.sbuf("WR%d" % i, [128, 8, 512], BF16) for i in range(NSLOT)]
    CST = S.sbuf("CST", [128, NCONST], F32)
    IDB = S.sbuf("IDB", [128, 128], BF16)
    GPRE = S.sbuf("GPRE", [128, NL, 2, 8], F32)
    CW = S.sbuf("CW", [128, NL, 24, 4], F32)
    PSC = S.sbuf("PSC", [128, NL, 4], F32)
    ONG = S.sbuf("ONG", [128, NL, 128], F32)
    NEGA = S.sbuf("NEGA", [128, NL, 8], F32)
    DTB = S.sbuf("DTB", [128, NL, 8], F32)
    EPSC = S.sbuf("EPSC", [128, 2], F32)
    S_ST = S.sbuf("S_ST", [128, NL, NH, 128], F32)
    CPRE = S.sbuf("CPRE", [128, NL, 24, 3], F32)
    PPRE = S.sbuf("PPRE", [128, NL, 4, 15], F32)
    PWL = S.sbuf("PWL", [128, 4, 128], BF16)
    WBA = S.sbuf("WBA", [128, 8, 16], BF16)
    STAT = S.sbuf("STAT", [128, 64], F32)
    BETA = S.sbuf("BETA", [128, TPG, 8], F32)
    NBETA = S.sbuf("NBETA", [128, TPG, 8], F32)
    GG = S.sbuf("GG", [128, TPG, 8], F32)
    TA = S.sbuf("TA", [128, 8], F32)
    EGC = S.sbuf("EGC", [128, 8], F32)
    EGD = S.sbuf("EGD", [128, 8], F32)
    EGL = S.sbuf("EGL", [128, 8], F32)
    GCs = S.sbuf("GCs", [128, 8], F32)
    GBS = S.sbuf("GBS", [128, 8, 16], F32)
    EGLS = S.sbuf("EGLS", [128, 8, 16], F32)
    SPOOLT = S.sbuf("SPOOLT", [128, 4, 16, 15], F32)
    SCONVT = S.sbuf("SCONVT", [128, 24, 16, 3], F32)

    SCR_BYTES = 52 * 1024
    SCR = S.sbuf("SCR", [128, SCR_BYTES // 2], BF16)

    class Arena:
        def __init__(self):
            self.off = 0

        def take(self, free, dtype):
            n = 1
            for f in free:
                n *= f
            nb = n * mybir.dt.size(dtype)
            nb_al = (nb + 31) // 32 * 32
            assert self.off + nb_al <= SCR_BYTES, ("arena overflow", self.off, nb_al)
            ap = SCR[:, self.off // 2:(self.off + nb) // 2]
            self.off += nb_al
            if dtype != BF16:
                ap = ap.bitcast(dtype)
            if len(free) == 2:
                ap = ap.rearrange("p (a b) -> p a b", b=free[1])
            elif len(free) == 3:
                ap = ap.rearrange("p (a b c) -> p a b c", b=free[1], c=free[2])
            return ap

    EXTW = 16 + NT
    A = Arena()
    JUNK = A.take([D], BF16)
    HB = [A.take([D], BF16) for _ in range(2)]
    offA0 = A.off
    UEXT = [A.take([EXTW], F32) for _ in range(2)]
    SCEXT = A.take([16, 11], F32)
    TMROW = A.take([512], F32)
    offA1 = A.off
    PA = A.take([EXTW], F32)
    PBf = A.take([EXTW], F32)
    MTP = A.take([NT], BF16)
    SEXT = A.take([16, 23], F32)
    SPA = A.take([16, 23], F32)
    SPB = A.take([16, 23], F32)
    A.off = offA1
    CQ2 = [A.take([NT], F32) for _ in range(2)]
    SQ2_2 = [A.take([NT], F32) for _ in range(2)]
    SQ_4 = [A.take([NT], F32) for _ in range(4)]
    RN_4 = [A.take([NT], F32) for _ in range(4)]
    A.off = offA0
    GPOSTb = A.take([D], F32)
    TMPX = [A.take([512], F32) for _ in range(2)]
    SGT = [A.take([CH], F32) for _ in range(4)]
    T12 = [A.take([CH], F32) for _ in range(4)]
    RL = [A.take([CH], F32) for _ in range(2)]
    G = Arena()
    NB = 4
    Bm = [G.take([128], F32) for _ in range(NB)]
    DTS = [G.take([128], F32) for _ in range(NB)]
    DTM = [G.take([128], F32) for _ in range(NB)]
    CA = [[G.take([128], CD) for _ in range(2)] for _ in range(NB)]
    CAT = [[G.take([128], CD) for _ in range(2)] for _ in range(NB)]
    CR = [[G.take([128], CD) for _ in range(2)] for _ in range(NB)]
    KE = [G.take([128], CD) for _ in range(NB)]
    KDEC = [G.take([128], BF16) for _ in range(NB)]
    VTM = [G.take([128], CD) for _ in range(NB)]
    WTt = [G.take([128], BF16) for _ in range(NB)]
    UB = [G.take([128], F32) for _ in range(NB)]
    VN = [G.take([128], BF16) for _ in range(NB)]
    ATt = [G.take([128], BF16) for _ in range(NB)]
    O1 = [G.take([128], F32) for _ in range(NB)]
    OO = [G.take([128], F32) for _ in range(NB)]
    DD = [G.take([128], BF16) for _ in range(NB)]
    SBF = [G.take([128], BF16) for _ in range(NB)]
    JG = [G.take([128], BF16) for _ in range(NB)]
    FMS = [G.take([128], F32) for _ in range(2)]
    S0F = G.take([16, 128], F32)
    S0B = G.take([16, 128], BF16)
    VBLK = G.take([16, 128], BF16)

    PB = [S.psum("PB%d" % i, [128, 512], F32) for i in range(6)]
    PH = [S.psum("PH%d" % i, [128, 1024], BF16) for i in range(2)]
    qctr = [0]

    def pq_align():
        qctr[0] = (qctr[0] + 3) // 4 * 4 % 8

    def pq():
        i = qctr[0]
        qctr[0] = (i + 1) % 8
        return PB[4 + i // 4][:, (i % 4) * 128:(i % 4 + 1) * 128]
    bctr = [0]

    def pbank():
        i = bctr[0]
        bctr[0] = (i + 1) % 6
        return PB[i]
    hctr = [0]
    hq = [0, 0, 0, 0]

    def pqb_align():
        hctr[0] = (hctr[0] + 3) // 4 * 4 % 16

    def pqb():
        i = hctr[0]
        hctr[0] = (i + 1) % 16
        slot = (i % 4) + 4 * ((i // 8) % 2)
        return PH[(i // 4) % 2][:, slot * 128:(slot + 1) * 128]

    ident = CST[:, C_ID:C_ID + 128]
    ident_cd = ident if CHAIN_F32 else IDB[:, :]
    ones_f = CST[:, C_ONES:C_ONES + 128]
    blk16 = CST[:, C_BLK:C_BLK + 16]
    eps_c = EPSC[:, 0:1]
    one_c = EPSC[:, 1:2]

    wlist = []

    def wsrc_k8(wt, l, c0, ncols=512):
        return wt[l, :, c0:c0 + ncols].rearrange("(dc p) c -> p dc c", p=128)

    wstate = {'issued': 0}

    def wissue(upto):
        while wstate['issued'] <= upto and wstate['issued'] < len(wlist):
            n = wstate['issued']
            src, kc, ncol = wlist[n]
            S.dma('pool', WR[n % NSLOT][:, 0:kc, 0:ncol], src)
            wstate['issued'] += 1

    wuse = [0]

    def wget(n, pref=None):
        pref = PREF if pref is None else pref
        if DEBUG is None:
            assert n == wuse[0], ("weight order mismatch", n, wuse[0])
        wuse[0] += 1
        if DEBUG is None:
            wissue(n + pref)
        else:
            src, kc, ncol = wlist[n]
            S.dma('pool', WR[n % NSLOT][:, 0:kc, 0:ncol], src)
        return WR[n % NSLOT]

    widx = {}
    for g in range(len(GROUPS)):
        for l in range(NL):
            def add(key, src, kc=8, ncol=512):
                widx[(g, l) + key] = len(wlist)
                wlist.append((src, kc, ncol))
            add(('u',), wsrc_k8(w_in, l, 0))
            for hb in range(2):
                for comp in range(3):
                    add(('qkv', comp, hb), wsrc_k8(w_in, l, 512 + comp * 1024 + hb * 512))
                add(('z', hb), wsrc_k8(w_in, l, 3600 + hb * 512))
            for half in range(2):
                add(('wbp', half), wbp[l, :, half * 512:(half + 1) * 512].rearrange("(dc p) c -> p dc c", p=128), kc=4)
                add(('wbd', half), wsrc_k8(wbd, l, half * 512))
                add(('gp', half), wsrc_k8(w_in, l, 4624 + half * 512))
                add(('gd', half), wsrc_k8(w_in, l, 5648 + half * 512))
            for half in range(2):
                add(('wo', half), wsrc_k8(wo, l, half * 512))
            for fh in range(2):
                for f4 in range(4):
                    add(('up', fh * 4 + f4), wsrc_k8(wup, l, (fh * 4 + f4) * 512))
                for half in range(2):
                    for kl in range(2):
                        kcg = fh * 2 + kl
                        add(('dn', half, kcg),
                            wdn[l, kcg * 1024:(kcg + 1) * 1024, half * 512:(half + 1) * 512].rearrange("(dc p) c -> p dc c", p=128))

    S.dma('sp', CST[:], consts)
    S.dma('sp', GPRE[:], gpre_d)
    S.dma('sp', CW[:], cw_d)
    S.dma('sp', PSC[:], pscale_d)
    S.dma('sp', ONG[:], ong_d)
    S.dma('sp', NEGA[:], alog_d)
    S.dma('sp', DTB[:], dtb_d)
    S.cp('dve', IDB[:], ident)
    S.memset('dve', EPSC[:, 0:1], EPS)
    S.memset('dve', EPSC[:, 1:2], 1.0)
    S.act(NEGA[:], NEGA[:], AF.Exp)
    S.ts('pool', NEGA[:], NEGA[:], -1.0, None, ALU.mult)
    S.memset('pool', S_ST[:], 0.0)
    S.memset('pool', CPRE[:], 0.0)
    S.memset('pool', PPRE[:], 0.0)

    def rstd_from(out, ssq, n):
        S.act(out, ssq, AF.Ln, bias=eps_c, scale=1.0 / n)
        S.act(out, out, AF.Exp, scale=-0.5)

    def group_info(g):
        tiles = GROUPS[g]
        npt = sum(1 for t in tiles if t[0] == 'P')
        has_s = any(t[0] == 'S' for t in tiles)
        return tiles, npt, npt * 128, has_s

    def chunk_ranges(c, NW, has_s):
        a, b = c * CH, min((c + 1) * CH, NW)
        pr = (a, b) if b > a else None
        sr = has_s and (c + 1) * CH == NT
        return pr, sr

    def load_x(g):
        tiles, npt, NW, has_s = group_info(g)
        ti = 0
        if tiles[0] == ('P', 0):
            S.memset('pool', X[:, 0, :], 0.0)
            S.dma('sp', X[112:128, 0, :], meta)
            ti = 1
        if npt > ti:
            pt0 = tiles[ti][1]
            S.dma('sp', X[:, ti:npt, :], xp[(pt0 - 1) * 128:(pt0 - 1 + npt - ti) * 128, :].rearrange("(t p) c -> p t c", p=128))
        if has_s:
            S.dma('sp', X[:, TPG - 1, :], xs)

    def store_x(g):
        tiles, npt, NW, has_s = group_info(g)
        ti = 1 if tiles[0] == ('P', 0) else 0
        if npt > ti:
            pt0 = tiles[ti][1]
            S.dma('sp', y_p[(pt0 - 1) * 128:(pt0 - 1 + npt - ti) * 128, :].rearrange("(t p) c -> p t c", p=128), X[:, ti:npt, :])
        if has_s:
            S.dma('sp', y_s, X[:, TPG - 1, :])

    def norm_to_HT(l, which):
        for ti in range(TPG):
            ssq = STAT[:, ti:ti + 1]
            rs = STAT[:, 8 + ti:9 + ti]
            S.act(JUNK, X[:, ti, :], AF.Square, accum_out=ssq)
            rstd_from(rs, ssq, D)
            hb = HB[ti % 2]
            S.ts('dve', hb, X[:, ti, :], rs, None, ALU.mult)
            ph = PH[ti % 2]
            for dc in range(8):
                S.tr(ph[:, dc * 128:(dc + 1) * 128], hb[:, dc * 128:(dc + 1) * 128], IDB[:])
            S.tt('dve', HT[:, :, ti * 128:(ti + 1) * 128],
                 ph[:, :].rearrange("p (a b) -> p a b", b=128),
                 GPRE[:, l, which, :].to_broadcast([128, 8, 128]), ALU.mult)

    def fm_proj(wtile, c0, rhs_buf, nk, cidx):
        acc = pbank()[:, 0:CH]
        for kc in range(nk):
            S.mm(acc, wtile[:, kc, c0:c0 + 128], rhs_buf[:, kc, cidx * CH:(cidx + 1) * CH],
                 start=(kc == 0), stop=(kc == nk - 1))
        return acc

    def tm_proj(wtile, ti, src_buf, nk=8, ncol=512):
        acc = pbank()[:, 0:ncol]
        for kc in range(nk):
            S.mm(acc, src_buf[:, kc, ti * 128:(ti + 1) * 128], wtile[:, kc, 0:ncol],
                 start=(kc == 0), stop=(kc == nk - 1))
        return acc

    def stage_pool(g, l):
        tiles, npt, NW, has_s = group_info(g)
        W0 = wget(widx[(g, l, 'u')])
        S.dma('pool', PWL[:], pool_w[l].rearrange("g c d -> c g d"))
        if has_s:
            S.dma('sp', SPOOLT[:], spoolT[l])
            S.dma('sp', nps[l, :, 0:7, :], spool[l, :, 8:15, :])
            acc = tm_proj(W0, npt - 1, HT)
            S.cp('act', TMROW, acc)
            S.dma('sp', npp[l], TMROW[113:128, :])
            acc = tm_proj(W0, TPG - 1, HT)
            S.cp('act', TMROW, acc)
            for t in range(8):
                S.dma('sp', nps[l, :, 7 + t, :], TMROW[t::8, :])
        for gi in range(4):
            w = POOL_W[gi]
            ue = UEXT[gi % 2]
            S.cp('pool', ue[:, 1:16], PPRE[:, l, gi, :])
            for c in range(NCH):
                acc = fm_proj(W0, gi * 128, HT, 8, c)
                pr, sr = chunk_ranges(c, NW, has_s)
                if pr:
                    S.cp('act', ue[:, 16 + pr[0]:16 + pr[1]], acc[:, pr[0] - c * CH:pr[1] - c * CH])
                if sr:
                    S.cp('act', SEXT[:, :, 15:23], acc[:, CH - 128:CH].rearrange("p (s t) -> p s t", t=8))
            Wd = 16 + NW
            S.cp('pool', PPRE[:, l, gi, :], ue[:, Wd - 15:Wd])
            src = ue
            for j in range(gi + 1):
                k = 1 << j
                sh = (1 << (j + 1))
                dst = PA if j % 2 == 0 else PBf
                S.tt('pool', dst[:, sh:Wd], src[:, sh:Wd], src[:, sh - k:Wd - k], ALU.add)
                src = dst
            S.stt(MTP[:, 0:NW], src[:, 16:Wd], 1.0 / w, ue[:, 16:Wd], ALU.mult, ALU.subtract)
            if g == 0:
                tmp = STAT[:, 32:48]
                S.tt('dve', tmp, src[:, 16 + 112:16 + 128], CST[:, C_INVC + gi * 16:C_INVC + gi * 16 + 16], ALU.mult)
                S.tt('dve', MTP[:, 112:128], tmp, ue[:, 16 + 112:16 + 128], ALU.subtract)
            if has_s:
                S.cp('pool', SEXT[:, :, 0:15], SPOOLT[:, gi, :, :])
                ssrc = SEXT
                for j in range(gi + 1):
                    k = 1 << j
                    sh = (1 << (j + 1)) - 1
                    dst = SPA if j % 2 == 0 else SPB
                    S.tt('pool', dst[:, :, sh:23], ssrc[:, :, sh:23], ssrc[:, :, sh - k:23 - k], ALU.add)
                    ssrc = dst
                S.stt(MTP[:, NW:NT].rearrange("p (s t) -> p s t", t=8), ssrc[:, :, 15:23], 1.0 / w,
                      SEXT[:, :, 15:23], ALU.mult, ALU.subtract)
            for c in range(NCH):
                acc = pbank()[:, 0:CH]
                S.mm(acc, PWL[:, gi, :], MTP[:, c * CH:(c + 1) * CH])
                S.act(AOT[:, gi, c * CH:(c + 1) * CH], acc, AF.Copy, scale=PSC[:, l, gi:gi + 1])

    def stage_sconv_prep(l):
        S.dma('sp', SCONVT[:], sconvT[l])

    def stage_qkv(g, l, hb):
        tiles, npt, NW, has_s = group_info(g)
        for comp in range(3):
            Wt = wget(widx[(g, l, 'qkv', comp, hb)])
            if has_s:
                colbase = comp * 1024 + hb * 512
                acc = tm_proj(Wt, npt - 1, HT)
                S.cp('act', TMROW, acc)
                S.dma('sp', ncp[l, :, colbase:colbase + 512], TMROW[125:128, :])
                acc = tm_proj(Wt, TPG - 1, HT)
                S.cp('act', TMROW, acc)
                for t in range(3):
                    S.dma('sp', ncs[l, :, t, colbase:colbase + 512], TMROW[5 + t::8, :])
            def a1(hl):
                h = hb * 4 + hl
                ch = comp * 8 + h
                c0 = hl * 128
                ext = UEXT[ch % 2]
                S.cp('pool', ext[:, 0:3], CPRE[:, l, ch, :])
                for c in range(NCH):
                    acc = fm_proj(Wt, c0, HT, 8, c)
                    pr, sr = chunk_ranges(c, NW, has_s)
                    if pr:
                        S.cp('act', ext[:, 3 + pr[0]:3 + pr[1]], acc[:, pr[0] - c * CH:pr[1] - c * CH])
                    if sr:
                        S.cp('act', SCEXT[:, :, 3:11], acc[:, CH - 128:CH].rearrange("p (s t) -> p s t", t=8))
                S.cp('pool', CPRE[:, l, ch, :], ext[:, NW:NW + 3])
                if has_s:
                    CQ = CQ2[ch % 2]
                    S.cp('pool', SCEXT[:, :, 0:3], SCONVT[:, ch, :, :])
                    cqs = CQ[:, NW:NT].rearrange("p (s t) -> p s t", t=8)
                    S.ts('dve', cqs, SCEXT[:, :, 0:8], CW[:, l, ch, 0:1], None, ALU.mult)
                    for j in range(1, 4):
                        S.stt(cqs, SCEXT[:, :, j:j + 8], CW[:, l, ch, j:j + 1], cqs, ALU.mult, ALU.add)

            def a2(hl):
                h = hb * 4 + hl
                ch = comp * 8 + h
                ext = UEXT[ch % 2]
                CQ, SQ2, SQ, RN = CQ2[ch % 2], SQ2_2[ch % 2], SQ_4[hl], RN_4[hl]
                S.ts('dve', CQ[:, 0:NW], ext[:, 0:NW], CW[:, l, ch, 0:1], None, ALU.mult)
                for j in range(1, 4):
                    S.stt(CQ[:, 0:NW], ext[:, j:NW + j], CW[:, l, ch, j:j + 1], CQ[:, 0:NW], ALU.mult, ALU.add)
                if comp == 2:
                    S.act(VT4[:, hl, :], CQ, AF.Silu)
                else:
                    S.act(SQ, CQ, AF.Silu)
                    S.tt('pool', SQ2, SQ, SQ, ALU.mult)
                    for c in range(NCH):
                        acc = pbank()[:, 0:CH]
                        S.mm(acc, ones_f, SQ2[:, c * CH:(c + 1) * CH])
                        S.cp('act', RN[:, c * CH:(c + 1) * CH], acc)

            def phase_b(hl):
                SQ, RN = SQ_4[hl], RN_4[hl]
                S.act(RN, RN, AF.Ln, bias=eps_c, scale=1.0)
                S.act(RN, RN, AF.Exp, scale=-0.5)
                dst = QT4 if comp == 0 else KT4
                sc = (128.0 ** -0.5) if comp == 0 else 1.0
                S.stt(dst[:, hl, :], SQ, sc, RN, ALU.mult, ALU.mult)

            a1(0)
            for hl in range(4):
                if hl + 1 < 4:
                    a1(hl + 1)
                a2(hl)
            if comp != 2:
                for hl in range(4):
                    phase_b(hl)

    def stage_ba(g, l):
        S.dma('pool', WBA[:], w_in[l, :, 3584:3600].rearrange("(dc p) c -> p dc c", p=128))
        for ti in range(TPG):
            acc = pq()[:, 0:16]
            for kc in range(8):
                S.mm(acc, HT[:, kc, ti * 128:(ti + 1) * 128], WBA[:, kc, :], start=(kc == 0), stop=(kc == 7))
            S.act(BETA[:, ti, :], acc[:, 0:8], AF.Sigmoid)
            S.cp('act', TA[:], acc[:, 8:16])
            S.tt('dve', TA[:], TA[:], DTB[:, l, :], ALU.add)
            S.act(TA[:], TA[:], AF.Exp)
            S.act(TA[:], TA[:], AF.Ln, bias=one_c, scale=1.0)
            S.tt('dve', GG[:, ti, :], TA[:], NEGA[:, l, :], ALU.mult)
        S.ts('pool', NBETA[:], BETA[:], -1.0, None, ALU.mult)

    def stage_z(g, l, hb):
        Wt = wget(widx[(g, l, 'z', hb)])
        for ti in range(TPG):
            acc = tm_proj(Wt, ti, HT)
            S.act(ZS4[:, ti, :], acc, AF.Silu)
            zv = ZS4[:, ti, :].rearrange("p (h v) -> p h v", v=128)
            S.tt('pool', zv, zv, ONG[:, l, :].unsqueeze(1).broadcast_to([128, 4, 128]), ALU.mult)

    def stage_gdn(g, l, hb):
        tiles, npt, NW, has_s = group_info(g)
        for ti, (kind, pidx) in enumerate(tiles):
            isS = (kind == 'S')
            Umat = CST[:, C_US:C_US + 128] if isS else CST[:, C_UP:C_UP + 128]
            SLmat = CST[:, C_SLS:C_SLS + 128] if isS else CST[:, C_SLP:C_SLP + 128]
            SAMEmat = CST[:, C_SAMES:C_SAMES + 128] if isS else ones_f
            NEGmat = CST[:, C_NEGS:C_NEGS + 128] if isS else CST[:, C_NEGP:C_NEGP + 128]
            L = 3 if isS else 6
            tsl = slice(ti * 128, (ti + 1) * 128)
            gc_ps = pq()[:, 0:8]
            S.mm(gc_ps, Umat, GG[:, ti, :])
            gl_ps = pq()[:, 0:8]
            S.mm(gl_ps, SAMEmat, GG[:, ti, :])
            S.act(EGC[:], gc_ps, AF.Exp)
            S.cp('act', GCs[:], gc_ps)
            S.cp('act', EGD[:], gl_ps)
            S.act(EGL[:], gl_ps, AF.Exp)
            S.tt('dve', EGD[:], EGD[:], GCs[:], ALU.subtract)
            S.act(EGD[:], EGD[:], AF.Exp)
            if isS:
                S.tt('pool', GBS[:], GG[:, ti, :].to_broadcast([128, 8, 16]),
                     blk16.unsqueeze(1).broadcast_to([128, 8, 16]), ALU.mult)
                egp = pq()
                S.mm(egp, ones_f, GBS[:, :, :].rearrange("p a b -> p (a b)"))
                S.act(EGLS[:, :, :].rearrange("p a b -> p (a b)"), egp, AF.Exp)
            head_sets = [[0], [1], [2], [3]] if isS else [[0, 1, 2, 3]]
            for hls in head_sets:
                gdn_heads(g, l, hb, ti, isS, hls, Umat, SLmat, NEGmat, L, tsl)
        if g == len(GROUPS) - 1 and hb == 1:
            S.dma('sp', nsp[l].rearrange("h k v -> k h v"), S_ST[:, l, :, :])

    def gdn_heads(g, l, hb, ti, isS, hls, Umat, SLmat, NEGmat, L, tsl):
        H = [(hl, hb * 4 + hl) for hl in hls]
        lock4 = (len(hls) == 4)

        def pqx(hl):
            if not lock4:
                return pq()
            q = hq[hl]
            hq[hl] = (q + 1) % 4
            return PB[hl][:, q * 128:(q + 1) * 128]
        dps = {}
        for hl, h in H:
            S.ts('pool', Bm[hl], SLmat, GG[:, ti, h:h + 1], None, ALU.mult)
        pq_align()
        for hl, h in H:
            d = pqx(hl)
            S.mm(d, Bm[hl], Umat, start=True, stop=False)
            S.mm(d, ident, NEGmat, start=False, stop=True)
            dps[hl] = d
        for hl, h in H:
            S.act(DTS[hl], dps[hl], AF.Exp)
        for hl, h in H:
            S.tt('pool', DTM[hl], DTS[hl], ident, ALU.add)
        gks = {}
        pq_align()
        for hl, h in H:
            gk = pqx(hl)
            S.mm(gk, KT4[:, hl, tsl], KT4[:, hl, tsl])
            gks[hl] = gk
        for hl, h in H:
            S.stt(CA[hl][0], gks[hl], BETA[:, ti, h:h + 1], DTS[hl], ALU.mult, ALU.mult)
        pts = {}
        pq_align()
        pqb_align()
        for hl, h in H:
            p_ = pqx(hl) if CHAIN_F32 else pqb()
            if CHAIN_F32:
                S.trf(p_, CA[hl][0], ident_cd)
            else:
                S.tr(p_, CA[hl][0], ident_cd)
            pts[hl] = p_
        for hl, h in H:
            S.cp('act', CAT[hl][0], pts[hl])
            S.tt('pool', CR[hl][0], ident_cd, CA[hl][0], ALU.subtract)
        kts, vts = {}, {}
        pqb_align()
        for hl, h in H:
            kt_ = pqb()
            S.tr(kt_, KT4[:, hl, tsl], IDB[:])
            kts[hl] = kt_
        pqb_align()
        for hl, h in H:
            vt_ = pqb()
            S.tr(vt_, VT4[:, hl, tsl], IDB[:])
            vts[hl] = vt_
        for hl, h in H:
            S.ts('dve', KE[hl], kts[hl], EGC[:, h:h + 1], None, ALU.mult)
            S.ts('dve', KDEC[hl], kts[hl], EGD[:, h:h + 1], None, ALU.mult)
            S.cp('act', VTM[hl], vts[hl])
        for k in range(1, L + 1):
            a_ps, at_ps = {}, {}
            pq_align()
            if k < L:
                for hl, h in H:
                    a = pqx(hl)
                    S.mm(a, CAT[hl][(k - 1) % 2], CA[hl][(k - 1) % 2])
                    a_ps[hl] = a
                pq_align()
            for hl, h in H:
                at = pqx(hl)
                S.mm(at, CA[hl][(k - 1) % 2], CAT[hl][(k - 1) % 2])
                at_ps[hl] = at
            for hl, h in H:
                if k < L:
                    S.cp('act', CA[hl][k % 2], a_ps[hl])
                S.cp('dve', CAT[hl][k % 2], at_ps[hl])
            r_ps = {}
            pq_align()
            for hl, h in H:
                r = pqx(hl)
                S.mm(r, CAT[hl][k % 2], CR[hl][(k - 1) % 2])
                r_ps[hl] = r
            for hl, h in H:
                S.tt('dve', CR[hl][k % 2], r_ps[hl], CR[hl][(k - 1) % 2], ALU.add)
        XT = {hl: CR[hl][L % 2] for hl, h in H}
        wps, ups = {}, {}
        pq_align()
        for hl, h in H:
            w_ = pqx(hl)
            S.mm(w_, KE[hl], XT[hl])
            wps[hl] = w_
            u_ = pqx(hl)
            S.mm(u_, XT[hl], VTM[hl])
            ups[hl] = u_
        for hl, h in H:
            S.cp('act', WTt[hl], wps[hl])
            S.act(UB[hl], ups[hl], AF.Copy, scale=BETA[:, ti, h:h + 1])
        wss = {}
        if not isS:
            for hl, h in H:
                S.cp('pool', SBF[hl], S_ST[:, l, h, :])
            pq_align()
            for hl, h in H:
                ws = pqx(hl)
                S.mm(ws, WTt[hl], SBF[hl])
                wss[hl] = ws
        else:
            for hl, h in H:
                S.dma('sp', S0F, sssm[l, :, h].rearrange("s k v -> k s v"))
                S.dma('pool', S0B, sssm[l, :, h].rearrange("s k v -> k s v"))
                wsT = pqx(hl)
                for s in range(16):
                    S.mm(wsT[:, 8 * s:8 * s + 8], S0B[:, s, :], WTt[hl][:, 8 * s:8 * s + 8])
                S.cp('act', FMS[0], wsT)
                ws = pqx(hl)
                S.trf(ws, FMS[0], ident)
                wss[hl] = ws
        for hl, h in H:
            S.stt(VN[hl], wss[hl], NBETA[:, ti, h:h + 1], UB[hl], ALU.mult, ALU.add)
        aps = {}
        pq_align()
        for hl, h in H:
            a_ = pqx(hl)
            S.mm(a_, KT4[:, hl, tsl], QT4[:, hl, tsl])
            aps[hl] = a_
        for hl, h in H:
            S.tt('dve', ATt[hl], aps[hl], DTM[hl], ALU.mult)
        avs, qss = {}, {}
        pq_align()
        for hl, h in H:
            av = pqx(hl)
            S.mm(av, ATt[hl], VN[hl])
            avs[hl] = av
        pq_align()
        for hl, h in H:
            if not isS:
                qs = pqx(hl)
                S.mm(qs, QT4[:, hl, tsl], SBF[hl])
            else:
                qsT = pqx(hl)
                for s in range(16):
                    S.mm(qsT[:, 8 * s:8 * s + 8], S0B[:, s, :], QT4[:, hl, ti * 128 + 8 * s:ti * 128 + 8 * s + 8])
                S.cp('act', FMS[1], qsT)
                qs = pqx(hl)
                S.trf(qs, FMS[1], ident)
            qss[hl] = qs
        for hl, h in H:
            S.act(O1[hl], qss[hl], AF.Copy, scale=EGC[:, h:h + 1])
            S.tt('dve', OO[hl], avs[hl], O1[hl], ALU.add)
        for hl, h in H:
            S.act(JG[hl], OO[hl], AF.Square, accum_out=STAT[:, 16 + hl:17 + hl])
        for hl, h in H:
            rstd_from(STAT[:, 24 + hl:25 + hl], STAT[:, 16 + hl:17 + hl], 128)
        dts = {}
        pqb_align()
        for hl, h in H:
            S.stt(DD[hl], OO[hl], STAT[:, 24 + hl:25 + hl], ZS4[:, ti, hl * 128:(hl + 1) * 128], ALU.mult, ALU.mult)
            d_ = pqb()
            S.tr(d_, DD[hl], IDB[:])
            dts[hl] = d_
        for hl, h in H:
            S.cp('act', DOT[:, h, tsl], dts[hl])
        if not isS:
            dss = {}
            pq_align()
            for hl, h in H:
                ds = pqx(hl)
                S.mm(ds, KDEC[hl], VN[hl])
                dss[hl] = ds
            for hl, h in H:
                S.stt(S_ST[:, l, h, :], S_ST[:, l, h, :], EGL[:, h:h + 1], dss[hl], ALU.mult, ALU.add)
        else:
            for hl, h in H:
                S.tt('pool', VBLK, VN[hl].unsqueeze(1).broadcast_to([128, 16, 128]),
                     blk16.to_broadcast([128, 16, 128]), ALU.mult)
                S.tt('pool', S0F, S0F, EGLS[:, h, :].to_broadcast([128, 16, 128]), ALU.mult)
                for q4 in range(4):
                    bk = pbank()
                    S.mm(bk[:, :], KDEC[hl], VBLK[:, 4 * q4:4 * q4 + 4, :].rearrange("p a b -> p (a b)"))
                    sl = S0F[:, 4 * q4:4 * q4 + 4, :].rearrange("p a b -> p (a b)")
                    S.tt('dve', sl, bk[:, :], sl, ALU.add)
                S.dma('sp', nss[l, :, h].rearrange("s k v -> k s v"), S0F)

    def stage_merge(g, l):
        for half in range(2):
            Wbp = wget(widx[(g, l, 'wbp', half)], min(PREF, 4))
            Wbd = wget(widx[(g, l, 'wbd', half)], min(PREF, 3))
            Wgp = wget(widx[(g, l, 'gp', half)], min(PREF, 2))
            Wgd = wget(widx[(g, l, 'gd', half)], 1)
            for dl in range(4):
                dcn = half * 4 + dl
                for c in range(NCH):
                    i0 = (dl * NCH + c) % 2
                    gp = fm_proj(Wgp, dl * 128, HT, 8, c)
                    gd = fm_proj(Wgd, dl * 128, HT, 8, c)
                    brp = fm_proj(Wbp, dl * 128, AOT, 4, c)
                    brd = fm_proj(Wbd, dl * 128, DOT, 8, c)
                    S.act(SGT[2 * i0], gp, AF.Sigmoid)
                    S.act(SGT[2 * i0 + 1], gd, AF.Sigmoid)
                    S.tt('dve', T12[2 * i0], brp, SGT[2 * i0], ALU.mult)
                    S.tt('dve', T12[2 * i0 + 1], brd, SGT[2 * i0 + 1], ALU.mult)
                    S.tt('pool', MT[:, dcn, c * CH:(c + 1) * CH], T12[2 * i0], T12[2 * i0 + 1], ALU.add)

    def post_tile(l, which, ti):
        ssq = STAT[:, 48 + ti:49 + ti]
        rs = STAT[:, 56 + ti:57 + ti]
        S.act(JUNK, FF[:, ti, :], AF.Square, accum_out=ssq)
        rstd_from(rs, ssq, D)
        for half in range(2):
            cs = slice(half * 512, (half + 1) * 512)
            S.stt(TMPX[half], FF[:, ti, cs], rs, GPOSTb[:, cs], ALU.mult, ALU.mult)
            S.tt('pool', X[:, ti, cs], X[:, ti, cs], TMPX[half], ALU.add)

    def stage_out(g, l):
        S.dma('sp', GPOSTb, gpost_d[:, l, 0, :])
        for half in range(2):
            Wt = wget(widx[(g, l, 'wo', half)])
            for ti in range(TPG):
                acc = tm_proj(Wt, ti, MT)
                S.cp('act', FF[:, ti, half * 512:(half + 1) * 512], acc)
                if half == 1:
                    post_tile(l, 0, ti)

    def stage_ffn(g, l):
        norm_to_HT(l, 1)
        S.dma('sp', GPOSTb, gpost_d[:, l, 1, :])
        for fh in range(2):
            for f4 in range(4):
                Wt = wget(widx[(g, l, 'up', fh * 4 + f4)])
                for fl in range(4):
                    fi = f4 * 4 + fl
                    for c in range(NCH):
                        acc = fm_proj(Wt, fl * 128, HT, 8, c)
                        rl = RL[(fl * NCH + c) % 2]
                        S.act(rl, acc, AF.Relu)
                        S.tt('pool', UT[:, fi, c * CH:(c + 1) * CH], rl, rl, ALU.mult)
            for half in range(2):
                for kl in range(2):
                    Wt = wget(widx[(g, l, 'dn', half, fh * 2 + kl)])
                    for ti in range(TPG):
                        for kc in range(8):
                            S.mm(PB[ti][:, :], UT[:, kl * 8 + kc, ti * 128:(ti + 1) * 128], Wt[:, kc, :],
                                 start=(kl == 0 and kc == 0), stop=(kl == 1 and kc == 7))
                for ti in range(TPG):
                    dst = FF[:, ti, half * 512:(half + 1) * 512]
                    if fh == 0:
                        S.cp('act', dst, PB[ti][:, :])
                    else:
                        S.tt('dve', dst, PB[ti][:, :], dst, ALU.add)
                        if half == 1:
                            post_tile(l, 1, ti)

    dbg = DEBUG
    def on(name):
        return dbg is None or name in dbg['stages']
    if dbg is None:
        wissue(PREF)
    for g in range(len(GROUPS)):
        if dbg is not None and g not in dbg['groups']:
            continue
        tiles, npt, NW, has_s = group_info(g)
        load_x(g)
        for l in range(NL):
            if dbg is not None and l not in dbg['layers']:
                continue
            if on('norm'):
                norm_to_HT(l, 0)
            if on('pool'):
                stage_pool(g, l)
            if on('ba'):
                stage_ba(g, l)
            if has_s and on('qkv'):
                stage_sconv_prep(l)
            for hb in range(2):
                if on('qkv'):
                    stage_qkv(g, l, hb)
                if on('z'):
                    stage_z(g, l, hb)
                if on('gdn'):
                    stage_gdn(g, l, hb)
            if on('merge'):
                stage_merge(g, l)
            if on('out'):
                stage_out(g, l)
            if on('ffn'):
                stage_ffn(g, l)
        store_x(g)
    if dbg is None:
        assert wuse[0] == len(wlist)
    S.emit()
    return nc, S


_CACHE = {}


def kernel(x_prompt, x_sample, state_conv, state_ssm, state_pool, meta_tokens, g_pre_mix, w_in, conv_w, a_log,
           dt_bias, o_norm_g, pool_w, pool_scale, w_branch_pool, w_branch_delta, w_out, g_post_mix, g_pre_ffn,
           w_up, w_down, g_post_ffn):
    f = lambda a: np.ascontiguousarray(np.asarray(a, dtype=np.float32))
    if 'nc' not in _CACHE:
        _CACHE['nc'] = build_program()[0]
    nc = _CACHE['nc']
    x_prompt, x_sample, state_conv, state_ssm, state_pool = map(f, (x_prompt, x_sample, state_conv, state_ssm, state_pool))
    gpre = np.stack([f(g_pre_mix), f(g_pre_ffn)], axis=1)
    gpre = f(gpre.reshape(NL, 2, 8, 128).transpose(3, 0, 1, 2))
    cw = f(f(conv_w).reshape(NL, 4, 24, 128).transpose(3, 0, 2, 1))
    psc = f(f(pool_scale).reshape(NL, 4, 128).transpose(2, 0, 1))
    gpost = np.stack([f(g_post_mix), f(g_post_ffn)], axis=1)
    gpost = f(np.broadcast_to(gpost[None], (128, NL, 2, D)))
    ong = f(np.broadcast_to(f(o_norm_g)[None], (128, NL, 128)))
    alog = f(np.broadcast_to(f(a_log)[None], (128, NL, 8)))
    dtb = f(np.broadcast_to(f(dt_bias)[None], (128, NL, 8)))
    shared = {
        "meta": f(meta_tokens), "w_in": f(w_in), "pool_w": f(pool_w), "wbp": f(w_branch_pool),
        "wbd": f(w_branch_delta), "wo": f(w_out), "wup": f(w_up), "wdn": f(w_down),
        "consts": make_consts(), "gpre": gpre, "cw": cw, "pscale": psc, "gpost": gpost, "ong": ong,
        "alog": alog, "dtb": dtb,
    }
    in_maps = []
    for c in range(8):
        m = dict(shared)
        m["xp"] = x_prompt[c]
        m["xs"] = f(x_sample[16 * c:16 * c + 16].reshape(128, D))
        m["sconv"] = f(state_conv[:, 16 * c:16 * c + 16])
        m["sssm"] = f(state_ssm[:, 16 * c:16 * c + 16])
        m["spool"] = f(state_pool[:, 16 * c:16 * c + 16])
        m["spoolT"] = f(m["spool"].reshape(NL, 16, 15, 4, 128).transpose(0, 4, 3, 1, 2))
        m["sconvT"] = f(m["sconv"].reshape(NL, 16, 3, 24, 128).transpose(0, 4, 3, 1, 2))
        in_maps.append(m)
    res = run_bass_kernel_spmd(nc, in_maps, core_ids=list(range(8)))
    R = res.results
    y_prompt = np.stack([R[c]["y_p"] for c in range(8)], axis=0)
    y_sample = np.concatenate([R[c]["y_s"].reshape(16, 8, D) for c in range(8)], axis=0)
    ncp = np.stack([R[c]["ncp"] for c in range(8)], axis=1)
    nsp = np.stack([R[c]["nsp"] for c in range(8)], axis=1)
    npp = np.stack([R[c]["npp"] for c in range(8)], axis=1)
    ncs = np.concatenate([R[c]["ncs"] for c in range(8)], axis=1)
    nss = np.concatenate([R[c]["nss"] for c in range(8)], axis=1)
    nps = np.concatenate([R[c]["nps"] for c in range(8)], axis=1)
    return tuple(np.ascontiguousarray(a, dtype=np.float32) for a in (y_prompt, y_sample, ncp, nsp, npp, ncs, nss, nps))
```

```python
import numpy as np
import concourse.bass as bass
import concourse.mybir as mybir
from concourse.bass_utils import run_bass_kernel_spmd

F32 = mybir.dt.float32
BF16 = mybir.dt.bfloat16
AF = mybir.ActivationFunctionType
ALU = mybir.AluOpType

ENGS = ('pe', 'act', 'dve', 'pool', 'sp')
MAXOPS = None


class _Op:
    __slots__ = ('eng', 'fn', 'waits', 'signal', 'dma', 'idx')

    def __init__(self, eng, fn, dma=None):
        self.eng = eng
        self.fn = fn
        self.waits = []
        self.signal = False
        self.dma = dma
        self.idx = 0


class Sched:
    NDMA = 24

    def __init__(self, nc):
        self.nc = nc
        self.ops = {e: [] for e in ENGS}
        self.recs = {}
        self.water = {e: {} for e in ENGS}
        self.dma_tot = [0] * self.NDMA
        self.dma_last = [None] * self.NDMA
        self.dma_rr = 0
        self.dma_rr_sw = 0
        self.tensors = []
        self.same_eng_dist = 1 << 30

    def sbuf(self, name, shape, dtype):
        t = self.nc.alloc_sbuf_tensor(name, list(shape), dtype)
        return t

    def psum(self, name, shape, dtype):
        return self.nc.alloc_psum_tensor(name, list(shape), dtype)

    @staticmethod
    def _box(ap):
        t = ap.tensor
        shp = list(t.shape)
        row = 1
        for s in shp[1:]:
            row *= s
        dsz = mybir.dt.size(ap.dtype)
        pat = list(ap.ap)
        off = ap.offset
        p0 = off // row
        f0 = (off % row) * dsz
        pstep, pcnt = pat[0]
        if pstep == 0:
            np_ = 1
        else:
            np_ = (pcnt - 1) * (pstep // row) + 1
        ext = 0
        for st, cnt in pat[1:]:
            ext += (cnt - 1) * abs(st)
        if type(t).__name__ == 'PSumTensorHandle':
            return (p0, p0 + np_, 0, 1 << 20)
        return (p0, p0 + np_, f0, f0 + (ext + 1) * dsz)

    def _track(self, op, aps_r, aps_w):
        deps = set()
        for kind, aps in (('r', aps_r), ('w', aps_w)):
            for ap in aps:
                name = ap.tensor.name
                box = self._box(ap)
                if type(ap.tensor).__name__ == 'PSumTensorHandle':
                    kind = 'w'
                recs = self.recs.setdefault(name, {})
                dead = []
                for (b, k, key), tok in recs.items():
                    if b[0] < box[1] and box[0] < b[1] and b[2] < box[3] and box[2] < b[3]:
                        if kind == 'w' or k == 'w':
                            deps.add(tok)
                        if kind == 'w' and box[0] <= b[0] and b[1] <= box[1] and box[2] <= b[2] and b[3] <= box[3]:
                            dead.append((b, k, key))
                for d in dead:
                    del recs[d]
        return deps

    def _record(self, op, tok, aps_r, aps_w):
        for kind, aps in (('r', aps_r), ('w', aps_w)):
            for ap in aps:
                name = ap.tensor.name
                box = self._box(ap)
                if type(ap.tensor).__name__ == 'PSumTensorHandle':
                    kind = 'w'
                key = tok[0] if tok[0] != 'dma' else ('dma', tok[1])
                self.recs.setdefault(name, {})[(box, kind, key)] = tok

    def _add_waits(self, op, deps):
        eng = op.eng
        my_idx = len(self.ops[eng])
        for tok in deps:
            if tok[0] == 'dma':
                _, s, val = tok
                key = ('dma', s)
                if self.water[eng].get(key, 0) >= val:
                    continue
                self.water[eng][key] = val
                op.waits.append(tok)
            else:
                f, i = tok
                if f == eng:
                    if eng in ('pe', 'sp'):
                        continue
                    if my_idx - i > self.same_eng_dist:
                        continue
                if self.water[eng].get(f, -1) >= i:
                    continue
                self.water[eng][f] = i
                self.ops[f][i].signal = True
                op.waits.append(tok)

    @staticmethod
    def _is_onchip(ap):
        return type(ap.tensor).__name__ in ('SBTensorHandle', 'PSumTensorHandle')

    def op(self, eng, fn, r, w):
        self.nrec = getattr(self, 'nrec', 0) + 1
        if MAXOPS is not None and self.nrec > MAXOPS:
            return None
        o = _Op(eng, fn)
        r = [a for a in r if self._is_onchip(a)]
        w = [a for a in w if self._is_onchip(a)]
        deps = self._track(o, r, w)
        self._add_waits(o, deps)
        o.idx = len(self.ops[eng])
        self.ops[eng].append(o)
        self._record(o, (eng, o.idx), r, w)
        return o

    def dma(self, queue, out, in_):
        self.nrec = getattr(self, 'nrec', 0) + 1
        if MAXOPS is not None and self.nrec > MAXOPS:
            return None
        if queue == 'pool':
            s = 16 + self.dma_rr_sw
            self.dma_rr_sw = (self.dma_rr_sw + 1) % 8
        else:
            s = self.dma_rr
            self.dma_rr = (self.dma_rr + 1) % 16
        o = _Op(queue, None, dma=(s, out, in_))
        r = [in_] if self._is_onchip(in_) else []
        w = [out] if self._is_onchip(out) else []
        deps = self._track(o, r, w)
        if self.dma_last[s] is not None:
            deps.add(self.dma_last[s])
        self._add_waits(o, deps)
        self.dma_tot[s] += 16
        tok = ('dma', s, self.dma_tot[s])
        self.dma_last[s] = tok
        o.idx = len(self.ops[queue])
        self.ops[queue].append(o)
        self._record(o, tok, r, w)
        return tok

    def mm(self, out, lhsT, rhs, start=True, stop=True):
        return self.op('pe', lambda e: e.matmul(out, lhsT, rhs, start=start, stop=stop), [lhsT, rhs], [out])

    def tr(self, out, in_, ident):
        return self.op('pe', lambda e: e.transpose(out, in_, ident), [in_, ident], [out])

    def trf(self, out, in_, ident):
        return self.op('pe', lambda e: e.matmul(out, in_, ident, start=True, stop=True), [in_, ident], [out])

    def act(self, out, in_, func, bias=None, scale=None, accum_out=None, eng='act'):
        kw = {}
        r = [in_]
        w = [out]
        if bias is not None:
            kw['bias'] = bias
            if not isinstance(bias, (int, float)):
                r.append(bias)
        if scale is not None:
            kw['scale'] = scale
            if not isinstance(scale, (int, float)):
                r.append(scale)
        if accum_out is not None:
            kw['accum_out'] = accum_out
            w.append(accum_out)
        return self.op('act', lambda e: e.activation(out, in_, func, **kw), r, w)

    def tt(self, eng, out, in0, in1, op):
        return self.op(eng, lambda e: e.tensor_tensor(out, in0, in1, op), [in0, in1], [out])

    def ts(self, eng, out, in0, s1, s2, op0, op1=None, accum_out=None):
        r = [in0]
        if not isinstance(s1, (int, float)) and s1 is not None:
            r.append(s1)
        if not isinstance(s2, (int, float)) and s2 is not None:
            r.append(s2)
        w = [out]
        kw = {}
        if op1 is not None:
            kw['op1'] = op1
        if accum_out is not None:
            kw['accum_out'] = accum_out
            w.append(accum_out)
        return self.op(eng, lambda e: e.tensor_scalar(out, in0, s1, s2, op0, **kw), r, w)

    def stt(self, out, in0, scalar, in1, op0, op1, eng='dve'):
        r = [in0, in1]
        if not isinstance(scalar, (int, float)):
            r.append(scalar)
        return self.op(eng, lambda e: e.scalar_tensor_tensor(out, in0, scalar, in1, op0, op1), r, [out])

    def cp(self, eng, out, in_):
        if eng == 'act':
            return self.op('act', lambda e: e.copy(out, in_), [in_], [out])
        return self.op(eng, lambda e: e.tensor_copy(out, in_), [in_], [out])

    def memset(self, eng, ap, val):
        return self.op(eng, lambda e: e.memset(ap, val), [], [ap])

    def emit(self, final_wait=True):
        nc = self.nc
        engobj = {'pe': nc.tensor, 'act': nc.scalar, 'dve': nc.vector, 'pool': nc.gpsimd, 'sp': nc.sync}
        sems = {e: nc.alloc_semaphore("prog_" + e) for e in ENGS}
        dsems = [nc.alloc_semaphore("dma_%d" % i) for i in range(self.NDMA)]
        cnt = {}
        for e in ENGS:
            c = 0
            arr = []
            for o in self.ops[e]:
                if o.signal and o.dma is None:
                    c += 1
                arr.append(c)
            cnt[e] = arr
        blockattr = {'pe': 'tensor', 'act': 'scalar', 'dve': 'vector', 'pool': 'gpsimd', 'sp': 'sync'}
        with nc.Block() as block:
            for e in ENGS:
                ops = self.ops[e]

                def body(eng, e=e, ops=ops):
                    for o in ops:
                        for tok in o.waits:
                            if tok[0] == 'dma':
                                eng.wait_ge(dsems[tok[1]], tok[2])
                            else:
                                eng.wait_ge(sems[tok[0]], cnt[tok[0]][tok[1]])
                        if o.dma is not None:
                            s, out, in_ = o.dma
                            eng.dma_start(out=out, in_=in_).then_inc(dsems[s], 16)
                        else:
                            ins = o.fn(eng)
                            if o.signal:
                                ins.then_inc(sems[e], 1)
                    if e == 'sp' and final_wait:
                        for s in range(self.NDMA):
                            if self.dma_tot[s] > 0:
                                eng.wait_ge(dsems[s], self.dma_tot[s])
                getattr(block, blockattr[e])(body)


D = 1024
NH = 8
IN_W = 6672
DFF = 4096
NL = 2
TPG = 6
NT = TPG * 128
CH = 384
NCH = NT // CH
EPS = 1e-6
POOL_W = (2, 4, 8, 16)
C_ID, C_UP, C_US, C_SLP, C_SLS, C_SAMES, C_ONES, C_NEGP, C_NEGS, C_BLK, C_INVC = (
    0, 128, 256, 384, 512, 640, 768, 896, 1024, 1152, 1168)
NCONST = 1232
GROUPS = [[('P', i) for i in range(0, 6)], [('P', i) for i in range(6, 12)],
          [('P', i) for i in range(12, 17)] + [('S', 0)]]
CHAIN_F32 = True
DEBUG = None


def make_consts():
    c = np.zeros((128, NCONST), np.float32)
    p = np.arange(128)
    same = (p[:, None] // 8) == (p[None, :] // 8)
    c[:, C_ID:C_ID + 128] = np.eye(128)
    c[:, C_UP:C_UP + 128] = (p[:, None] <= p[None, :])
    c[:, C_US:C_US + 128] = (p[:, None] <= p[None, :]) & same
    c[:, C_SLP:C_SLP + 128] = (p[:, None] > p[None, :])
    c[:, C_SLS:C_SLS + 128] = (p[:, None] > p[None, :]) & same
    c[:, C_SAMES:C_SAMES + 128] = same
    c[:, C_ONES:C_ONES + 128] = 1.0
    c[:, C_NEGP:C_NEGP + 128] = np.where(p[:, None] < p[None, :], 0.0, -30000.0)
    c[:, C_NEGS:C_NEGS + 128] = np.where((p[:, None] < p[None, :]) & same, 0.0, -30000.0)
    c[:, C_BLK:C_BLK + 16] = (p[:, None] // 8) == np.arange(16)[None, :]
    for gi, w in enumerate(POOL_W):
        for pos in range(16):
            c[:, C_INVC + gi * 16 + pos] = 1.0 / min(pos + 1, w)
    return c


def build_program():
    nc = bass.Bass("TRN2", target_bir_lowering=False)

    def din(name, shape):
        return nc.dram_tensor(name, list(shape), F32, kind="ExternalInput").ap()

    def dout(name, shape):
        return nc.dram_tensor(name, list(shape), F32, kind="ExternalOutput").ap()

    xp = din("xp", [2048, D]); xs = din("xs", [128, D]); meta = din("meta", [16, D])
    sconv = din("sconv", [NL, 16, 3, 3072]); sssm = din("sssm", [NL, 16, NH, 128, 128])
    spool = din("spool", [NL, 16, 15, 512])
    spoolT = din("spoolT", [NL, 128, 4, 16, 15]); sconvT = din("sconvT", [NL, 128, 24, 16, 3])
    w_in = din("w_in", [NL, D, IN_W]); pool_w = din("pool_w", [NL, 4, 128, 128])
    wbp = din("wbp", [NL, 512, D]); wbd = din("wbd", [NL, D, D]); wo = din("wo", [NL, D, D])
    wup = din("wup", [NL, D, DFF]); wdn = din("wdn", [NL, DFF, D])
    consts = din("consts", [128, NCONST]); gpre_d = din("gpre", [128, NL, 2, 8])
    cw_d = din("cw", [128, NL, 24, 4]); pscale_d = din("pscale", [128, NL, 4])
    gpost_d = din("gpost", [128, NL, 2, D]); ong_d = din("ong", [128, NL, 128])
    alog_d = din("alog", [128, NL, 8]); dtb_d = din("dtb", [128, NL, 8])
    y_p = dout("y_p", [2048, D]); y_s = dout("y_s", [128, D])
    ncp = dout("ncp", [NL, 3, 3072]); nsp = dout("nsp", [NL, NH, 128, 128]); npp = dout("npp", [NL, 15, 512])
    ncs = dout("ncs", [NL, 16, 3, 3072]); nss = dout("nss", [NL, 16, NH, 128, 128]); nps = dout("nps", [NL, 16, 15, 512])

    S = Sched(nc)
    CD = F32 if CHAIN_F32 else BF16

    X = S.sbuf("X", [128, TPG, D], F32)
    HT = S.sbuf("HT", [128, 8, NT], BF16)
    BIG = S.sbuf("BIG", [128, 16 * NT], BF16)
    UT = BIG[:, :].rearrange("p (a b) -> p a b", b=NT)
    QT4 = BIG[:, 0:4 * NT].rearrange("p (a b) -> p a b", b=NT)
    KT4 = BIG[:, 4 * NT:8 * NT].rearrange("p (a b) -> p a b", b=NT)
    VT4 = BIG[:, 8 * NT:12 * NT].rearrange("p (a b) -> p a b", b=NT)
    ZS4 = BIG[:, 12 * NT:16 * NT].rearrange("p (t c) -> p t c", c=512)
    MT = BIG[:, 0:8 * NT].rearrange("p (a b) -> p a b", b=NT)
    DM = S.sbuf("DM", [128, 16 * NT], BF16)
    DOT = DM[:, 0:8 * NT].rearrange("p (a b) -> p a b", b=NT)
    AOT = DM[:, 8 * NT:12 * NT].rearrange("p (a b) -> p a b", b=NT)
    FF = DM[:, :].bitcast(F32).rearrange("p (t c) -> p t c", c=D)
    NSLOT = 5
    PREF = 4
    WR = [S.sbuf("WR%d" % i, [128, 8, 512], BF16) for i in range(NSLOT)]
    CST = S.sbuf("CST", [128, NCONST], F32)
    IDB = S.sbuf("IDB", [128, 128], BF16)
    GPRE = S.sbuf("GPRE", [128, NL, 2, 8], F32)
    CW = S.sbuf("CW", [128, NL, 24, 4], F32)
    PSC = S.sbuf("PSC", [128, NL, 4], F32)
    ONG = S.sbuf("ONG", [128, NL, 128], F32)
    NEGA = S.sbuf("NEGA", [128, NL, 8], F32)
    DTB = S.sbuf("DTB", [128, NL, 8], F32)
    EPSC = S.sbuf("EPSC", [128, 2], F32)
    S_ST = S.sbuf("S_ST", [128, NL, NH, 128], F32)
    CPRE = S.sbuf("CPRE", [128, NL, 24, 3], F32)
    PPRE = S.sbuf("PPRE", [128, NL, 4, 15], F32)
    PWL = S.sbuf("PWL", [128, 4, 128], BF16)
    WBA = S.sbuf("WBA", [128, 8, 16], BF16)
    STAT = S.sbuf("STAT", [128, 64], F32)
    BETA = S.sbuf("BETA", [128, TPG, 8], F32)
    NBETA = S.sbuf("NBETA", [128, TPG, 8], F32)
    GG = S.sbuf("GG", [128, TPG, 8], F32)
    TA = S.sbuf("TA", [128, 8], F32)
    BAR = S.sbuf("BAR", [128, TPG, 16], F32)
    TAB = S.sbuf("TAB", [128, TPG, 8], F32)
    EGC = S.sbuf("EGC", [128, 8], F32)
    EGD = S.sbuf("EGD", [128, 8], F32)
    EGL = S.sbuf("EGL", [128, 8], F32)
    GCs = S.sbuf("GCs", [128, 8], F32)
    GBS = S.sbuf("GBS", [128, 8, 16], F32)
    EGLS = S.sbuf("EGLS", [128, 8, 16], F32)
    SPOOLT = S.sbuf("SPOOLT", [128, 4, 16, 15], F32)
    SCONVT = S.sbuf("SCONVT", [128, 24, 16, 3], F32)

    SCR_BYTES = 52 * 1024
    SCR = S.sbuf("SCR", [128, SCR_BYTES // 2], BF16)

    class Arena:
        def __init__(self):
            self.off = 0

        def take(self, free, dtype):
            n = 1
            for f in free:
                n *= f
            nb = n * mybir.dt.size(dtype)
            nb_al = (nb + 31) // 32 * 32
            assert self.off + nb_al <= SCR_BYTES, ("arena overflow", self.off, nb_al)
            ap = SCR[:, self.off // 2:(self.off + nb) // 2]
            self.off += nb_al
            if dtype != BF16:
                ap = ap.bitcast(dtype)
            if len(free) == 2:
                ap = ap.rearrange("p (a b) -> p a b", b=free[1])
            elif len(free) == 3:
                ap = ap.rearrange("p (a b c) -> p a b c", b=free[1], c=free[2])
            return ap

    EXTW = 16 + NT
    A = Arena()
    JUNK = A.take([D], BF16)
    HB = [A.take([D], BF16) for _ in range(2)]
    offA0 = A.off
    UEXT = [A.take([EXTW], F32) for _ in range(2)]
    SCEXT = A.take([16, 11], F32)
    TMROW = A.take([512], F32)
    offA1 = A.off
    PA = A.take([EXTW], F32)
    PBf = A.take([EXTW], F32)
    MTP = A.take([NT], BF16)
    SEXT = A.take([16, 23], F32)
    SPA = A.take([16, 23], F32)
    SPB = A.take([16, 23], F32)
    A.off = offA1
    CQ2 = [A.take([NT], F32) for _ in range(2)]
    SQ2_2 = [A.take([NT], F32) for _ in range(2)]
    SQ_4 = [A.take([NT], F32) for _ in range(4)]
    RN_4 = [A.take([NT], F32) for _ in range(4)]
    A.off = offA0
    GPOSTb = A.take([D], F32)
    TMPX = [A.take([512], F32) for _ in range(2)]
    SGT = [A.take([CH], F32) for _ in range(4)]
    T12 = [A.take([CH], F32) for _ in range(4)]
    RL = [A.take([CH], F32) for _ in range(2)]
    G = Arena()
    NB = 4
    Bm = [G.take([128], F32) for _ in range(NB)]
    DTS = [G.take([128], F32) for _ in range(NB)]
    DTM = [G.take([128], F32) for _ in range(NB)]
    CA = [[G.take([128], CD) for _ in range(2)] for _ in range(NB)]
    CAT = [[G.take([128], CD) for _ in range(2)] for _ in range(NB)]
    CR = [[G.take([128], CD) for _ in range(2)] for _ in range(NB)]
    KE = [G.take([128], CD) for _ in range(NB)]
    KDEC = [G.take([128], BF16) for _ in range(NB)]
    VTM = [G.take([128], CD) for _ in range(NB)]
    WTt = [G.take([128], BF16) for _ in range(NB)]
    UB = [G.take([128], F32) for _ in range(NB)]
    VN = [G.take([128], BF16) for _ in range(NB)]
    ATt = [G.take([128], BF16) for _ in range(NB)]
    O1 = [G.take([128], F32) for _ in range(NB)]
    OO = [G.take([128], F32) for _ in range(NB)]
    DD = [G.take([128], BF16) for _ in range(NB)]
    SBF = [G.take([128], BF16) for _ in range(NB)]
    JG = [G.take([128], BF16) for _ in range(NB)]
    FMS = [G.take([128], F32) for _ in range(2)]
    S0F = G.take([16, 128], F32)
    S0B = G.take([16, 128], BF16)
    VBLK = G.take([16, 128], BF16)

    PB = [S.psum("PB%d" % i, [128, 512], F32) for i in range(6)]
    PH = [S.psum("PH%d" % i, [128, 1024], BF16) for i in range(2)]
    qctr = [0]

    def pq_align():
        qctr[0] = (qctr[0] + 3) // 4 * 4 % 8

    def pq():
        i = qctr[0]
        qctr[0] = (i + 1) % 8
        return PB[4 + i // 4][:, (i % 4) * 128:(i % 4 + 1) * 128]
    bctr = [0]

    def pbank():
        i = bctr[0]
        bctr[0] = (i + 1) % 6
        return PB[i]
    hctr = [0]
    hq = [0, 0, 0, 0]

    def pqb_align():
        hctr[0] = (hctr[0] + 3) // 4 * 4 % 16

    def pqb():
        i = hctr[0]
        hctr[0] = (i + 1) % 16
        slot = (i % 4) + 4 * ((i // 8) % 2)
        return PH[(i // 4) % 2][:, slot * 128:(slot + 1) * 128]

    ident = CST[:, C_ID:C_ID + 128]
    ident_cd = ident if CHAIN_F32 else IDB[:, :]
    ones_f = CST[:, C_ONES:C_ONES + 128]
    blk16 = CST[:, C_BLK:C_BLK + 16]
    eps_c = EPSC[:, 0:1]
    one_c = EPSC[:, 1:2]

    wlist = []

    def wsrc_k8(wt, l, c0, ncols=512):
        return wt[l, :, c0:c0 + ncols].rearrange("(dc p) c -> p dc c", p=128)

    wstate = {'issued': 0}

    def wissue(upto):
        while wstate['issued'] <= upto and wstate['issued'] < len(wlist):
            n = wstate['issued']
            src, kc, ncol = wlist[n]
            S.dma('pool', WR[n % NSLOT][:, 0:kc, 0:ncol], src)
            wstate['issued'] += 1

    wuse = [0]

    def wget(n, pref=None):
        pref = PREF if pref is None else pref
        if DEBUG is None:
            assert n == wuse[0], ("weight order mismatch", n, wuse[0])
        wuse[0] += 1
        if DEBUG is None:
            wissue(n + pref)
        else:
            src, kc, ncol = wlist[n]
            S.dma('pool', WR[n % NSLOT][:, 0:kc, 0:ncol], src)
        return WR[n % NSLOT]

    widx = {}
    for g in range(len(GROUPS)):
        for l in range(NL):
            def add(key, src, kc=8, ncol=512):
                widx[(g, l) + key] = len(wlist)
                wlist.append((src, kc, ncol))
            add(('u',), wsrc_k8(w_in, l, 0))
            for hb in range(2):
                for comp in range(3):
                    add(('qkv', comp, hb), wsrc_k8(w_in, l, 512 + comp * 1024 + hb * 512))
                add(('z', hb), wsrc_k8(w_in, l, 3600 + hb * 512))
            for half in range(2):
                add(('wbp', half), wbp[l, :, half * 512:(half + 1) * 512].rearrange("(dc p) c -> p dc c", p=128), kc=4)
                add(('wbd', half), wsrc_k8(wbd, l, half * 512))
                add(('gp', half), wsrc_k8(w_in, l, 4624 + half * 512))
                add(('gd', half), wsrc_k8(w_in, l, 5648 + half * 512))
            for half in range(2):
                add(('wo', half), wsrc_k8(wo, l, half * 512))
            for fh in range(2):
                for f4 in range(4):
                    add(('up', fh * 4 + f4), wsrc_k8(wup, l, (fh * 4 + f4) * 512))
                for half in range(2):
                    for kl in range(2):
                        kcg = fh * 2 + kl
                        add(('dn', half, kcg),
                            wdn[l, kcg * 1024:(kcg + 1) * 1024, half * 512:(half + 1) * 512].rearrange("(dc p) c -> p dc c", p=128))

    S.dma('sp', CST[:], consts)
    S.dma('sp', GPRE[:], gpre_d)
    S.dma('sp', CW[:], cw_d)
    S.dma('sp', PSC[:], pscale_d)
    S.dma('sp', ONG[:], ong_d)
    S.dma('sp', NEGA[:], alog_d)
    S.dma('sp', DTB[:], dtb_d)
    S.cp('dve', IDB[:], ident)
    S.memset('dve', EPSC[:, 0:1], EPS)
    S.memset('dve', EPSC[:, 1:2], 1.0)
    S.act(NEGA[:], NEGA[:], AF.Exp)
    S.ts('pool', NEGA[:], NEGA[:], -1.0, None, ALU.mult)
    S.memset('pool', S_ST[:], 0.0)
    S.memset('pool', CPRE[:], 0.0)
    S.memset('pool', PPRE[:], 0.0)

    def rstd_from(out, ssq, n):
        S.act(out, ssq, AF.Ln, bias=eps_c, scale=1.0 / n)
        S.act(out, out, AF.Exp, scale=-0.5)

    def group_info(g):
        tiles = GROUPS[g]
        npt = sum(1 for t in tiles if t[0] == 'P')
        has_s = any(t[0] == 'S' for t in tiles)
        return tiles, npt, npt * 128, has_s

    def chunk_ranges(c, NW, has_s):
        a, b = c * CH, min((c + 1) * CH, NW)
        pr = (a, b) if b > a else None
        sr = has_s and (c + 1) * CH == NT
        return pr, sr

    def load_x(g):
        tiles, npt, NW, has_s = group_info(g)
        ti = 0
        if tiles[0] == ('P', 0):
            S.memset('pool', X[:, 0, :], 0.0)
            S.dma('sp', X[112:128, 0, :], meta)
            ti = 1
        if npt > ti:
            pt0 = tiles[ti][1]
            S.dma('sp', X[:, ti:npt, :], xp[(pt0 - 1) * 128:(pt0 - 1 + npt - ti) * 128, :].rearrange("(t p) c -> p t c", p=128))
        if has_s:
            S.dma('sp', X[:, TPG - 1, :], xs)

    def store_x(g):
        tiles, npt, NW, has_s = group_info(g)
        ti = 1 if tiles[0] == ('P', 0) else 0
        if npt > ti:
            pt0 = tiles[ti][1]
            S.dma('sp', y_p[(pt0 - 1) * 128:(pt0 - 1 + npt - ti) * 128, :].rearrange("(t p) c -> p t c", p=128), X[:, ti:npt, :])
        if has_s:
            S.dma('sp', y_s, X[:, TPG - 1, :])

    def norm_to_HT(l, which):
        for ti in range(TPG):
            ssq = STAT[:, ti:ti + 1]
            rs = STAT[:, 8 + ti:9 + ti]
            S.act(JUNK, X[:, ti, :], AF.Square, accum_out=ssq)
            rstd_from(rs, ssq, D)
            hb = HB[ti % 2]
            S.ts('dve', hb, X[:, ti, :], rs, None, ALU.mult)
            ph = PH[ti % 2]
            for dc in range(8):
                S.tr(ph[:, dc * 128:(dc + 1) * 128], hb[:, dc * 128:(dc + 1) * 128], IDB[:])
            S.tt('dve', HT[:, :, ti * 128:(ti + 1) * 128],
                 ph[:, :].rearrange("p (a b) -> p a b", b=128),
                 GPRE[:, l, which, :].to_broadcast([128, 8, 128]), ALU.mult)

    def fm_proj(wtile, c0, rhs_buf, nk, cidx):
        acc = pbank()[:, 0:CH]
        for kc in range(nk):
            S.mm(acc, wtile[:, kc, c0:c0 + 128], rhs_buf[:, kc, cidx * CH:(cidx + 1) * CH],
                 start=(kc == 0), stop=(kc == nk - 1))
        return acc

    def tm_proj(wtile, ti, src_buf, nk=8, ncol=512):
        acc = pbank()[:, 0:ncol]
        for kc in range(nk):
            S.mm(acc, src_buf[:, kc, ti * 128:(ti + 1) * 128], wtile[:, kc, 0:ncol],
                 start=(kc == 0), stop=(kc == nk - 1))
        return acc

    def stage_pool(g, l):
        tiles, npt, NW, has_s = group_info(g)
        W0 = wget(widx[(g, l, 'u')])
        S.dma('pool', PWL[:], pool_w[l].rearrange("g c d -> c g d"))
        if has_s:
            S.dma('sp', SPOOLT[:], spoolT[l])
            S.dma('sp', nps[l, :, 0:7, :], spool[l, :, 8:15, :])
            acc = tm_proj(W0, npt - 1, HT)
            S.cp('act', TMROW, acc)
            S.dma('sp', npp[l], TMROW[113:128, :])
            acc = tm_proj(W0, TPG - 1, HT)
            S.cp('act', TMROW, acc)
            for t in range(8):
                S.dma('sp', nps[l, :, 7 + t, :], TMROW[t::8, :])
        for gi in range(4):
            w = POOL_W[gi]
            ue = UEXT[gi % 2]
            S.cp('pool', ue[:, 1:16], PPRE[:, l, gi, :])
            for c in range(NCH):
                acc = fm_proj(W0, gi * 128, HT, 8, c)
                pr, sr = chunk_ranges(c, NW, has_s)
                if pr:
                    S.cp('act', ue[:, 16 + pr[0]:16 + pr[1]], acc[:, pr[0] - c * CH:pr[1] - c * CH])
                if sr:
                    S.cp('act', SEXT[:, :, 15:23], acc[:, CH - 128:CH].rearrange("p (s t) -> p s t", t=8))
            Wd = 16 + NW
            S.cp('pool', PPRE[:, l, gi, :], ue[:, Wd - 15:Wd])
            src = ue
            for j in range(gi + 1):
                k = 1 << j
                sh = (1 << (j + 1))
                dst = PA if j % 2 == 0 else PBf
                S.tt('pool', dst[:, sh:Wd], src[:, sh:Wd], src[:, sh - k:Wd - k], ALU.add)
                src = dst
            S.stt(MTP[:, 0:NW], src[:, 16:Wd], 1.0 / w, ue[:, 16:Wd], ALU.mult, ALU.subtract)
            if g == 0:
                tmp = STAT[:, 32:48]
                S.tt('dve', tmp, src[:, 16 + 112:16 + 128], CST[:, C_INVC + gi * 16:C_INVC + gi * 16 + 16], ALU.mult)
                S.tt('dve', MTP[:, 112:128], tmp, ue[:, 16 + 112:16 + 128], ALU.subtract)
            if has_s:
                S.cp('pool', SEXT[:, :, 0:15], SPOOLT[:, gi, :, :])
                ssrc = SEXT
                for j in range(gi + 1):
                    k = 1 << j
                    sh = (1 << (j + 1)) - 1
                    dst = SPA if j % 2 == 0 else SPB
                    S.tt('pool', dst[:, :, sh:23], ssrc[:, :, sh:23], ssrc[:, :, sh - k:23 - k], ALU.add)
                    ssrc = dst
                S.stt(MTP[:, NW:NT].rearrange("p (s t) -> p s t", t=8), ssrc[:, :, 15:23], 1.0 / w,
                      SEXT[:, :, 15:23], ALU.mult, ALU.subtract)
            for c in range(NCH):
                acc = pbank()[:, 0:CH]
                S.mm(acc, PWL[:, gi, :], MTP[:, c * CH:(c + 1) * CH])
                S.act(AOT[:, gi, c * CH:(c + 1) * CH], acc, AF.Copy, scale=PSC[:, l, gi:gi + 1])

    def stage_sconv_prep(l):
        S.dma('sp', SCONVT[:], sconvT[l])

    def stage_qkv(g, l, hb):
        tiles, npt, NW, has_s = group_info(g)
        for comp in range(3):
            Wt = wget(widx[(g, l, 'qkv', comp, hb)])
            if has_s:
                colbase = comp * 1024 + hb * 512
                acc = tm_proj(Wt, npt - 1, HT)
                S.cp('act', TMROW, acc)
                S.dma('sp', ncp[l, :, colbase:colbase + 512], TMROW[125:128, :])
                acc = tm_proj(Wt, TPG - 1, HT)
                S.cp('act', TMROW, acc)
                for t in range(3):
                    S.dma('sp', ncs[l, :, t, colbase:colbase + 512], TMROW[5 + t::8, :])
            def a1(hl):
                h = hb * 4 + hl
                ch = comp * 8 + h
                c0 = hl * 128
                ext = UEXT[ch % 2]
                S.cp('pool', ext[:, 0:3], CPRE[:, l, ch, :])
                for c in range(NCH):
                    acc = fm_proj(Wt, c0, HT, 8, c)
                    pr, sr = chunk_ranges(c, NW, has_s)
                    if pr:
                        S.cp('act', ext[:, 3 + pr[0]:3 + pr[1]], acc[:, pr[0] - c * CH:pr[1] - c * CH])
                    if sr:
                        S.cp('act', SCEXT[:, :, 3:11], acc[:, CH - 128:CH].rearrange("p (s t) -> p s t", t=8))
                S.cp('pool', CPRE[:, l, ch, :], ext[:, NW:NW + 3])
                if has_s:
                    CQ = CQ2[ch % 2]
                    S.cp('pool', SCEXT[:, :, 0:3], SCONVT[:, ch, :, :])
                    cqs = CQ[:, NW:NT].rearrange("p (s t) -> p s t", t=8)
                    S.ts('dve', cqs, SCEXT[:, :, 0:8], CW[:, l, ch, 0:1], None, ALU.mult)
                    for j in range(1, 4):
                        S.stt(cqs, SCEXT[:, :, j:j + 8], CW[:, l, ch, j:j + 1], cqs, ALU.mult, ALU.add)

            def a2(hl):
                h = hb * 4 + hl
                ch = comp * 8 + h
                ext = UEXT[ch % 2]
                CQ, SQ2, SQ, RN = CQ2[ch % 2], SQ2_2[ch % 2], SQ_4[hl], RN_4[hl]
                S.ts('dve', CQ[:, 0:NW], ext[:, 0:NW], CW[:, l, ch, 0:1], None, ALU.mult)
                for j in range(1, 4):
                    S.stt(CQ[:, 0:NW], ext[:, j:NW + j], CW[:, l, ch, j:j + 1], CQ[:, 0:NW], ALU.mult, ALU.add)
                if comp == 2:
                    S.act(VT4[:, hl, :], CQ, AF.Silu)
                else:
                    S.act(SQ, CQ, AF.Silu)
                    S.tt('pool', SQ2, SQ, SQ, ALU.mult)
                    for c in range(NCH):
                        acc = pbank()[:, 0:CH]
                        S.mm(acc, ones_f, SQ2[:, c * CH:(c + 1) * CH])
                        S.cp('act', RN[:, c * CH:(c + 1) * CH], acc)

            def phase_b(hl):
                SQ, RN = SQ_4[hl], RN_4[hl]
                S.act(RN, RN, AF.Ln, bias=eps_c, scale=1.0)
                S.act(RN, RN, AF.Exp, scale=-0.5)
                dst = QT4 if comp == 0 else KT4
                sc = (128.0 ** -0.5) if comp == 0 else 1.0
                S.stt(dst[:, hl, :], SQ, sc, RN, ALU.mult, ALU.mult)

            a1(0)
            for hl in range(4):
                if hl + 1 < 4:
                    a1(hl + 1)
                a2(hl)
            if comp != 2:
                for hl in range(4):
                    phase_b(hl)

    def stage_ba(g, l):
        S.dma('pool', WBA[:], w_in[l, :, 3584:3600].rearrange("(dc p) c -> p dc c", p=128))
        for ti in range(TPG):
            acc = pq()[:, 0:16]
            for kc in range(8):
                S.mm(acc, HT[:, kc, ti * 128:(ti + 1) * 128], WBA[:, kc, :], start=(kc == 0), stop=(kc == 7))
            S.cp('act', BAR[:, ti, :], acc)
        S.act(BETA[:, :, :], BAR[:, :, 0:8], AF.Sigmoid)
        S.tt('dve', TAB[:, :, :], BAR[:, :, 8:16], DTB[:, l, :].unsqueeze(1).broadcast_to([128, TPG, 8]), ALU.add)
        S.act(TAB[:, :, :], TAB[:, :, :], AF.Exp)
        S.act(TAB[:, :, :], TAB[:, :, :], AF.Ln, bias=one_c, scale=1.0)
        S.tt('dve', GG[:, :, :], TAB[:, :, :], NEGA[:, l, :].unsqueeze(1).broadcast_to([128, TPG, 8]), ALU.mult)
        S.ts('pool', NBETA[:], BETA[:], -1.0, None, ALU.mult)

    def stage_z(g, l, hb):
        Wt = wget(widx[(g, l, 'z', hb)])
        for ti in range(TPG):
            acc = tm_proj(Wt, ti, HT)
            S.act(ZS4[:, ti, :], acc, AF.Silu)
            zv = ZS4[:, ti, :].rearrange("p (h v) -> p h v", v=128)
            S.tt('pool', zv, zv, ONG[:, l, :].unsqueeze(1).broadcast_to([128, 4, 128]), ALU.mult)

    def stage_gdn(g, l, hb):
        tiles, npt, NW, has_s = group_info(g)
        for ti, (kind, pidx) in enumerate(tiles):
            isS = (kind == 'S')
            Umat = CST[:, C_US:C_US + 128] if isS else CST[:, C_UP:C_UP + 128]
            SLmat = CST[:, C_SLS:C_SLS + 128] if isS else CST[:, C_SLP:C_SLP + 128]
            SAMEmat = CST[:, C_SAMES:C_SAMES + 128] if isS else ones_f
            NEGmat = CST[:, C_NEGS:C_NEGS + 128] if isS else CST[:, C_NEGP:C_NEGP + 128]
            L = 3 if isS else 6
            tsl = slice(ti * 128, (ti + 1) * 128)
            gc_ps = pq()[:, 0:8]
            S.mm(gc_ps, Umat, GG[:, ti, :])
            gl_ps = pq()[:, 0:8]
            S.mm(gl_ps, SAMEmat, GG[:, ti, :])
            S.act(EGC[:], gc_ps, AF.Exp)
            S.cp('act', GCs[:], gc_ps)
            S.cp('act', EGD[:], gl_ps)
            S.act(EGL[:], gl_ps, AF.Exp)
            S.tt('dve', EGD[:], EGD[:], GCs[:], ALU.subtract)
            S.act(EGD[:], EGD[:], AF.Exp)
            if isS:
                S.tt('pool', GBS[:], GG[:, ti, :].to_broadcast([128, 8, 16]),
                     blk16.unsqueeze(1).broadcast_to([128, 8, 16]), ALU.mult)
                egp = pq()
                S.mm(egp, ones_f, GBS[:, :, :].rearrange("p a b -> p (a b)"))
                S.act(EGLS[:, :, :].rearrange("p a b -> p (a b)"), egp, AF.Exp)
            head_sets = [[0], [1], [2], [3]] if isS else [[0, 1, 2, 3]]
            for hls in head_sets:
                gdn_heads(g, l, hb, ti, isS, hls, Umat, SLmat, NEGmat, L, tsl)
        if g == len(GROUPS) - 1 and hb == 1:
            S.dma('sp', nsp[l].rearrange("h k v -> k h v"), S_ST[:, l, :, :])

    def gdn_heads(g, l, hb, ti, isS, hls, Umat, SLmat, NEGmat, L, tsl):
        H = [(hl, hb * 4 + hl) for hl in hls]
        lock4 = (len(hls) == 4)

        def pqx(hl):
            if not lock4:
                return pq()
            q = hq[hl]
            hq[hl] = (q + 1) % 4
            return PB[hl][:, q * 128:(q + 1) * 128]
        dps = {}
        for hl, h in H:
            S.ts('pool', Bm[hl], SLmat, GG[:, ti, h:h + 1], None, ALU.mult)
        pq_align()
        for hl, h in H:
            d = pqx(hl)
            S.mm(d, Bm[hl], Umat, start=True, stop=False)
            S.mm(d, ident, NEGmat, start=False, stop=True)
            dps[hl] = d
        for hl, h in H:
            S.act(DTS[hl], dps[hl], AF.Exp)
        for hl, h in H:
            S.tt('pool', DTM[hl], DTS[hl], ident, ALU.add)
        gks = {}
        pq_align()
        for hl, h in H:
            gk = pqx(hl)
            S.mm(gk, KT4[:, hl, tsl], KT4[:, hl, tsl])
            gks[hl] = gk
        for hl, h in H:
            S.stt(CA[hl][0], gks[hl], BETA[:, ti, h:h + 1], DTS[hl], ALU.mult, ALU.mult)
        pts = {}
        pq_align()
        pqb_align()
        for hl, h in H:
            p_ = pqx(hl) if CHAIN_F32 else pqb()
            if CHAIN_F32:
                S.trf(p_, CA[hl][0], ident_cd)
            else:
                S.tr(p_, CA[hl][0], ident_cd)
            pts[hl] = p_
        for hl, h in H:
            S.cp('act', CAT[hl][0], pts[hl])
            S.tt('pool', CR[hl][0], ident_cd, CA[hl][0], ALU.subtract)
        kts, vts = {}, {}
        pqb_align()
        for hl, h in H:
            kt_ = pqb()
            S.tr(kt_, KT4[:, hl, tsl], IDB[:])
            kts[hl] = kt_
        pqb_align()
        for hl, h in H:
            vt_ = pqb()
            S.tr(vt_, VT4[:, hl, tsl], IDB[:])
            vts[hl] = vt_
        for hl, h in H:
            S.ts('dve', KE[hl], kts[hl], EGC[:, h:h + 1], None, ALU.mult)
            S.ts('dve', KDEC[hl], kts[hl], EGD[:, h:h + 1], None, ALU.mult)
            S.cp('act', VTM[hl], vts[hl])
        for k in range(1, L + 1):
            a_ps, at_ps = {}, {}
            pq_align()
            if k < L:
                for hl, h in H:
                    a = pqx(hl)
                    S.mm(a, CAT[hl][(k - 1) % 2], CA[hl][(k - 1) % 2])
                    a_ps[hl] = a
                pq_align()
            for hl, h in H:
                at = pqx(hl)
                S.mm(at, CA[hl][(k - 1) % 2], CAT[hl][(k - 1) % 2])
                at_ps[hl] = at
            for hl, h in H:
                if k < L:
                    S.cp('act', CA[hl][k % 2], a_ps[hl])
                S.cp('dve', CAT[hl][k % 2], at_ps[hl])
            r_ps = {}
            pq_align()
            for hl, h in H:
                r = pqx(hl)
                S.mm(r, CAT[hl][k % 2], CR[hl][(k - 1) % 2])
                r_ps[hl] = r
            for hl, h in H:
                S.tt('dve', CR[hl][k % 2], r_ps[hl], CR[hl][(k - 1) % 2], ALU.add)
        XT = {hl: CR[hl][L % 2] for hl, h in H}
        wps, ups = {}, {}
        pq_align()
        for hl, h in H:
            w_ = pqx(hl)
            S.mm(w_, KE[hl], XT[hl])
            wps[hl] = w_
            u_ = pqx(hl)
            S.mm(u_, XT[hl], VTM[hl])
            ups[hl] = u_
        for hl, h in H:
            S.cp('act', WTt[hl], wps[hl])
            S.act(UB[hl], ups[hl], AF.Copy, scale=BETA[:, ti, h:h + 1])
        wss = {}
        if not isS:
            for hl, h in H:
                S.cp('pool', SBF[hl], S_ST[:, l, h, :])
            pq_align()
            for hl, h in H:
                ws = pqx(hl)
                S.mm(ws, WTt[hl], SBF[hl])
                wss[hl] = ws
        else:
            for hl, h in H:
                S.dma('sp', S0F, sssm[l, :, h].rearrange("s k v -> k s v"))
                S.dma('pool', S0B, sssm[l, :, h].rearrange("s k v -> k s v"))
                wsT = pqx(hl)
                for s in range(16):
                    S.mm(wsT[:, 8 * s:8 * s + 8], S0B[:, s, :], WTt[hl][:, 8 * s:8 * s + 8])
                S.cp('act', FMS[0], wsT)
                ws = pqx(hl)
                S.trf(ws, FMS[0], ident)
                wss[hl] = ws
        for hl, h in H:
            S.stt(VN[hl], wss[hl], NBETA[:, ti, h:h + 1], UB[hl], ALU.mult, ALU.add)
        aps = {}
        pq_align()
        for hl, h in H:
            a_ = pqx(hl)
            S.mm(a_, KT4[:, hl, tsl], QT4[:, hl, tsl])
            aps[hl] = a_
        for hl, h in H:
            S.tt('dve', ATt[hl], aps[hl], DTM[hl], ALU.mult)
        avs, qss = {}, {}
        pq_align()
        for hl, h in H:
            av = pqx(hl)
            S.mm(av, ATt[hl], VN[hl])
            avs[hl] = av
        pq_align()
        for hl, h in H:
            if not isS:
                qs = pqx(hl)
                S.mm(qs, QT4[:, hl, tsl], SBF[hl])
            else:
                qsT = pqx(hl)
                for s in range(16):
                    S.mm(qsT[:, 8 * s:8 * s + 8], S0B[:, s, :], QT4[:, hl, ti * 128 + 8 * s:ti * 128 + 8 * s + 8])
                S.cp('act', FMS[1], qsT)
                qs = pqx(hl)
                S.trf(qs, FMS[1], ident)
            qss[hl] = qs
        for hl, h in H:
            S.act(O1[hl], qss[hl], AF.Copy, scale=EGC[:, h:h + 1])
            S.tt('dve', OO[hl], avs[hl], O1[hl], ALU.add)
        for hl, h in H:
            S.act(JG[hl], OO[hl], AF.Square, accum_out=STAT[:, 16 + hl:17 + hl])
        for hl, h in H:
            rstd_from(STAT[:, 24 + hl:25 + hl], STAT[:, 16 + hl:17 + hl], 128)
        dts = {}
        pqb_align()
        for hl, h in H:
            S.stt(DD[hl], OO[hl], STAT[:, 24 + hl:25 + hl], ZS4[:, ti, hl * 128:(hl + 1) * 128], ALU.mult, ALU.mult)
            d_ = pqb()
            S.tr(d_, DD[hl], IDB[:])
            dts[hl] = d_
        for hl, h in H:
            S.cp('act', DOT[:, h, tsl], dts[hl])
        if not isS:
            dss = {}
            pq_align()
            for hl, h in H:
                ds = pqx(hl)
                S.mm(ds, KDEC[hl], VN[hl])
                dss[hl] = ds
            for hl, h in H:
                S.stt(S_ST[:, l, h, :], S_ST[:, l, h, :], EGL[:, h:h + 1], dss[hl], ALU.mult, ALU.add)
        else:
            for hl, h in H:
                S.tt('pool', VBLK, VN[hl].unsqueeze(1).broadcast_to([128, 16, 128]),
                     blk16.to_broadcast([128, 16, 128]), ALU.mult)
                S.tt('pool', S0F, S0F, EGLS[:, h, :].to_broadcast([128, 16, 128]), ALU.mult)
                for q4 in range(4):
                    bk = pbank()
                    S.mm(bk[:, :], KDEC[hl], VBLK[:, 4 * q4:4 * q4 + 4, :].rearrange("p a b -> p (a b)"))
                    sl = S0F[:, 4 * q4:4 * q4 + 4, :].rearrange("p a b -> p (a b)")
                    S.tt('dve', sl, bk[:, :], sl, ALU.add)
                S.dma('sp', nss[l, :, h].rearrange("s k v -> k s v"), S0F)

    def stage_merge(g, l):
        for half in range(2):
            Wbp = wget(widx[(g, l, 'wbp', half)], min(PREF, 4))
            Wbd = wget(widx[(g, l, 'wbd', half)], min(PREF, 3))
            Wgp = wget(widx[(g, l, 'gp', half)], min(PREF, 2))
            Wgd = wget(widx[(g, l, 'gd', half)], 1)
            for dl in range(4):
                dcn = half * 4 + dl
                for c in range(NCH):
                    i0 = (dl * NCH + c) % 2
                    gp = fm_proj(Wgp, dl * 128, HT, 8, c)
                    gd = fm_proj(Wgd, dl * 128, HT, 8, c)
                    brp = fm_proj(Wbp, dl * 128, AOT, 4, c)
                    brd = fm_proj(Wbd, dl * 128, DOT, 8, c)
                    S.act(SGT[2 * i0], gp, AF.Sigmoid)
                    S.act(SGT[2 * i0 + 1], gd, AF.Sigmoid)
                    S.tt('dve', T12[2 * i0], brp, SGT[2 * i0], ALU.mult)
                    S.tt('dve', T12[2 * i0 + 1], brd, SGT[2 * i0 + 1], ALU.mult)
                    S.tt('pool', MT[:, dcn, c * CH:(c + 1) * CH], T12[2 * i0], T12[2 * i0 + 1], ALU.add)

    def post_tile(l, which, ti):
        ssq = STAT[:, 48 + ti:49 + ti]
        rs = STAT[:, 56 + ti:57 + ti]
        S.act(JUNK, FF[:, ti, :], AF.Square, accum_out=ssq)
        rstd_from(rs, ssq, D)
        for half in range(2):
            cs = slice(half * 512, (half + 1) * 512)
            S.stt(TMPX[half], FF[:, ti, cs], rs, GPOSTb[:, cs], ALU.mult, ALU.mult)
            S.tt('pool', X[:, ti, cs], X[:, ti, cs], TMPX[half], ALU.add)

    def stage_out(g, l):
        S.dma('sp', GPOSTb, gpost_d[:, l, 0, :])
        for half in range(2):
            Wt = wget(widx[(g, l, 'wo', half)])
            for ti in range(TPG):
                acc = tm_proj(Wt, ti, MT)
                S.cp('act', FF[:, ti, half * 512:(half + 1) * 512], acc)
                if half == 1:
                    post_tile(l, 0, ti)

    def stage_ffn(g, l):
        norm_to_HT(l, 1)
        S.dma('sp', GPOSTb, gpost_d[:, l, 1, :])
        for fh in range(2):
            for f4 in range(4):
                Wt = wget(widx[(g, l, 'up', fh * 4 + f4)])
                for fl in range(4):
                    fi = f4 * 4 + fl
                    for c in range(NCH):
                        acc = fm_proj(Wt, fl * 128, HT, 8, c)
                        rl = RL[(fl * NCH + c) % 2]
                        S.act(rl, acc, AF.Relu)
                        S.tt('pool', UT[:, fi, c * CH:(c + 1) * CH], rl, rl, ALU.mult)
            for half in range(2):
                for kl in range(2):
                    Wt = wget(widx[(g, l, 'dn', half, fh * 2 + kl)])
                    for ti in range(TPG):
                        for kc in range(8):
                            S.mm(PB[ti][:, :], UT[:, kl * 8 + kc, ti * 128:(ti + 1) * 128], Wt[:, kc, :],
                                 start=(kl == 0 and kc == 0), stop=(kl == 1 and kc == 7))
                for ti in range(TPG):
                    dst = FF[:, ti, half * 512:(half + 1) * 512]
                    if fh == 0:
                        S.cp('act', dst, PB[ti][:, :])
                    else:
                        S.tt('dve', dst, PB[ti][:, :], dst, ALU.add)
                        if half == 1:
                            post_tile(l, 1, ti)

    dbg = DEBUG
    def on(name):
        return dbg is None or name in dbg['stages']
    if dbg is None:
        wissue(PREF)
    for g in range(len(GROUPS)):
        if dbg is not None and g not in dbg['groups']:
            continue
        tiles, npt, NW, has_s = group_info(g)
        load_x(g)
        for l in range(NL):
            if dbg is not None and l not in dbg['layers']:
                continue
            if on('norm'):
                norm_to_HT(l, 0)
            if on('pool'):
                stage_pool(g, l)
            if on('ba'):
                stage_ba(g, l)
            if has_s and on('qkv'):
                stage_sconv_prep(l)
            for hb in range(2):
                if on('qkv'):
                    stage_qkv(g, l, hb)
                if on('z'):
                    stage_z(g, l, hb)
                if on('gdn'):
                    stage_gdn(g, l, hb)
            if on('merge'):
                stage_merge(g, l)
            if on('out'):
                stage_out(g, l)
            if on('ffn'):
                stage_ffn(g, l)
        store_x(g)
    if dbg is None:
        assert wuse[0] == len(wlist)
    S.emit()
    return nc, S


_CACHE = {}


def kernel(x_prompt, x_sample, state_conv, state_ssm, state_pool, meta_tokens, g_pre_mix, w_in, conv_w, a_log,
           dt_bias, o_norm_g, pool_w, pool_scale, w_branch_pool, w_branch_delta, w_out, g_post_mix, g_pre_ffn,
           w_up, w_down, g_post_ffn):
    f = lambda a: np.ascontiguousarray(np.asarray(a, dtype=np.float32))
    if 'nc' not in _CACHE:
        _CACHE['nc'] = build_program()[0]
    nc = _CACHE['nc']
    x_prompt, x_sample, state_conv, state_ssm, state_pool = map(f, (x_prompt, x_sample, state_conv, state_ssm, state_pool))
    gpre = np.stack([f(g_pre_mix), f(g_pre_ffn)], axis=1)
    gpre = f(gpre.reshape(NL, 2, 8, 128).transpose(3, 0, 1, 2))
    cw = f(f(conv_w).reshape(NL, 4, 24, 128).transpose(3, 0, 2, 1))
    psc = f(f(pool_scale).reshape(NL, 4, 128).transpose(2, 0, 1))
    gpost = np.stack([f(g_post_mix), f(g_post_ffn)], axis=1)
    gpost = f(np.broadcast_to(gpost[None], (128, NL, 2, D)))
    ong = f(np.broadcast_to(f(o_norm_g)[None], (128, NL, 128)))
    alog = f(np.broadcast_to(f(a_log)[None], (128, NL, 8)))
    dtb = f(np.broadcast_to(f(dt_bias)[None], (128, NL, 8)))
    shared = {
        "meta": f(meta_tokens), "w_in": f(w_in), "pool_w": f(pool_w), "wbp": f(w_branch_pool),
        "wbd": f(w_branch_delta), "wo": f(w_out), "wup": f(w_up), "wdn": f(w_down),
        "consts": make_consts(), "gpre": gpre, "cw": cw, "pscale": psc, "gpost": gpost, "ong": ong,
        "alog": alog, "dtb": dtb,
    }
    in_maps = []
    for c in range(8):
        m = dict(shared)
        m["xp"] = x_prompt[c]
        m["xs"] = f(x_sample[16 * c:16 * c + 16].reshape(128, D))
        m["sconv"] = f(state_conv[:, 16 * c:16 * c + 16])
        m["sssm"] = f(state_ssm[:, 16 * c:16 * c + 16])
        m["spool"] = f(state_pool[:, 16 * c:16 * c + 16])
        m["spoolT"] = f(m["spool"].reshape(NL, 16, 15, 4, 128).transpose(0, 4, 3, 1, 2))
        m["sconvT"] = f(m["sconv"].reshape(NL, 16, 3, 24, 128).transpose(0, 4, 3, 1, 2))
        in_maps.append(m)
    res = run_bass_kernel_spmd(nc, in_maps, core_ids=list(range(8)))
    R = res.results
    y_prompt = np.stack([R[c]["y_p"] for c in range(8)], axis=0)
    y_sample = np.concatenate([R[c]["y_s"].reshape(16, 8, D) for c in range(8)], axis=0)
    ncp = np.stack([R[c]["ncp"] for c in range(8)], axis=1)
    nsp = np.stack([R[c]["nsp"] for c in range(8)], axis=1)
    npp = np.stack([R[c]["npp"] for c in range(8)], axis=1)
    ncs = np.concatenate([R[c]["ncs"] for c in range(8)], axis=1)
    nss = np.concatenate([R[c]["nss"] for c in range(8)], axis=1)
    nps = np.concatenate([R[c]["nps"] for c in range(8)], axis=1)
    return tuple(np.ascontiguousarray(a, dtype=np.float32) for a in (y_prompt, y_sample, ncp, nsp, npp, ncs, nss, nps))
```

```python
import numpy as np
import concourse.bass as bass
import concourse.mybir as mybir
from concourse.bass_utils import run_bass_kernel_spmd

F32 = mybir.dt.float32
BF16 = mybir.dt.bfloat16
AF = mybir.ActivationFunctionType
ALU = mybir.AluOpType

ENGS = ('pe', 'act', 'dve', 'pool', 'sp')
MAXOPS = None


class _Op:
    __slots__ = ('eng', 'fn', 'waits', 'signal', 'dma', 'idx')

    def __init__(self, eng, fn, dma=None):
        self.eng = eng
        self.fn = fn
        self.waits = []
        self.signal = False
        self.dma = dma
        self.idx = 0


class Sched:
    NDMA = 24

    def __init__(self, nc):
        self.nc = nc
        self.ops = {e: [] for e in ENGS}
        self.recs = {}
        self.water = {e: {} for e in ENGS}
        self.dma_tot = [0] * self.NDMA
        self.dma_last = [None] * self.NDMA
        self.dma_rr = 0
        self.dma_rr_sw = 0
        self.tensors = []
        self.same_eng_dist = 1 << 30

    def sbuf(self, name, shape, dtype):
        t = self.nc.alloc_sbuf_tensor(name, list(shape), dtype)
        return t

    def psum(self, name, shape, dtype):
        return self.nc.alloc_psum_tensor(name, list(shape), dtype)

    @staticmethod
    def _box(ap):
        t = ap.tensor
        shp = list(t.shape)
        row = 1
        for s in shp[1:]:
            row *= s
        dsz = mybir.dt.size(ap.dtype)
        pat = list(ap.ap)
        off = ap.offset
        p0 = off // row
        f0 = (off % row) * dsz
        pstep, pcnt = pat[0]
        if pstep == 0:
            np_ = 1
        else:
            np_ = (pcnt - 1) * (pstep // row) + 1
        ext = 0
        for st, cnt in pat[1:]:
            ext += (cnt - 1) * abs(st)
        if type(t).__name__ == 'PSumTensorHandle':
            return (p0, p0 + np_, 0, 1 << 20)
        return (p0, p0 + np_, f0, f0 + (ext + 1) * dsz)

    def _track(self, op, aps_r, aps_w):
        deps = set()
        for kind, aps in (('r', aps_r), ('w', aps_w)):
            for ap in aps:
                name = ap.tensor.name
                box = self._box(ap)
                if type(ap.tensor).__name__ == 'PSumTensorHandle':
                    kind = 'w'
                recs = self.recs.setdefault(name, {})
                dead = []
                for (b, k, key), tok in recs.items():
                    if b[0] < box[1] and box[0] < b[1] and b[2] < box[3] and box[2] < b[3]:
                        if kind == 'w' or k == 'w':
                            deps.add(tok)
                        if kind == 'w' and box[0] <= b[0] and b[1] <= box[1] and box[2] <= b[2] and b[3] <= box[3]:
                            dead.append((b, k, key))
                for d in dead:
                    del recs[d]
        return deps

    def _record(self, op, tok, aps_r, aps_w):
        for kind, aps in (('r', aps_r), ('w', aps_w)):
            for ap in aps:
                name = ap.tensor.name
                box = self._box(ap)
                if type(ap.tensor).__name__ == 'PSumTensorHandle':
                    kind = 'w'
                key = tok[0] if tok[0] != 'dma' else ('dma', tok[1])
                self.recs.setdefault(name, {})[(box, kind, key)] = tok

    def _add_waits(self, op, deps):
        eng = op.eng
        my_idx = len(self.ops[eng])
        for tok in deps:
            if tok[0] == 'dma':
                _, s, val = tok
                key = ('dma', s)
                if self.water[eng].get(key, 0) >= val:
                    continue
                self.water[eng][key] = val
                op.waits.append(tok)
            else:
                f, i = tok
                if f == eng:
                    if eng in ('pe', 'sp'):
                        continue
                    if my_idx - i > self.same_eng_dist:
                        continue
                if self.water[eng].get(f, -1) >= i:
                    continue
                self.water[eng][f] = i
                self.ops[f][i].signal = True
                op.waits.append(tok)

    @staticmethod
    def _is_onchip(ap):
        return type(ap.tensor).__name__ in ('SBTensorHandle', 'PSumTensorHandle')

    def op(self, eng, fn, r, w):
        self.nrec = getattr(self, 'nrec', 0) + 1
        if MAXOPS is not None and self.nrec > MAXOPS:
            return None
        o = _Op(eng, fn)
        r = [a for a in r if self._is_onchip(a)]
        w = [a for a in w if self._is_onchip(a)]
        deps = self._track(o, r, w)
        self._add_waits(o, deps)
        o.idx = len(self.ops[eng])
        self.ops[eng].append(o)
        self._record(o, (eng, o.idx), r, w)
        return o

    def dma(self, queue, out, in_):
        self.nrec = getattr(self, 'nrec', 0) + 1
        if MAXOPS is not None and self.nrec > MAXOPS:
            return None
        if queue == 'pool':
            s = 16 + self.dma_rr_sw
            self.dma_rr_sw = (self.dma_rr_sw + 1) % 8
        else:
            s = self.dma_rr
            self.dma_rr = (self.dma_rr + 1) % 16
        o = _Op(queue, None, dma=(s, out, in_))
        r = [in_] if self._is_onchip(in_) else []
        w = [out] if self._is_onchip(out) else []
        deps = self._track(o, r, w)
        if self.dma_last[s] is not None:
            deps.add(self.dma_last[s])
        self._add_waits(o, deps)
        self.dma_tot[s] += 16
        tok = ('dma', s, self.dma_tot[s])
        self.dma_last[s] = tok
        o.idx = len(self.ops[queue])
        self.ops[queue].append(o)
        self._record(o, tok, r, w)
        return tok

    def mm(self, out, lhsT, rhs, start=True, stop=True):
        return self.op('pe', lambda e: e.matmul(out, lhsT, rhs, start=start, stop=stop), [lhsT, rhs], [out])

    def tr(self, out, in_, ident):
        return self.op('pe', lambda e: e.transpose(out, in_, ident), [in_, ident], [out])

    def trf(self, out, in_, ident):
        return self.op('pe', lambda e: e.matmul(out, in_, ident, start=True, stop=True), [in_, ident], [out])

    def act(self, out, in_, func, bias=None, scale=None, accum_out=None, eng='act'):
        kw = {}
        r = [in_]
        w = [out]
        if bias is not None:
            kw['bias'] = bias
            if not isinstance(bias, (int, float)):
                r.append(bias)
        if scale is not None:
            kw['scale'] = scale
            if not isinstance(scale, (int, float)):
                r.append(scale)
        if accum_out is not None:
            kw['accum_out'] = accum_out
            w.append(accum_out)
        return self.op('act', lambda e: e.activation(out, in_, func, **kw), r, w)

    def tt(self, eng, out, in0, in1, op):
        return self.op(eng, lambda e: e.tensor_tensor(out, in0, in1, op), [in0, in1], [out])

    def ts(self, eng, out, in0, s1, s2, op0, op1=None, accum_out=None):
        r = [in0]
        if not isinstance(s1, (int, float)) and s1 is not None:
            r.append(s1)
        if not isinstance(s2, (int, float)) and s2 is not None:
            r.append(s2)
        w = [out]
        kw = {}
        if op1 is not None:
            kw['op1'] = op1
        if accum_out is not None:
            kw['accum_out'] = accum_out
            w.append(accum_out)
        return self.op(eng, lambda e: e.tensor_scalar(out, in0, s1, s2, op0, **kw), r, w)

    def stt(self, out, in0, scalar, in1, op0, op1, eng='dve'):
        r = [in0, in1]
        if not isinstance(scalar, (int, float)):
            r.append(scalar)
        return self.op(eng, lambda e: e.scalar_tensor_tensor(out, in0, scalar, in1, op0, op1), r, [out])

    def cp(self, eng, out, in_):
        if eng == 'act':
            return self.op('act', lambda e: e.copy(out, in_), [in_], [out])
        return self.op(eng, lambda e: e.tensor_copy(out, in_), [in_], [out])

    def memset(self, eng, ap, val):
        return self.op(eng, lambda e: e.memset(ap, val), [], [ap])

    def emit(self, final_wait=True):
        nc = self.nc
        engobj = {'pe': nc.tensor, 'act': nc.scalar, 'dve': nc.vector, 'pool': nc.gpsimd, 'sp': nc.sync}
        sems = {e: nc.alloc_semaphore("prog_" + e) for e in ENGS}
        dsems = [nc.alloc_semaphore("dma_%d" % i) for i in range(self.NDMA)]
        cnt = {}
        for e in ENGS:
            c = 0
            arr = []
            for o in self.ops[e]:
                if o.signal and o.dma is None:
                    c += 1
                arr.append(c)
            cnt[e] = arr
        blockattr = {'pe': 'tensor', 'act': 'scalar', 'dve': 'vector', 'pool': 'gpsimd', 'sp': 'sync'}
        with nc.Block() as block:
            for e in ENGS:
                ops = self.ops[e]

                def body(eng, e=e, ops=ops):
                    for o in ops:
                        for tok in o.waits:
                            if tok[0] == 'dma':
                                eng.wait_ge(dsems[tok[1]], tok[2])
                            else:
                                eng.wait_ge(sems[tok[0]], cnt[tok[0]][tok[1]])
                        if o.dma is not None:
                            s, out, in_ = o.dma
                            eng.dma_start(out=out, in_=in_).then_inc(dsems[s], 16)
                        else:
                            ins = o.fn(eng)
                            if o.signal:
                                ins.then_inc(sems[e], 1)
                    if e == 'sp' and final_wait:
                        for s in range(self.NDMA):
                            if self.dma_tot[s] > 0:
                                eng.wait_ge(dsems[s], self.dma_tot[s])
                getattr(block, blockattr[e])(body)


D = 1024
NH = 8
IN_W = 6672
DFF = 4096
NL = 2
TPG = 6
NT = TPG * 128
CH = 384
NCH = NT // CH
EPS = 1e-6
POOL_W = (2, 4, 8, 16)
C_ID, C_UP, C_US, C_SLP, C_SLS, C_SAMES, C_ONES, C_NEGP, C_NEGS, C_BLK, C_INVC = (
    0, 128, 256, 384, 512, 640, 768, 896, 1024, 1152, 1168)
NCONST = 1232
GROUPS = [[('P', i) for i in range(0, 6)], [('P', i) for i in range(6, 12)],
          [('P', i) for i in range(12, 17)] + [('S', 0)]]
CHAIN_F32 = True
DEBUG = None


def make_consts():
    c = np.zeros((128, NCONST), np.float32)
    p = np.arange(128)
    same = (p[:, None] // 8) == (p[None, :] // 8)
    c[:, C_ID:C_ID + 128] = np.eye(128)
    c[:, C_UP:C_UP + 128] = (p[:, None] <= p[None, :])
    c[:, C_US:C_US + 128] = (p[:, None] <= p[None, :]) & same
    c[:, C_SLP:C_SLP + 128] = (p[:, None] > p[None, :])
    c[:, C_SLS:C_SLS + 128] = (p[:, None] > p[None, :]) & same
    c[:, C_SAMES:C_SAMES + 128] = same
    c[:, C_ONES:C_ONES + 128] = 1.0
    c[:, C_NEGP:C_NEGP + 128] = np.where(p[:, None] < p[None, :], 0.0, -30000.0)
    c[:, C_NEGS:C_NEGS + 128] = np.where((p[:, None] < p[None, :]) & same, 0.0, -30000.0)
    c[:, C_BLK:C_BLK + 16] = (p[:, None] // 8) == np.arange(16)[None, :]
    for gi, w in enumerate(POOL_W):
        for pos in range(16):
            c[:, C_INVC + gi * 16 + pos] = 1.0 / min(pos + 1, w)
    return c


def build_program():
    nc = bass.Bass("TRN2", target_bir_lowering=False)

    def din(name, shape):
        return nc.dram_tensor(name, list(shape), F32, kind="ExternalInput").ap()

    def dout(name, shape):
        return nc.dram_tensor(name, list(shape), F32, kind="ExternalOutput").ap()

    xp = din("xp", [2048, D]); xs = din("xs", [128, D]); meta = din("meta", [16, D])
    sconv = din("sconv", [NL, 16, 3, 3072]); sssm = din("sssm", [NL, 16, NH, 128, 128])
    spool = din("spool", [NL, 16, 15, 512])
    spoolT = din("spoolT", [NL, 128, 4, 16, 15]); sconvT = din("sconvT", [NL, 128, 24, 16, 3])
    w_in = din("w_in", [NL, D, IN_W]); pool_w = din("pool_w", [NL, 4, 128, 128])
    wbp = din("wbp", [NL, 512, D]); wbd = din("wbd", [NL, D, D]); wo = din("wo", [NL, D, D])
    wup = din("wup", [NL, D, DFF]); wdn = din("wdn", [NL, DFF, D])
    consts = din("consts", [128, NCONST]); gpre_d = din("gpre", [128, NL, 2, 8])
    cw_d = din("cw", [128, NL, 24, 4]); pscale_d = din("pscale", [128, NL, 4])
    gpost_d = din("gpost", [128, NL, 2, D]); ong_d = din("ong", [128, NL, 128])
    alog_d = din("alog", [128, NL, 8]); dtb_d = din("dtb", [128, NL, 8])
    y_p = dout("y_p", [2048, D]); y_s = dout("y_s", [128, D])
    ncp = dout("ncp", [NL, 3, 3072]); nsp = dout("nsp", [NL, NH, 128, 128]); npp = dout("npp", [NL, 15, 512])
    ncs = dout("ncs", [NL, 16, 3, 3072]); nss = dout("nss", [NL, 16, NH, 128, 128]); nps = dout("nps", [NL, 16, 15, 512])

    S = Sched(nc)
    CD = F32 if CHAIN_F32 else BF16

    X = S.sbuf("X", [128, TPG, D], F32)
    HT = S.sbuf("HT", [128, 8, NT], BF16)
    BIG = S.sbuf("BIG", [128, 16 * NT], BF16)
    UT = BIG[:, :].rearrange("p (a b) -> p a b", b=NT)
    QT4 = BIG[:, 0:4 * NT].rearrange("p (a b) -> p a b", b=NT)
    KT4 = BIG[:, 4 * NT:8 * NT].rearrange("p (a b) -> p a b", b=NT)
    VT4 = BIG[:, 8 * NT:12 * NT].rearrange("p (a b) -> p a b", b=NT)
    ZS4 = BIG[:, 12 * NT:16 * NT].rearrange("p (t c) -> p t c", c=512)
    MT = BIG[:, 0:8 * NT].rearrange("p (a b) -> p a b", b=NT)
    DM = S.sbuf("DM", [128, 16 * NT], BF16)
    DOT = DM[:, 0:8 * NT].rearrange("p (a b) -> p a b", b=NT)
    AOT = DM[:, 8 * NT:12 * NT].rearrange("p (a b) -> p a b", b=NT)
    FF = DM[:, :].bitcast(F32).rearrange("p (t c) -> p t c", c=D)
    NSLOT = 5
    PREF = 4
    WR = [S.sbuf("WR%d" % i, [128, 8, 512], BF16) for i in range(NSLOT)]
    CST = S.sbuf("CST", [128, NCONST], F32)
    IDB = S.sbuf("IDB", [128, 128], BF16)
    GPRE = S.sbuf("GPRE", [128, NL, 2, 8], F32)
    CW = S.sbuf("CW", [128, NL, 24, 4], F32)
    PSC = S.sbuf("PSC", [128, NL, 4], F32)
    ONG = S.sbuf("ONG", [128, NL, 128], F32)
    NEGA = S.sbuf("NEGA", [128, NL, 8], F32)
    DTB = S.sbuf("DTB", [128, NL, 8], F32)
    EPSC = S.sbuf("EPSC", [128, 2], F32)
    S_ST = S.sbuf("S_ST", [128, NL, NH, 128], F32)
    CPRE = S.sbuf("CPRE", [128, NL, 24, 3], F32)
    PPRE = S.sbuf("PPRE", [128, NL, 4, 15], F32)
    PWL = S.sbuf("PWL", [128, 4, 128], BF16)
    WBA = S.sbuf("WBA", [128, 8, 16], BF16)
    STAT = S.sbuf("STAT", [128, 64], F32)
    BETA = S.sbuf("BETA", [128, TPG, 8], F32)
    NBETA = S.sbuf("NBETA", [128, TPG, 8], F32)
    GG = S.sbuf("GG", [128, TPG, 8], F32)
    TA = S.sbuf("TA", [128, 8], F32)
    EGC = S.sbuf("EGC", [128, 8], F32)
    EGD = S.sbuf("EGD", [128, 8], F32)
    EGL = S.sbuf("EGL", [128, 8], F32)
    GCs = S.sbuf("GCs", [128, 8], F32)
    GBS = S.sbuf("GBS", [128, 8, 16], F32)
    EGLS = S.sbuf("EGLS", [128, 8, 16], F32)
    SPOOLT = S.sbuf("SPOOLT", [128, 4, 16, 15], F32)
    SCONVT = S.sbuf("SCONVT", [128, 24, 16, 3], F32)

    SCR_BYTES = 52 * 1024
    SCR = S.sbuf("SCR", [128, SCR_BYTES // 2], BF16)

    class Arena:
        def __init__(self):
            self.off = 0

        def take(self, free, dtype):
            n = 1
            for f in free:
                n *= f
            nb = n * mybir.dt.size(dtype)
            nb_al = (nb + 31) // 32 * 32
            assert self.off + nb_al <= SCR_BYTES, ("arena overflow", self.off, nb_al)
            ap = SCR[:, self.off // 2:(self.off + nb) // 2]
            self.off += nb_al
            if dtype != BF16:
                ap = ap.bitcast(dtype)
            if len(free) == 2:
                ap = ap.rearrange("p (a b) -> p a b", b=free[1])
            elif len(free) == 3:
                ap = ap.rearrange("p (a b c) -> p a b c", b=free[1], c=free[2])
            return ap

    EXTW = 16 + NT
    A = Arena()
    JUNK = A.take([D], BF16)
    HB = [A.take([D], BF16) for _ in range(2)]
    offA0 = A.off
    UEXT = [A.take([EXTW], F32) for _ in range(2)]
    SCEXT = A.take([16, 11], F32)
    TMROW = A.take([512], F32)
    offA1 = A.off
    PA = A.take([EXTW], F32)
    PBf = A.take([EXTW], F32)
    MTP = A.take([NT], BF16)
    SEXT = A.take([16, 23], F32)
    SPA = A.take([16, 23], F32)
    SPB = A.take([16, 23], F32)
    A.off = offA1
    CQ2 = [A.take([NT], F32) for _ in range(2)]
    SQ2_2 = [A.take([NT], F32) for _ in range(2)]
    SQ_4 = [A.take([NT], F32) for _ in range(4)]
    RN_4 = [A.take([NT], F32) for _ in range(4)]
    A.off = offA0
    GPOSTb = A.take([D], F32)
    TMPX = [A.take([512], F32) for _ in range(2)]
    SGT = [A.take([CH], F32) for _ in range(4)]
    T12 = [A.take([CH], F32) for _ in range(4)]
    RL = [A.take([CH], F32) for _ in range(2)]
    G = Arena()
    NB = 4
    Bm = [G.take([128], F32) for _ in range(NB)]
    DTS = [G.take([128], F32) for _ in range(NB)]
    DTM = [G.take([128], F32) for _ in range(NB)]
    CA = [[G.take([128], CD) for _ in range(2)] for _ in range(NB)]
    CAT = [[G.take([128], CD) for _ in range(2)] for _ in range(NB)]
    CR = [[G.take([128], CD) for _ in range(2)] for _ in range(NB)]
    KE = [G.take([128], CD) for _ in range(NB)]
    KDEC = [G.take([128], BF16) for _ in range(NB)]
    VTM = [G.take([128], CD) for _ in range(NB)]
    WTt = [G.take([128], BF16) for _ in range(NB)]
    UB = [G.take([128], F32) for _ in range(NB)]
    VN = [G.take([128], BF16) for _ in range(NB)]
    ATt = [G.take([128], BF16) for _ in range(NB)]
    O1 = [G.take([128], F32) for _ in range(NB)]
    OO = [G.take([128], F32) for _ in range(NB)]
    DD = [G.take([128], BF16) for _ in range(NB)]
    SBF = [G.take([128], BF16) for _ in range(NB)]
    JG = [G.take([128], BF16) for _ in range(NB)]
    FMS = [G.take([128], F32) for _ in range(2)]
    S0F = G.take([16, 128], F32)
    S0B = G.take([16, 128], BF16)
    VBLK = G.take([16, 128], BF16)

    PB = [S.psum("PB%d" % i, [128, 512], F32) for i in range(6)]
    PH = [S.psum("PH%d" % i, [128, 1024], BF16) for i in range(2)]
    qctr = [0]

    def pq_align():
        qctr[0] = (qctr[0] + 3) // 4 * 4 % 8

    def pq():
        i = qctr[0]
        qctr[0] = (i + 1) % 8
        return PB[4 + i // 4][:, (i % 4) * 128:(i % 4 + 1) * 128]
    bctr = [0]

    def pbank():
        i = bctr[0]
        bctr[0] = (i + 1) % 6
        return PB[i]
    hctr = [0]
    hq = [0, 0, 0, 0]

    def pqb_align():
        hctr[0] = (hctr[0] + 3) // 4 * 4 % 16

    def pqb():
        i = hctr[0]
        hctr[0] = (i + 1) % 16
        slot = (i % 4) + 4 * ((i // 8) % 2)
        return PH[(i // 4) % 2][:, slot * 128:(slot + 1) * 128]

    ident = CST[:, C_ID:C_ID + 128]
    ident_cd = ident if CHAIN_F32 else IDB[:, :]
    ones_f = CST[:, C_ONES:C_ONES + 128]
    blk16 = CST[:, C_BLK:C_BLK + 16]
    eps_c = EPSC[:, 0:1]
    one_c = EPSC[:, 1:2]

    wlist = []

    def wsrc_k8(wt, l, c0, ncols=512):
        return wt[l, :, c0:c0 + ncols].rearrange("(dc p) c -> p dc c", p=128)

    wstate = {'issued': 0}

    def wdma(n):
        for (src, kc, coff, ncol) in wlist[n]:
            S.dma('pool', WR[n % NSLOT][:, 0:kc, coff:coff + ncol], src)

    def wissue(upto):
        while wstate['issued'] <= upto and wstate['issued'] < len(wlist):
            wdma(wstate['issued'])
            wstate['issued'] += 1

    wuse = [0]

    def wget(n, pref=None):
        pref = PREF if pref is None else pref
        if DEBUG is None:
            assert n == wuse[0], ("weight order mismatch", n, wuse[0])
        wuse[0] += 1
        if DEBUG is None:
            wissue(n + pref)
        else:
            wdma(n)
        return WR[n % NSLOT]

    widx = {}
    for g in range(len(GROUPS)):
        for l in range(NL):
            def add(key, src, kc=8, ncol=512):
                widx[(g, l) + key] = len(wlist)
                wlist.append([(src, kc, 0, ncol)])

            def add2(key, parts):
                widx[(g, l) + key] = len(wlist)
                wlist.append(parts)
            add(('u',), wsrc_k8(w_in, l, 0))
            for hb in range(2):
                for comp in range(3):
                    add(('qkv', comp, hb), wsrc_k8(w_in, l, 512 + comp * 1024 + hb * 512))
                add(('z', hb), wsrc_k8(w_in, l, 3600 + hb * 512))
            for q in range(4):
                add2(('mg', q), [(wsrc_k8(w_in, l, 4624 + q * 256, 256), 8, 0, 256),
                                 (wsrc_k8(w_in, l, 5648 + q * 256, 256), 8, 256, 256)])
                add2(('mb', q), [(wbp[l, :, q * 256:(q + 1) * 256].rearrange("(dc p) c -> p dc c", p=128), 4, 0, 256),
                                 (wsrc_k8(wbd, l, q * 256, 256), 8, 256, 256)])
            for half in range(2):
                add(('wo', half), wsrc_k8(wo, l, half * 512))
            for fh in range(2):
                for f4 in range(4):
                    add(('up', fh * 4 + f4), wsrc_k8(wup, l, (fh * 4 + f4) * 512))
                for half in range(2):
                    for kl in range(2):
                        kcg = fh * 2 + kl
                        add(('dn', half, kcg),
                            wdn[l, kcg * 1024:(kcg + 1) * 1024, half * 512:(half + 1) * 512].rearrange("(dc p) c -> p dc c", p=128))

    S.dma('sp', CST[:], consts)
    S.dma('sp', GPRE[:], gpre_d)
    S.dma('sp', CW[:], cw_d)
    S.dma('sp', PSC[:], pscale_d)
    S.dma('sp', ONG[:], ong_d)
    S.dma('sp', NEGA[:], alog_d)
    S.dma('sp', DTB[:], dtb_d)
    S.cp('dve', IDB[:], ident)
    S.memset('dve', EPSC[:, 0:1], EPS)
    S.memset('dve', EPSC[:, 1:2], 1.0)
    S.act(NEGA[:], NEGA[:], AF.Exp)
    S.ts('pool', NEGA[:], NEGA[:], -1.0, None, ALU.mult)
    S.memset('pool', S_ST[:], 0.0)
    S.memset('pool', CPRE[:], 0.0)
    S.memset('pool', PPRE[:], 0.0)

    def rstd_from(out, ssq, n):
        S.act(out, ssq, AF.Ln, bias=eps_c, scale=1.0 / n)
        S.act(out, out, AF.Exp, scale=-0.5)

    def group_info(g):
        tiles = GROUPS[g]
        npt = sum(1 for t in tiles if t[0] == 'P')
        has_s = any(t[0] == 'S' for t in tiles)
        return tiles, npt, npt * 128, has_s

    def chunk_ranges(c, NW, has_s):
        a, b = c * CH, min((c + 1) * CH, NW)
        pr = (a, b) if b > a else None
        sr = has_s and (c + 1) * CH == NT
        return pr, sr

    def load_x(g):
        tiles, npt, NW, has_s = group_info(g)
        ti = 0
        if tiles[0] == ('P', 0):
            S.memset('pool', X[:, 0, :], 0.0)
            S.dma('sp', X[112:128, 0, :], meta)
            ti = 1
        if npt > ti:
            pt0 = tiles[ti][1]
            S.dma('sp', X[:, ti:npt, :], xp[(pt0 - 1) * 128:(pt0 - 1 + npt - ti) * 128, :].rearrange("(t p) c -> p t c", p=128))
        if has_s:
            S.dma('sp', X[:, TPG - 1, :], xs)

    def store_x(g):
        tiles, npt, NW, has_s = group_info(g)
        ti = 1 if tiles[0] == ('P', 0) else 0
        if npt > ti:
            pt0 = tiles[ti][1]
            S.dma('sp', y_p[(pt0 - 1) * 128:(pt0 - 1 + npt - ti) * 128, :].rearrange("(t p) c -> p t c", p=128), X[:, ti:npt, :])
        if has_s:
            S.dma('sp', y_s, X[:, TPG - 1, :])

    def norm_to_HT(l, which):
        for ti in range(TPG):
            ssq = STAT[:, ti:ti + 1]
            rs = STAT[:, 8 + ti:9 + ti]
            S.act(JUNK, X[:, ti, :], AF.Square, accum_out=ssq)
            rstd_from(rs, ssq, D)
            hb = HB[ti % 2]
            S.ts('dve', hb, X[:, ti, :], rs, None, ALU.mult)
            ph = PH[ti % 2]
            for dc in range(8):
                S.tr(ph[:, dc * 128:(dc + 1) * 128], hb[:, dc * 128:(dc + 1) * 128], IDB[:])
            S.tt('dve', HT[:, :, ti * 128:(ti + 1) * 128],
                 ph[:, :].rearrange("p (a b) -> p a b", b=128),
                 GPRE[:, l, which, :].to_broadcast([128, 8, 128]), ALU.mult)

    def fm_proj(wtile, c0, rhs_buf, nk, cidx):
        acc = pbank()[:, 0:CH]
        for kc in range(nk):
            S.mm(acc, wtile[:, kc, c0:c0 + 128], rhs_buf[:, kc, cidx * CH:(cidx + 1) * CH],
                 start=(kc == 0), stop=(kc == nk - 1))
        return acc

    def tm_proj(wtile, ti, src_buf, nk=8, ncol=512):
        acc = pbank()[:, 0:ncol]
        for kc in range(nk):
            S.mm(acc, src_buf[:, kc, ti * 128:(ti + 1) * 128], wtile[:, kc, 0:ncol],
                 start=(kc == 0), stop=(kc == nk - 1))
        return acc

    def stage_pool(g, l):
        tiles, npt, NW, has_s = group_info(g)
        W0 = wget(widx[(g, l, 'u')])
        S.dma('pool', PWL[:], pool_w[l].rearrange("g c d -> c g d"))
        if has_s:
            S.dma('sp', SPOOLT[:], spoolT[l])
            S.dma('sp', nps[l, :, 0:7, :], spool[l, :, 8:15, :])
            acc = tm_proj(W0, npt - 1, HT)
            S.cp('act', TMROW, acc)
            S.dma('sp', npp[l], TMROW[113:128, :])
            acc = tm_proj(W0, TPG - 1, HT)
            S.cp('act', TMROW, acc)
            for t in range(8):
                S.dma('sp', nps[l, :, 7 + t, :], TMROW[t::8, :])
        for gi in range(4):
            w = POOL_W[gi]
            ue = UEXT[gi % 2]
            S.cp('pool', ue[:, 1:16], PPRE[:, l, gi, :])
            for c in range(NCH):
                acc = fm_proj(W0, gi * 128, HT, 8, c)
                pr, sr = chunk_ranges(c, NW, has_s)
                if pr:
                    S.cp('act', ue[:, 16 + pr[0]:16 + pr[1]], acc[:, pr[0] - c * CH:pr[1] - c * CH])
                if sr:
                    S.cp('act', SEXT[:, :, 15:23], acc[:, CH - 128:CH].rearrange("p (s t) -> p s t", t=8))
            Wd = 16 + NW
            S.cp('pool', PPRE[:, l, gi, :], ue[:, Wd - 15:Wd])
            src = ue
            for j in range(gi + 1):
                k = 1 << j
                sh = (1 << (j + 1))
                dst = PA if j % 2 == 0 else PBf
                S.tt('pool', dst[:, sh:Wd], src[:, sh:Wd], src[:, sh - k:Wd - k], ALU.add)
                src = dst
            S.stt(MTP[:, 0:NW], src[:, 16:Wd], 1.0 / w, ue[:, 16:Wd], ALU.mult, ALU.subtract)
            if g == 0:
                tmp = STAT[:, 32:48]
                S.tt('dve', tmp, src[:, 16 + 112:16 + 128], CST[:, C_INVC + gi * 16:C_INVC + gi * 16 + 16], ALU.mult)
                S.tt('dve', MTP[:, 112:128], tmp, ue[:, 16 + 112:16 + 128], ALU.subtract)
            if has_s:
                S.cp('pool', SEXT[:, :, 0:15], SPOOLT[:, gi, :, :])
                ssrc = SEXT
                for j in range(gi + 1):
                    k = 1 << j
                    sh = (1 << (j + 1)) - 1
                    dst = SPA if j % 2 == 0 else SPB
                    S.tt('pool', dst[:, :, sh:23], ssrc[:, :, sh:23], ssrc[:, :, sh - k:23 - k], ALU.add)
                    ssrc = dst
                S.stt(MTP[:, NW:NT].rearrange("p (s t) -> p s t", t=8), ssrc[:, :, 15:23], 1.0 / w,
                      SEXT[:, :, 15:23], ALU.mult, ALU.subtract)
            for c in range(NCH):
                acc = pbank()[:, 0:CH]
                S.mm(acc, PWL[:, gi, :], MTP[:, c * CH:(c + 1) * CH])
                S.act(AOT[:, gi, c * CH:(c + 1) * CH], acc, AF.Copy, scale=PSC[:, l, gi:gi + 1])

    def stage_sconv_prep(l):
        S.dma('sp', SCONVT[:], sconvT[l])

    def stage_qkv(g, l, hb):
        tiles, npt, NW, has_s = group_info(g)
        for comp in range(3):
            Wt = wget(widx[(g, l, 'qkv', comp, hb)])
            if has_s:
                colbase = comp * 1024 + hb * 512
                acc = tm_proj(Wt, npt - 1, HT)
                S.cp('act', TMROW, acc)
                S.dma('sp', ncp[l, :, colbase:colbase + 512], TMROW[125:128, :])
                acc = tm_proj(Wt, TPG - 1, HT)
                S.cp('act', TMROW, acc)
                for t in range(3):
                    S.dma('sp', ncs[l, :, t, colbase:colbase + 512], TMROW[5 + t::8, :])
            def a1(hl):
                h = hb * 4 + hl
                ch = comp * 8 + h
                c0 = hl * 128
                ext = UEXT[ch % 2]
                S.cp('pool', ext[:, 0:3], CPRE[:, l, ch, :])
                for c in range(NCH):
                    acc = fm_proj(Wt, c0, HT, 8, c)
                    pr, sr = chunk_ranges(c, NW, has_s)
                    if pr:
                        S.cp('act', ext[:, 3 + pr[0]:3 + pr[1]], acc[:, pr[0] - c * CH:pr[1] - c * CH])
                    if sr:
                        S.cp('act', SCEXT[:, :, 3:11], acc[:, CH - 128:CH].rearrange("p (s t) -> p s t", t=8))
                S.cp('pool', CPRE[:, l, ch, :], ext[:, NW:NW + 3])
                if has_s:
                    CQ = CQ2[ch % 2]
                    S.cp('pool', SCEXT[:, :, 0:3], SCONVT[:, ch, :, :])
                    cqs = CQ[:, NW:NT].rearrange("p (s t) -> p s t", t=8)
                    S.ts('dve', cqs, SCEXT[:, :, 0:8], CW[:, l, ch, 0:1], None, ALU.mult)
                    for j in range(1, 4):
                        S.stt(cqs, SCEXT[:, :, j:j + 8], CW[:, l, ch, j:j + 1], cqs, ALU.mult, ALU.add)

            def a2(hl):
                h = hb * 4 + hl
                ch = comp * 8 + h
                ext = UEXT[ch % 2]
                CQ, SQ2, SQ, RN = CQ2[ch % 2], SQ2_2[ch % 2], SQ_4[hl], RN_4[hl]
                S.ts('dve', CQ[:, 0:NW], ext[:, 0:NW], CW[:, l, ch, 0:1], None, ALU.mult)
                for j in range(1, 4):
                    S.stt(CQ[:, 0:NW], ext[:, j:NW + j], CW[:, l, ch, j:j + 1], CQ[:, 0:NW], ALU.mult, ALU.add)
                if comp == 2:
                    S.act(VT4[:, hl, :], CQ, AF.Silu)
                else:
                    S.act(SQ, CQ, AF.Silu)
                    S.tt('pool', SQ2, SQ, SQ, ALU.mult)
                    for c in range(NCH):
                        acc = pbank()[:, 0:CH]
                        S.mm(acc, ones_f, SQ2[:, c * CH:(c + 1) * CH])
                        S.cp('act', RN[:, c * CH:(c + 1) * CH], acc)

            def phase_b(hl):
                SQ, RN = SQ_4[hl], RN_4[hl]
                S.act(RN, RN, AF.Ln, bias=eps_c, scale=1.0)
                S.act(RN, RN, AF.Exp, scale=-0.5)
                dst = QT4 if comp == 0 else KT4
                sc = (128.0 ** -0.5) if comp == 0 else 1.0
                S.stt(dst[:, hl, :], SQ, sc, RN, ALU.mult, ALU.mult)

            a1(0)
            for hl in range(4):
                if hl + 1 < 4:
                    a1(hl + 1)
                a2(hl)
            if comp != 2:
                for hl in range(4):
                    phase_b(hl)

    def stage_ba(g, l):
        S.dma('pool', WBA[:], w_in[l, :, 3584:3600].rearrange("(dc p) c -> p dc c", p=128))
        for ti in range(TPG):
            acc = pq()[:, 0:16]
            for kc in range(8):
                S.mm(acc, HT[:, kc, ti * 128:(ti + 1) * 128], WBA[:, kc, :], start=(kc == 0), stop=(kc == 7))
            S.act(BETA[:, ti, :], acc[:, 0:8], AF.Sigmoid)
            S.cp('act', TA[:], acc[:, 8:16])
            S.tt('dve', TA[:], TA[:], DTB[:, l, :], ALU.add)
            S.act(TA[:], TA[:], AF.Exp)
            S.act(TA[:], TA[:], AF.Ln, bias=one_c, scale=1.0)
            S.tt('dve', GG[:, ti, :], TA[:], NEGA[:, l, :], ALU.mult)
        S.ts('pool', NBETA[:], BETA[:], -1.0, None, ALU.mult)

    def stage_z(g, l, hb):
        Wt = wget(widx[(g, l, 'z', hb)])
        for ti in range(TPG):
            acc = tm_proj(Wt, ti, HT)
            S.act(ZS4[:, ti, :], acc, AF.Silu)
            zv = ZS4[:, ti, :].rearrange("p (h v) -> p h v", v=128)
            S.tt('pool', zv, zv, ONG[:, l, :].unsqueeze(1).broadcast_to([128, 4, 128]), ALU.mult)

    def stage_gdn(g, l, hb):
        tiles, npt, NW, has_s = group_info(g)
        for ti, (kind, pidx) in enumerate(tiles):
            isS = (kind == 'S')
            Umat = CST[:, C_US:C_US + 128] if isS else CST[:, C_UP:C_UP + 128]
            SLmat = CST[:, C_SLS:C_SLS + 128] if isS else CST[:, C_SLP:C_SLP + 128]
            SAMEmat = CST[:, C_SAMES:C_SAMES + 128] if isS else ones_f
            NEGmat = CST[:, C_NEGS:C_NEGS + 128] if isS else CST[:, C_NEGP:C_NEGP + 128]
            L = 3 if isS else 6
            tsl = slice(ti * 128, (ti + 1) * 128)
            gc_ps = pq()[:, 0:8]
            S.mm(gc_ps, Umat, GG[:, ti, :])
            gl_ps = pq()[:, 0:8]
            S.mm(gl_ps, SAMEmat, GG[:, ti, :])
            S.act(EGC[:], gc_ps, AF.Exp)
            S.cp('act', GCs[:], gc_ps)
            S.cp('act', EGD[:], gl_ps)
            S.act(EGL[:], gl_ps, AF.Exp)
            S.tt('dve', EGD[:], EGD[:], GCs[:], ALU.subtract)
            S.act(EGD[:], EGD[:], AF.Exp)
            if isS:
                S.tt('pool', GBS[:], GG[:, ti, :].to_broadcast([128, 8, 16]),
                     blk16.unsqueeze(1).broadcast_to([128, 8, 16]), ALU.mult)
                egp = pq()
                S.mm(egp, ones_f, GBS[:, :, :].rearrange("p a b -> p (a b)"))
                S.act(EGLS[:, :, :].rearrange("p a b -> p (a b)"), egp, AF.Exp)
            head_sets = [[0], [1], [2], [3]] if isS else [[0, 1, 2, 3]]
            for hls in head_sets:
                gdn_heads(g, l, hb, ti, isS, hls, Umat, SLmat, NEGmat, L, tsl)
        if g == len(GROUPS) - 1 and hb == 1:
            S.dma('sp', nsp[l].rearrange("h k v -> k h v"), S_ST[:, l, :, :])

    def gdn_heads(g, l, hb, ti, isS, hls, Umat, SLmat, NEGmat, L, tsl):
        H = [(hl, hb * 4 + hl) for hl in hls]
        lock4 = (len(hls) == 4)

        def pqx(hl):
            if not lock4:
                return pq()
            q = hq[hl]
            hq[hl] = (q + 1) % 4
            return PB[hl][:, q * 128:(q + 1) * 128]
        dps = {}
        for hl, h in H:
            S.ts('pool', Bm[hl], SLmat, GG[:, ti, h:h + 1], None, ALU.mult)
        pq_align()
        for hl, h in H:
            d = pqx(hl)
            S.mm(d, Bm[hl], Umat, start=True, stop=False)
            S.mm(d, ident, NEGmat, start=False, stop=True)
            dps[hl] = d
        for hl, h in H:
            S.act(DTS[hl], dps[hl], AF.Exp)
        for hl, h in H:
            S.tt('pool', DTM[hl], DTS[hl], ident, ALU.add)
        gks = {}
        pq_align()
        for hl, h in H:
            gk = pqx(hl)
            S.mm(gk, KT4[:, hl, tsl], KT4[:, hl, tsl])
            gks[hl] = gk
        for hl, h in H:
            S.stt(CA[hl][0], gks[hl], BETA[:, ti, h:h + 1], DTS[hl], ALU.mult, ALU.mult)
        pts = {}
        pq_align()
        pqb_align()
        for hl, h in H:
            p_ = pqx(hl) if CHAIN_F32 else pqb()
            if CHAIN_F32:
                S.trf(p_, CA[hl][0], ident_cd)
            else:
                S.tr(p_, CA[hl][0], ident_cd)
            pts[hl] = p_
        for hl, h in H:
            S.cp('act', CAT[hl][0], pts[hl])
            S.tt('pool', CR[hl][0], ident_cd, CA[hl][0], ALU.subtract)
        kts, vts = {}, {}
        pqb_align()
        for hl, h in H:
            kt_ = pqb()
            S.tr(kt_, KT4[:, hl, tsl], IDB[:])
            kts[hl] = kt_
        pqb_align()
        for hl, h in H:
            vt_ = pqb()
            S.tr(vt_, VT4[:, hl, tsl], IDB[:])
            vts[hl] = vt_
        for hl, h in H:
            S.ts('dve', KE[hl], kts[hl], EGC[:, h:h + 1], None, ALU.mult)
            S.ts('dve', KDEC[hl], kts[hl], EGD[:, h:h + 1], None, ALU.mult)
            S.cp('act', VTM[hl], vts[hl])
        for k in range(1, L + 1):
            a_ps, at_ps = {}, {}
            pq_align()
            if k < L:
                for hl, h in H:
                    a = pqx(hl)
                    S.mm(a, CAT[hl][(k - 1) % 2], CA[hl][(k - 1) % 2])
                    a_ps[hl] = a
                pq_align()
            for hl, h in H:
                at = pqx(hl)
                S.mm(at, CA[hl][(k - 1) % 2], CAT[hl][(k - 1) % 2])
                at_ps[hl] = at
            for hl, h in H:
                if k < L:
                    S.cp('act', CA[hl][k % 2], a_ps[hl])
                S.cp('dve', CAT[hl][k % 2], at_ps[hl])
            r_ps = {}
            pq_align()
            for hl, h in H:
                r = pqx(hl)
                S.mm(r, CAT[hl][k % 2], CR[hl][(k - 1) % 2])
                r_ps[hl] = r
            for hl, h in H:
                S.tt('dve', CR[hl][k % 2], r_ps[hl], CR[hl][(k - 1) % 2], ALU.add)
        XT = {hl: CR[hl][L % 2] for hl, h in H}
        wps, ups = {}, {}
        pq_align()
        for hl, h in H:
            w_ = pqx(hl)
            S.mm(w_, KE[hl], XT[hl])
            wps[hl] = w_
            u_ = pqx(hl)
            S.mm(u_, XT[hl], VTM[hl])
            ups[hl] = u_
        for hl, h in H:
            S.cp('act', WTt[hl], wps[hl])
            S.act(UB[hl], ups[hl], AF.Copy, scale=BETA[:, ti, h:h + 1])
        wss = {}
        if not isS:
            for hl, h in H:
                S.cp('pool', SBF[hl], S_ST[:, l, h, :])
            pq_align()
            for hl, h in H:
                ws = pqx(hl)
                S.mm(ws, WTt[hl], SBF[hl])
                wss[hl] = ws
        else:
            for hl, h in H:
                S.dma('sp', S0F, sssm[l, :, h].rearrange("s k v -> k s v"))
                S.dma('pool', S0B, sssm[l, :, h].rearrange("s k v -> k s v"))
                wsT = pqx(hl)
                for s in range(16):
                    S.mm(wsT[:, 8 * s:8 * s + 8], S0B[:, s, :], WTt[hl][:, 8 * s:8 * s + 8])
                S.cp('act', FMS[0], wsT)
                ws = pqx(hl)
                S.trf(ws, FMS[0], ident)
                wss[hl] = ws
        for hl, h in H:
            S.stt(VN[hl], wss[hl], NBETA[:, ti, h:h + 1], UB[hl], ALU.mult, ALU.add)
        aps = {}
        pq_align()
        for hl, h in H:
            a_ = pqx(hl)
            S.mm(a_, KT4[:, hl, tsl], QT4[:, hl, tsl])
            aps[hl] = a_
        for hl, h in H:
            S.tt('dve', ATt[hl], aps[hl], DTM[hl], ALU.mult)
        avs, qss = {}, {}
        pq_align()
        for hl, h in H:
            av = pqx(hl)
            S.mm(av, ATt[hl], VN[hl])
            avs[hl] = av
        pq_align()
        for hl, h in H:
            if not isS:
                qs = pqx(hl)
                S.mm(qs, QT4[:, hl, tsl], SBF[hl])
            else:
                qsT = pqx(hl)
                for s in range(16):
                    S.mm(qsT[:, 8 * s:8 * s + 8], S0B[:, s, :], QT4[:, hl, ti * 128 + 8 * s:ti * 128 + 8 * s + 8])
                S.cp('act', FMS[1], qsT)
                qs = pqx(hl)
                S.trf(qs, FMS[1], ident)
            qss[hl] = qs
        for hl, h in H:
            S.act(O1[hl], qss[hl], AF.Copy, scale=EGC[:, h:h + 1])
            S.tt('dve', OO[hl], avs[hl], O1[hl], ALU.add)
        for hl, h in H:
            S.act(JG[hl], OO[hl], AF.Square, accum_out=STAT[:, 16 + hl:17 + hl])
        for hl, h in H:
            rstd_from(STAT[:, 24 + hl:25 + hl], STAT[:, 16 + hl:17 + hl], 128)
        dts = {}
        pqb_align()
        for hl, h in H:
            S.stt(DD[hl], OO[hl], STAT[:, 24 + hl:25 + hl], ZS4[:, ti, hl * 128:(hl + 1) * 128], ALU.mult, ALU.mult)
            d_ = pqb()
            S.tr(d_, DD[hl], IDB[:])
            dts[hl] = d_
        for hl, h in H:
            S.cp('act', DOT[:, h, tsl], dts[hl])
        if not isS:
            dss = {}
            pq_align()
            for hl, h in H:
                ds = pqx(hl)
                S.mm(ds, KDEC[hl], VN[hl])
                dss[hl] = ds
            for hl, h in H:
                S.stt(S_ST[:, l, h, :], S_ST[:, l, h, :], EGL[:, h:h + 1], dss[hl], ALU.mult, ALU.add)
        else:
            for hl, h in H:
                S.tt('pool', VBLK, VN[hl].unsqueeze(1).broadcast_to([128, 16, 128]),
                     blk16.to_broadcast([128, 16, 128]), ALU.mult)
                S.tt('pool', S0F, S0F, EGLS[:, h, :].to_broadcast([128, 16, 128]), ALU.mult)
                for q4 in range(4):
                    bk = pbank()
                    S.mm(bk[:, :], KDEC[hl], VBLK[:, 4 * q4:4 * q4 + 4, :].rearrange("p a b -> p (a b)"))
                    sl = S0F[:, 4 * q4:4 * q4 + 4, :].rearrange("p a b -> p (a b)")
                    S.tt('dve', sl, bk[:, :], sl, ALU.add)
                S.dma('sp', nss[l, :, h].rearrange("s k v -> k s v"), S0F)

    def stage_merge(g, l):
        for q in range(4):
            Wg = wget(widx[(g, l, 'mg', q)], min(PREF, 4))
            Wb = wget(widx[(g, l, 'mb', q)], min(PREF, 3))
            for dl in range(2):
                dcn = q * 2 + dl
                for c in range(NCH):
                    i0 = (dl * NCH + c) % 2
                    gp = fm_proj(Wg, dl * 128, HT, 8, c)
                    gd = fm_proj(Wg, 256 + dl * 128, HT, 8, c)
                    brp = fm_proj(Wb, dl * 128, AOT, 4, c)
                    brd = fm_proj(Wb, 256 + dl * 128, DOT, 8, c)
                    S.act(SGT[2 * i0], gp, AF.Sigmoid)
                    S.act(SGT[2 * i0 + 1], gd, AF.Sigmoid)
                    S.tt('dve', T12[2 * i0], brp, SGT[2 * i0], ALU.mult)
                    S.tt('dve', T12[2 * i0 + 1], brd, SGT[2 * i0 + 1], ALU.mult)
                    S.tt('pool', MT[:, dcn, c * CH:(c + 1) * CH], T12[2 * i0], T12[2 * i0 + 1], ALU.add)

    def post_tile(l, which, ti):
        ssq = STAT[:, 48 + ti:49 + ti]
        rs = STAT[:, 56 + ti:57 + ti]
        S.act(JUNK, FF[:, ti, :], AF.Square, accum_out=ssq)
        rstd_from(rs, ssq, D)
        for half in range(2):
            cs = slice(half * 512, (half + 1) * 512)
            S.stt(TMPX[half], FF[:, ti, cs], rs, GPOSTb[:, cs], ALU.mult, ALU.mult)
            S.tt('pool', X[:, ti, cs], X[:, ti, cs], TMPX[half], ALU.add)

    def stage_out(g, l):
        S.dma('sp', GPOSTb, gpost_d[:, l, 0, :])
        for half in range(2):
            Wt = wget(widx[(g, l, 'wo', half)])
            for ti in range(TPG):
                acc = tm_proj(Wt, ti, MT)
                S.cp('act', FF[:, ti, half * 512:(half + 1) * 512], acc)
                if half == 1:
                    post_tile(l, 0, ti)

    def stage_ffn(g, l):
        norm_to_HT(l, 1)
        S.dma('sp', GPOSTb, gpost_d[:, l, 1, :])
        for fh in range(2):
            for f4 in range(4):
                Wt = wget(widx[(g, l, 'up', fh * 4 + f4)])
                for fl in range(4):
                    fi = f4 * 4 + fl
                    for c in range(NCH):
                        acc = fm_proj(Wt, fl * 128, HT, 8, c)
                        rl = RL[(fl * NCH + c) % 2]
                        S.act(rl, acc, AF.Relu)
                        S.tt('pool', UT[:, fi, c * CH:(c + 1) * CH], rl, rl, ALU.mult)
            for half in range(2):
                for kl in range(2):
                    Wt = wget(widx[(g, l, 'dn', half, fh * 2 + kl)])
                    for ti in range(TPG):
                        for kc in range(8):
                            S.mm(PB[ti][:, :], UT[:, kl * 8 + kc, ti * 128:(ti + 1) * 128], Wt[:, kc, :],
                                 start=(kl == 0 and kc == 0), stop=(kl == 1 and kc == 7))
                for ti in range(TPG):
                    dst = FF[:, ti, half * 512:(half + 1) * 512]
                    if fh == 0:
                        S.cp('act', dst, PB[ti][:, :])
                    else:
                        S.tt('dve', dst, PB[ti][:, :], dst, ALU.add)
                        if half == 1:
                            post_tile(l, 1, ti)

    dbg = DEBUG
    def on(name):
        return dbg is None or name in dbg['stages']
    if dbg is None:
        wissue(PREF)
    for g in range(len(GROUPS)):
        if dbg is not None and g not in dbg['groups']:
            continue
        tiles, npt, NW, has_s = group_info(g)
        load_x(g)
        for l in range(NL):
            if dbg is not None and l not in dbg['layers']:
                continue
            if on('norm'):
                norm_to_HT(l, 0)
            if on('pool'):
                stage_pool(g, l)
            if on('ba'):
                stage_ba(g, l)
            if has_s and on('qkv'):
                stage_sconv_prep(l)
            for hb in range(2):
                if on('qkv'):
                    stage_qkv(g, l, hb)
                if on('z'):
                    stage_z(g, l, hb)
                if on('gdn'):
                    stage_gdn(g, l, hb)
            if on('merge'):
                stage_merge(g, l)
            if on('out'):
                stage_out(g, l)
            if on('ffn'):
                stage_ffn(g, l)
        store_x(g)
    if dbg is None:
        assert wuse[0] == len(wlist)
    S.emit()
    return nc, S


_CACHE = {}


def kernel(x_prompt, x_sample, state_conv, state_ssm, state_pool, meta_tokens, g_pre_mix, w_in, conv_w, a_log,
           dt_bias, o_norm_g, pool_w, pool_scale, w_branch_pool, w_branch_delta, w_out, g_post_mix, g_pre_ffn,
           w_up, w_down, g_post_ffn):
    f = lambda a: np.ascontiguousarray(np.asarray(a, dtype=np.float32))
    if 'nc' not in _CACHE:
        _CACHE['nc'] = build_program()[0]
    nc = _CACHE['nc']
    x_prompt, x_sample, state_conv, state_ssm, state_pool = map(f, (x_prompt, x_sample, state_conv, state_ssm, state_pool))
    gpre = np.stack([f(g_pre_mix), f(g_pre_ffn)], axis=1)
    gpre = f(gpre.reshape(NL, 2, 8, 128).transpose(3, 0, 1, 2))
    cw = f(f(conv_w).reshape(NL, 4, 24, 128).transpose(3, 0, 2, 1))
    psc = f(f(pool_scale).reshape(NL, 4, 128).transpose(2, 0, 1))
    gpost = np.stack([f(g_post_mix), f(g_post_ffn)], axis=1)
    gpost = f(np.broadcast_to(gpost[None], (128, NL, 2, D)))
    ong = f(np.broadcast_to(f(o_norm_g)[None], (128, NL, 128)))
    alog = f(np.broadcast_to(f(a_log)[None], (128, NL, 8)))
    dtb = f(np.broadcast_to(f(dt_bias)[None], (128, NL, 8)))
    shared = {
        "meta": f(meta_tokens), "w_in": f(w_in), "pool_w": f(pool_w), "wbp": f(w_branch_pool),
        "wbd": f(w_branch_delta), "wo": f(w_out), "wup": f(w_up), "wdn": f(w_down),
        "consts": make_consts(), "gpre": gpre, "cw": cw, "pscale": psc, "gpost": gpost, "ong": ong,
        "alog": alog, "dtb": dtb,
    }
    in_maps = []
    for c in range(8):
        m = dict(shared)
        m["xp"] = x_prompt[c]
        m["xs"] = f(x_sample[16 * c:16 * c + 16].reshape(128, D))
        m["sconv"] = f(state_conv[:, 16 * c:16 * c + 16])
        m["sssm"] = f(state_ssm[:, 16 * c:16 * c + 16])
        m["spool"] = f(state_pool[:, 16 * c:16 * c + 16])
        m["spoolT"] = f(m["spool"].reshape(NL, 16, 15, 4, 128).transpose(0, 4, 3, 1, 2))
        m["sconvT"] = f(m["sconv"].reshape(NL, 16, 3, 24, 128).transpose(0, 4, 3, 1, 2))
        in_maps.append(m)
    res = run_bass_kernel_spmd(nc, in_maps, core_ids=list(range(8)))
    R = res.results
    y_prompt = np.stack([R[c]["y_p"] for c in range(8)], axis=0)
    y_sample = np.concatenate([R[c]["y_s"].reshape(16, 8, D) for c in range(8)], axis=0)
    ncp = np.stack([R[c]["ncp"] for c in range(8)], axis=1)
    nsp = np.stack([R[c]["nsp"] for c in range(8)], axis=1)
    npp = np.stack([R[c]["npp"] for c in range(8)], axis=1)
    ncs = np.concatenate([R[c]["ncs"] for c in range(8)], axis=1)
    nss = np.concatenate([R[c]["nss"] for c in range(8)], axis=1)
    nps = np.concatenate([R[c]["nps"] for c in range(8)], axis=1)
    return tuple(np.ascontiguousarray(a, dtype=np.float32) for a in (y_prompt, y_sample, ncp, nsp, npp, ncs, nss, nps))
```

```python
import numpy as np
import concourse.bass as bass
import concourse.mybir as mybir
from concourse.bass_utils import run_bass_kernel_spmd

F32 = mybir.dt.float32
BF16 = mybir.dt.bfloat16
AF = mybir.ActivationFunctionType
ALU = mybir.AluOpType

ENGS = ('pe', 'act', 'dve', 'pool', 'sp')
MAXOPS = None


class _Op:
    __slots__ = ('eng', 'fn', 'waits', 'signal', 'dma', 'idx')

    def __init__(self, eng, fn, dma=None):
        self.eng = eng
        self.fn = fn
        self.waits = []
        self.signal = False
        self.dma = dma
        self.idx = 0


class Sched:
    NDMA = 24

    def __init__(self, nc):
        self.nc = nc
        self.ops = {e: [] for e in ENGS}
        self.recs = {}
        self.water = {e: {} for e in ENGS}
        self.dma_tot = [0] * self.NDMA
        self.dma_last = [None] * self.NDMA
        self.dma_rr = 0
        self.dma_rr_sw = 0
        self.tensors = []
        self.same_eng_dist = 1 << 30

    def sbuf(self, name, shape, dtype):
        t = self.nc.alloc_sbuf_tensor(name, list(shape), dtype)
        return t

    def psum(self, name, shape, dtype):
        return self.nc.alloc_psum_tensor(name, list(shape), dtype)

    @staticmethod
    def _box(ap):
        t = ap.tensor
        shp = list(t.shape)
        row = 1
        for s in shp[1:]:
            row *= s
        dsz = mybir.dt.size(ap.dtype)
        pat = list(ap.ap)
        off = ap.offset
        p0 = off // row
        f0 = (off % row) * dsz
        pstep, pcnt = pat[0]
        if pstep == 0:
            np_ = 1
        else:
            np_ = (pcnt - 1) * (pstep // row) + 1
        ext = 0
        for st, cnt in pat[1:]:
            ext += (cnt - 1) * abs(st)
        if type(t).__name__ == 'PSumTensorHandle':
            return (p0, p0 + np_, 0, 1 << 20)
        return (p0, p0 + np_, f0, f0 + (ext + 1) * dsz)

    def _track(self, op, aps_r, aps_w):
        deps = set()
        for kind, aps in (('r', aps_r), ('w', aps_w)):
            for ap in aps:
                name = ap.tensor.name
                box = self._box(ap)
                if type(ap.tensor).__name__ == 'PSumTensorHandle':
                    kind = 'w'
                recs = self.recs.setdefault(name, {})
                dead = []
                for (b, k, key), tok in recs.items():
                    if b[0] < box[1] and box[0] < b[1] and b[2] < box[3] and box[2] < b[3]:
                        if kind == 'w' or k == 'w':
                            deps.add(tok)
                        if kind == 'w' and box[0] <= b[0] and b[1] <= box[1] and box[2] <= b[2] and b[3] <= box[3]:
                            dead.append((b, k, key))
                for d in dead:
                    del recs[d]
        return deps

    def _record(self, op, tok, aps_r, aps_w):
        for kind, aps in (('r', aps_r), ('w', aps_w)):
            for ap in aps:
                name = ap.tensor.name
                box = self._box(ap)
                if type(ap.tensor).__name__ == 'PSumTensorHandle':
                    kind = 'w'
                key = tok[0] if tok[0] != 'dma' else ('dma', tok[1])
                self.recs.setdefault(name, {})[(box, kind, key)] = tok

    def _add_waits(self, op, deps):
        eng = op.eng
        my_idx = len(self.ops[eng])
        for tok in deps:
            if tok[0] == 'dma':
                _, s, val = tok
                key = ('dma', s)
                if self.water[eng].get(key, 0) >= val:
                    continue
                self.water[eng][key] = val
                op.waits.append(tok)
            else:
                f, i = tok
                if f == eng:
                    if eng in ('pe', 'sp'):
                        continue
                    if my_idx - i > self.same_eng_dist:
                        continue
                if self.water[eng].get(f, -1) >= i:
                    continue
                self.water[eng][f] = i
                self.ops[f][i].signal = True
                op.waits.append(tok)

    @staticmethod
    def _is_onchip(ap):
        return type(ap.tensor).__name__ in ('SBTensorHandle', 'PSumTensorHandle')

    def op(self, eng, fn, r, w):
        self.nrec = getattr(self, 'nrec', 0) + 1
        if MAXOPS is not None and self.nrec > MAXOPS:
            return None
        o = _Op(eng, fn)
        r = [a for a in r if self._is_onchip(a)]
        w = [a for a in w if self._is_onchip(a)]
        deps = self._track(o, r, w)
        self._add_waits(o, deps)
        o.idx = len(self.ops[eng])
        self.ops[eng].append(o)
        self._record(o, (eng, o.idx), r, w)
        return o

    def dma(self, queue, out, in_):
        self.nrec = getattr(self, 'nrec', 0) + 1
        if MAXOPS is not None and self.nrec > MAXOPS:
            return None
        if queue == 'pool':
            s = 16 + self.dma_rr_sw
            self.dma_rr_sw = (self.dma_rr_sw + 1) % 8
        else:
            s = self.dma_rr
            self.dma_rr = (self.dma_rr + 1) % 16
        o = _Op(queue, None, dma=(s, out, in_))
        r = [in_] if self._is_onchip(in_) else []
        w = [out] if self._is_onchip(out) else []
        deps = self._track(o, r, w)
        if self.dma_last[s] is not None:
            deps.add(self.dma_last[s])
        self._add_waits(o, deps)
        self.dma_tot[s] += 16
        tok = ('dma', s, self.dma_tot[s])
        self.dma_last[s] = tok
        o.idx = len(self.ops[queue])
        self.ops[queue].append(o)
        self._record(o, tok, r, w)
        return tok

    def mm(self, out, lhsT, rhs, start=True, stop=True):
        return self.op('pe', lambda e: e.matmul(out, lhsT, rhs, start=start, stop=stop), [lhsT, rhs], [out])

    def tr(self, out, in_, ident):
        return self.op('pe', lambda e: e.transpose(out, in_, ident), [in_, ident], [out])

    def trf(self, out, in_, ident):
        return self.op('pe', lambda e: e.matmul(out, in_, ident, start=True, stop=True), [in_, ident], [out])

    def act(self, out, in_, func, bias=None, scale=None, accum_out=None, eng='act'):
        kw = {}
        r = [in_]
        w = [out]
        if bias is not None:
            kw['bias'] = bias
            if not isinstance(bias, (int, float)):
                r.append(bias)
        if scale is not None:
            kw['scale'] = scale
            if not isinstance(scale, (int, float)):
                r.append(scale)
        if accum_out is not None:
            kw['accum_out'] = accum_out
            w.append(accum_out)
        return self.op('act', lambda e: e.activation(out, in_, func, **kw), r, w)

    def tt(self, eng, out, in0, in1, op):
        return self.op(eng, lambda e: e.tensor_tensor(out, in0, in1, op), [in0, in1], [out])

    def ts(self, eng, out, in0, s1, s2, op0, op1=None, accum_out=None):
        r = [in0]
        if not isinstance(s1, (int, float)) and s1 is not None:
            r.append(s1)
        if not isinstance(s2, (int, float)) and s2 is not None:
            r.append(s2)
        w = [out]
        kw = {}
        if op1 is not None:
            kw['op1'] = op1
        if accum_out is not None:
            kw['accum_out'] = accum_out
            w.append(accum_out)
        return self.op(eng, lambda e: e.tensor_scalar(out, in0, s1, s2, op0, **kw), r, w)

    def stt(self, out, in0, scalar, in1, op0, op1, eng='dve'):
        r = [in0, in1]
        if not isinstance(scalar, (int, float)):
            r.append(scalar)
        return self.op(eng, lambda e: e.scalar_tensor_tensor(out, in0, scalar, in1, op0, op1), r, [out])

    def cp(self, eng, out, in_):
        if eng == 'act':
            return self.op('act', lambda e: e.copy(out, in_), [in_], [out])
        return self.op(eng, lambda e: e.tensor_copy(out, in_), [in_], [out])

    def memset(self, eng, ap, val):
        return self.op(eng, lambda e: e.memset(ap, val), [], [ap])

    def emit(self, final_wait=True):
        nc = self.nc
        engobj = {'pe': nc.tensor, 'act': nc.scalar, 'dve': nc.vector, 'pool': nc.gpsimd, 'sp': nc.sync}
        sems = {e: nc.alloc_semaphore("prog_" + e) for e in ENGS}
        dsems = [nc.alloc_semaphore("dma_%d" % i) for i in range(self.NDMA)]
        cnt = {}
        for e in ENGS:
            c = 0
            arr = []
            for o in self.ops[e]:
                if o.signal and o.dma is None:
                    c += 1
                arr.append(c)
            cnt[e] = arr
        blockattr = {'pe': 'tensor', 'act': 'scalar', 'dve': 'vector', 'pool': 'gpsimd', 'sp': 'sync'}
        with nc.Block() as block:
            for e in ENGS:
                ops = self.ops[e]

                def body(eng, e=e, ops=ops):
                    for o in ops:
                        for tok in o.waits:
                            if tok[0] == 'dma':
                                eng.wait_ge(dsems[tok[1]], tok[2])
                            else:
                                eng.wait_ge(sems[tok[0]], cnt[tok[0]][tok[1]])
                        if o.dma is not None:
                            s, out, in_ = o.dma
                            eng.dma_start(out=out, in_=in_).then_inc(dsems[s], 16)
                        else:
                            ins = o.fn(eng)
                            if o.signal:
                                ins.then_inc(sems[e], 1)
                    if e == 'sp' and final_wait:
                        for s in range(self.NDMA):
                            if self.dma_tot[s] > 0:
                                eng.wait_ge(dsems[s], self.dma_tot[s])
                getattr(block, blockattr[e])(body)


D = 1024
NH = 8
IN_W = 6672
DFF = 4096
NL = 2
TPG = 6
NT = TPG * 128
CH = 384
NCH = NT // CH
EPS = 1e-6
POOL_W = (2, 4, 8, 16)
C_ID, C_UP, C_US, C_SLP, C_SLS, C_SAMES, C_ONES, C_NEGP, C_NEGS, C_BLK, C_INVC = (
    0, 128, 256, 384, 512, 640, 768, 896, 1024, 1152, 1168)
NCONST = 1232
GROUPS = [[('P', i) for i in range(0, 6)], [('P', i) for i in range(6, 12)],
          [('P', i) for i in range(12, 17)] + [('S', 0)]]
CHAIN_F32 = True
DEBUG = None


def make_consts():
    c = np.zeros((128, NCONST), np.float32)
    p = np.arange(128)
    same = (p[:, None] // 8) == (p[None, :] // 8)
    c[:, C_ID:C_ID + 128] = np.eye(128)
    c[:, C_UP:C_UP + 128] = (p[:, None] <= p[None, :])
    c[:, C_US:C_US + 128] = (p[:, None] <= p[None, :]) & same
    c[:, C_SLP:C_SLP + 128] = (p[:, None] > p[None, :])
    c[:, C_SLS:C_SLS + 128] = (p[:, None] > p[None, :]) & same
    c[:, C_SAMES:C_SAMES + 128] = same
    c[:, C_ONES:C_ONES + 128] = 1.0
    c[:, C_NEGP:C_NEGP + 128] = np.where(p[:, None] < p[None, :], 0.0, -30000.0)
    c[:, C_NEGS:C_NEGS + 128] = np.where((p[:, None] < p[None, :]) & same, 0.0, -30000.0)
    c[:, C_BLK:C_BLK + 16] = (p[:, None] // 8) == np.arange(16)[None, :]
    for gi, w in enumerate(POOL_W):
        for pos in range(16):
            c[:, C_INVC + gi * 16 + pos] = 1.0 / min(pos + 1, w)
    return c


def build_program():
    nc = bass.Bass("TRN2", target_bir_lowering=False)

    def din(name, shape):
        return nc.dram_tensor(name, list(shape), F32, kind="ExternalInput").ap()

    def dout(name, shape):
        return nc.dram_tensor(name, list(shape), F32, kind="ExternalOutput").ap()

    xp = din("xp", [2048, D]); xs = din("xs", [128, D]); meta = din("meta", [16, D])
    sconv = din("sconv", [NL, 16, 3, 3072]); sssm = din("sssm", [NL, 16, NH, 128, 128])
    spool = din("spool", [NL, 16, 15, 512])
    spoolT = din("spoolT", [NL, 128, 4, 16, 15]); sconvT = din("sconvT", [NL, 128, 24, 16, 3])
    w_in = din("w_in", [NL, D, IN_W]); pool_w = din("pool_w", [NL, 4, 128, 128])
    wbp = din("wbp", [NL, 512, D]); wbd = din("wbd", [NL, D, D]); wo = din("wo", [NL, D, D])
    wup = din("wup", [NL, D, DFF]); wdn = din("wdn", [NL, DFF, D])
    consts = din("consts", [128, NCONST]); gpre_d = din("gpre", [128, NL, 2, 8])
    cw_d = din("cw", [128, NL, 24, 4]); pscale_d = din("pscale", [128, NL, 4])
    gpost_d = din("gpost", [128, NL, 2, D]); ong_d = din("ong", [128, NL, 128])
    alog_d = din("alog", [128, NL, 8]); dtb_d = din("dtb", [128, NL, 8])
    y_p = dout("y_p", [2048, D]); y_s = dout("y_s", [128, D])
    ncp = dout("ncp", [NL, 3, 3072]); nsp = dout("nsp", [NL, NH, 128, 128]); npp = dout("npp", [NL, 15, 512])
    ncs = dout("ncs", [NL, 16, 3, 3072]); nss = dout("nss", [NL, 16, NH, 128, 128]); nps = dout("nps", [NL, 16, 15, 512])

    S = Sched(nc)
    CD = F32 if CHAIN_F32 else BF16

    X = S.sbuf("X", [128, TPG, D], F32)
    HT = S.sbuf("HT", [128, 8, NT], BF16)
    BIG = S.sbuf("BIG", [128, 16 * NT], BF16)
    UT = BIG[:, :].rearrange("p (a b) -> p a b", b=NT)
    QT4 = BIG[:, 0:4 * NT].rearrange("p (a b) -> p a b", b=NT)
    KT4 = BIG[:, 4 * NT:8 * NT].rearrange("p (a b) -> p a b", b=NT)
    VT4 = BIG[:, 8 * NT:12 * NT].rearrange("p (a b) -> p a b", b=NT)
    ZS4 = BIG[:, 12 * NT:16 * NT].rearrange("p (t c) -> p t c", c=512)
    MT = BIG[:, 0:8 * NT].rearrange("p (a b) -> p a b", b=NT)
    DM = S.sbuf("DM", [128, 16 * NT], BF16)
    DOT = DM[:, 0:8 * NT].rearrange("p (a b) -> p a b", b=NT)
    AOT = DM[:, 8 * NT:12 * NT].rearrange("p (a b) -> p a b", b=NT)
    FF = DM[:, :].bitcast(F32).rearrange("p (t c) -> p t c", c=D)
    NSLOT = 5
    PREF = 4
    WR = [S.sbuf("WR%d" % i, [128, 8, 512], BF16) for i in range(NSLOT)]
    CST = S.sbuf("CST", [128, NCONST], F32)
    IDB = S.sbuf("IDB", [128, 128], BF16)
    GPRE = S.sbuf("GPRE", [128, NL, 2, 8], F32)
    CW = S.sbuf("CW", [128, NL, 24, 4], F32)
    PSC = S.sbuf("PSC", [128, NL, 4], F32)
    ONG = S.sbuf("ONG", [128, NL, 128], F32)
    NEGA = S.sbuf("NEGA", [128, NL, 8], F32)
    DTB = S.sbuf("DTB", [128, NL, 8], F32)
    EPSC = S.sbuf("EPSC", [128, 2], F32)
    S_ST = S.sbuf("S_ST", [128, NL, NH, 128], F32)
    CPRE = S.sbuf("CPRE", [128, NL, 24, 3], F32)
    PPRE = S.sbuf("PPRE", [128, NL, 4, 15], F32)
    PWL2 = S.sbuf("PWL2", [128, NL, 4, 128], BF16)
    WBA2 = S.sbuf("WBA2", [128, NL, 8, 16], BF16)
    STAT = S.sbuf("STAT", [128, 64], F32)
    BETA = S.sbuf("BETA", [128, TPG, 8], F32)
    NBETA = S.sbuf("NBETA", [128, TPG, 8], F32)
    GG = S.sbuf("GG", [128, TPG, 8], F32)
    TA = S.sbuf("TA", [128, 8], F32)
    EGC = S.sbuf("EGC", [128, 8], F32)
    EGD = S.sbuf("EGD", [128, 8], F32)
    EGL = S.sbuf("EGL", [128, 8], F32)
    GCs = S.sbuf("GCs", [128, 8], F32)
    GBS = S.sbuf("GBS", [128, 8, 16], F32)
    EGLS = S.sbuf("EGLS", [128, 8, 16], F32)
    SPOOLT = S.sbuf("SPOOLT", [128, 4, 16, 15], F32)
    SCONVT = S.sbuf("SCONVT", [128, 24, 16, 3], F32)

    SCR_BYTES = 52 * 1024
    SCR = S.sbuf("SCR", [128, SCR_BYTES // 2], BF16)

    class Arena:
        def __init__(self):
            self.off = 0

        def take(self, free, dtype):
            n = 1
            for f in free:
                n *= f
            nb = n * mybir.dt.size(dtype)
            nb_al = (nb + 31) // 32 * 32
            assert self.off + nb_al <= SCR_BYTES, ("arena overflow", self.off, nb_al)
            ap = SCR[:, self.off // 2:(self.off + nb) // 2]
            self.off += nb_al
            if dtype != BF16:
                ap = ap.bitcast(dtype)
            if len(free) == 2:
                ap = ap.rearrange("p (a b) -> p a b", b=free[1])
            elif len(free) == 3:
                ap = ap.rearrange("p (a b c) -> p a b c", b=free[1], c=free[2])
            return ap

    EXTW = 16 + NT
    A = Arena()
    JUNK = A.take([D], BF16)
    HB = [A.take([D], BF16) for _ in range(2)]
    offA0 = A.off
    UEXT = [A.take([EXTW], F32) for _ in range(2)]
    SCEXT = A.take([16, 11], F32)
    TMROW = A.take([512], F32)
    offA1 = A.off
    PA = A.take([EXTW], F32)
    PBf = A.take([EXTW], F32)
    MTP = A.take([NT], BF16)
    SEXT = A.take([16, 23], F32)
    SPA = A.take([16, 23], F32)
    SPB = A.take([16, 23], F32)
    A.off = offA1
    CQ2 = [A.take([NT], F32) for _ in range(2)]
    SQ2_2 = [A.take([NT], F32) for _ in range(2)]
    SQ_4 = [A.take([NT], F32) for _ in range(4)]
    RN_4 = [A.take([NT], F32) for _ in range(4)]
    A.off = offA0
    GPOSTb = A.take([D], F32)
    TMPX = [A.take([512], F32) for _ in range(2)]
    SGT = [A.take([CH], F32) for _ in range(4)]
    T12 = [A.take([CH], F32) for _ in range(4)]
    RL = [A.take([CH], F32) for _ in range(2)]
    G = Arena()
    NB = 4
    Bm = [G.take([128], F32) for _ in range(NB)]
    DTS = [G.take([128], F32) for _ in range(NB)]
    DTM = [G.take([128], F32) for _ in range(NB)]
    CA = [[G.take([128], CD) for _ in range(2)] for _ in range(NB)]
    CAT = [[G.take([128], CD) for _ in range(2)] for _ in range(NB)]
    CR = [[G.take([128], CD) for _ in range(2)] for _ in range(NB)]
    KE = [G.take([128], CD) for _ in range(NB)]
    KDEC = [G.take([128], BF16) for _ in range(NB)]
    VTM = [G.take([128], CD) for _ in range(NB)]
    WTt = [G.take([128], BF16) for _ in range(NB)]
    UB = [G.take([128], F32) for _ in range(NB)]
    VN = [G.take([128], BF16) for _ in range(NB)]
    ATt = [G.take([128], BF16) for _ in range(NB)]
    O1 = [G.take([128], F32) for _ in range(NB)]
    OO = [G.take([128], F32) for _ in range(NB)]
    DD = [G.take([128], BF16) for _ in range(NB)]
    SBF = [G.take([128], BF16) for _ in range(NB)]
    JG = [G.take([128], BF16) for _ in range(NB)]
    FMS = [G.take([128], F32) for _ in range(2)]
    S0F = G.take([16, 128], F32)
    S0B = G.take([16, 128], BF16)
    VBLK = G.take([16, 128], BF16)

    PB = [S.psum("PB%d" % i, [128, 512], F32) for i in range(6)]
    PH = [S.psum("PH%d" % i, [128, 1024], BF16) for i in range(2)]
    qctr = [0]

    def pq_align():
        qctr[0] = (qctr[0] + 3) // 4 * 4 % 8

    def pq():
        i = qctr[0]
        qctr[0] = (i + 1) % 8
        return PB[4 + i // 4][:, (i % 4) * 128:(i % 4 + 1) * 128]
    bctr = [0]

    def pbank():
        i = bctr[0]
        bctr[0] = (i + 1) % 6
        return PB[i]
    hctr = [0]
    hq = [0, 0, 0, 0]

    def pqb_align():
        hctr[0] = (hctr[0] + 3) // 4 * 4 % 16

    def pqb():
        i = hctr[0]
        hctr[0] = (i + 1) % 16
        slot = (i % 4) + 4 * ((i // 8) % 2)
        return PH[(i // 4) % 2][:, slot * 128:(slot + 1) * 128]

    ident = CST[:, C_ID:C_ID + 128]
    ident_cd = ident if CHAIN_F32 else IDB[:, :]
    ones_f = CST[:, C_ONES:C_ONES + 128]
    blk16 = CST[:, C_BLK:C_BLK + 16]
    eps_c = EPSC[:, 0:1]
    one_c = EPSC[:, 1:2]

    wlist = []

    def wsrc_k8(wt, l, c0, ncols=512):
        return wt[l, :, c0:c0 + ncols].rearrange("(dc p) c -> p dc c", p=128)

    wstate = {'issued': 0}

    def wdma(n):
        for (src, kc, coff, ncol) in wlist[n]:
            S.dma('pool', WR[n % NSLOT][:, 0:kc, coff:coff + ncol], src)

    def wissue(upto):
        while wstate['issued'] <= upto and wstate['issued'] < len(wlist):
            wdma(wstate['issued'])
            wstate['issued'] += 1

    wuse = [0]

    def wget(n, pref=None):
        pref = PREF if pref is None else pref
        if DEBUG is None:
            assert n == wuse[0], ("weight order mismatch", n, wuse[0])
        wuse[0] += 1
        if DEBUG is None:
            wissue(n + pref)
        else:
            wdma(n)
        return WR[n % NSLOT]

    widx = {}
    for g in range(len(GROUPS)):
        for l in range(NL):
            def add(key, src, kc=8, ncol=512):
                widx[(g, l) + key] = len(wlist)
                wlist.append([(src, kc, 0, ncol)])

            def add2(key, parts):
                widx[(g, l) + key] = len(wlist)
                wlist.append(parts)
            add(('u',), wsrc_k8(w_in, l, 0))
            for hb in range(2):
                for comp in range(3):
                    add(('qkv', comp, hb), wsrc_k8(w_in, l, 512 + comp * 1024 + hb * 512))
                add(('z', hb), wsrc_k8(w_in, l, 3600 + hb * 512))
            for q in range(4):
                add2(('mg', q), [(wsrc_k8(w_in, l, 4624 + q * 256, 256), 8, 0, 256),
                                 (wsrc_k8(w_in, l, 5648 + q * 256, 256), 8, 256, 256)])
                add2(('mb', q), [(wbp[l, :, q * 256:(q + 1) * 256].rearrange("(dc p) c -> p dc c", p=128), 4, 0, 256),
                                 (wsrc_k8(wbd, l, q * 256, 256), 8, 256, 256)])
            for half in range(2):
                add(('wo', half), wsrc_k8(wo, l, half * 512))
            for fh in range(2):
                for f4 in range(4):
                    add(('up', fh * 4 + f4), wsrc_k8(wup, l, (fh * 4 + f4) * 512))
                for half in range(2):
                    for kl in range(2):
                        kcg = fh * 2 + kl
                        add(('dn', half, kcg),
                            wdn[l, kcg * 1024:(kcg + 1) * 1024, half * 512:(half + 1) * 512].rearrange("(dc p) c -> p dc c", p=128))

    S.dma('sp', CST[:], consts)
    for l_ in range(NL):
        S.dma('pool', PWL2[:, l_, :, :], pool_w[l_].rearrange("g c d -> c g d"))
        S.dma('pool', WBA2[:, l_, :, :], w_in[l_, :, 3584:3600].rearrange("(dc p) c -> p dc c", p=128))
    S.dma('sp', GPRE[:], gpre_d)
    S.dma('sp', CW[:], cw_d)
    S.dma('sp', PSC[:], pscale_d)
    S.dma('sp', ONG[:], ong_d)
    S.dma('sp', NEGA[:], alog_d)
    S.dma('sp', DTB[:], dtb_d)
    S.cp('dve', IDB[:], ident)
    S.memset('dve', EPSC[:, 0:1], EPS)
    S.memset('dve', EPSC[:, 1:2], 1.0)
    S.act(NEGA[:], NEGA[:], AF.Exp)
    S.ts('pool', NEGA[:], NEGA[:], -1.0, None, ALU.mult)
    S.memset('pool', S_ST[:], 0.0)
    S.memset('pool', CPRE[:], 0.0)
    S.memset('pool', PPRE[:], 0.0)

    def rstd_from(out, ssq, n):
        S.act(out, ssq, AF.Ln, bias=eps_c, scale=1.0 / n)
        S.act(out, out, AF.Exp, scale=-0.5)

    def group_info(g):
        tiles = GROUPS[g]
        npt = sum(1 for t in tiles if t[0] == 'P')
        has_s = any(t[0] == 'S' for t in tiles)
        return tiles, npt, npt * 128, has_s

    def chunk_ranges(c, NW, has_s):
        a, b = c * CH, min((c + 1) * CH, NW)
        pr = (a, b) if b > a else None
        sr = has_s and (c + 1) * CH == NT
        return pr, sr

    def load_x(g):
        tiles, npt, NW, has_s = group_info(g)
        ti = 0
        if tiles[0] == ('P', 0):
            S.memset('pool', X[:, 0, :], 0.0)
            S.dma('sp', X[112:128, 0, :], meta)
            ti = 1
        if npt > ti:
            pt0 = tiles[ti][1]
            S.dma('sp', X[:, ti:npt, :], xp[(pt0 - 1) * 128:(pt0 - 1 + npt - ti) * 128, :].rearrange("(t p) c -> p t c", p=128))
        if has_s:
            S.dma('sp', X[:, TPG - 1, :], xs)

    def store_x(g):
        tiles, npt, NW, has_s = group_info(g)
        ti = 1 if tiles[0] == ('P', 0) else 0
        if npt > ti:
            pt0 = tiles[ti][1]
            S.dma('sp', y_p[(pt0 - 1) * 128:(pt0 - 1 + npt - ti) * 128, :].rearrange("(t p) c -> p t c", p=128), X[:, ti:npt, :])
        if has_s:
            S.dma('sp', y_s, X[:, TPG - 1, :])

    def norm_to_HT(l, which):
        for ti in range(TPG):
            ssq = STAT[:, ti:ti + 1]
            rs = STAT[:, 8 + ti:9 + ti]
            S.act(JUNK, X[:, ti, :], AF.Square, accum_out=ssq)
            rstd_from(rs, ssq, D)
            hb = HB[ti % 2]
            S.ts('dve', hb, X[:, ti, :], rs, None, ALU.mult)
            ph = PH[ti % 2]
            for dc in range(8):
                S.tr(ph[:, dc * 128:(dc + 1) * 128], hb[:, dc * 128:(dc + 1) * 128], IDB[:])
            S.tt('dve', HT[:, :, ti * 128:(ti + 1) * 128],
                 ph[:, :].rearrange("p (a b) -> p a b", b=128),
                 GPRE[:, l, which, :].to_broadcast([128, 8, 128]), ALU.mult)

    def fm_proj(wtile, c0, rhs_buf, nk, cidx):
        acc = pbank()[:, 0:CH]
        for kc in range(nk):
            S.mm(acc, wtile[:, kc, c0:c0 + 128], rhs_buf[:, kc, cidx * CH:(cidx + 1) * CH],
                 start=(kc == 0), stop=(kc == nk - 1))
        return acc

    def tm_proj(wtile, ti, src_buf, nk=8, ncol=512):
        acc = pbank()[:, 0:ncol]
        for kc in range(nk):
            S.mm(acc, src_buf[:, kc, ti * 128:(ti + 1) * 128], wtile[:, kc, 0:ncol],
                 start=(kc == 0), stop=(kc == nk - 1))
        return acc

    def stage_pool(g, l):
        tiles, npt, NW, has_s = group_info(g)
        W0 = wget(widx[(g, l, 'u')])
        PWL = PWL2[:, l, :, :]
        if has_s:
            S.dma('sp', SPOOLT[:], spoolT[l])
            S.dma('sp', nps[l, :, 0:7, :], spool[l, :, 8:15, :])
            acc = tm_proj(W0, npt - 1, HT)
            S.cp('act', TMROW, acc)
            S.dma('sp', npp[l], TMROW[113:128, :])
            acc = tm_proj(W0, TPG - 1, HT)
            S.cp('act', TMROW, acc)
            for t in range(8):
                S.dma('sp', nps[l, :, 7 + t, :], TMROW[t::8, :])
        for gi in range(4):
            w = POOL_W[gi]
            ue = UEXT[gi % 2]
            S.cp('pool', ue[:, 1:16], PPRE[:, l, gi, :])
            for c in range(NCH):
                acc = fm_proj(W0, gi * 128, HT, 8, c)
                pr, sr = chunk_ranges(c, NW, has_s)
                if pr:
                    S.cp('act', ue[:, 16 + pr[0]:16 + pr[1]], acc[:, pr[0] - c * CH:pr[1] - c * CH])
                if sr:
                    S.cp('act', SEXT[:, :, 15:23], acc[:, CH - 128:CH].rearrange("p (s t) -> p s t", t=8))
            Wd = 16 + NW
            S.cp('pool', PPRE[:, l, gi, :], ue[:, Wd - 15:Wd])
            src = ue
            for j in range(gi + 1):
                k = 1 << j
                sh = (1 << (j + 1))
                dst = PA if j % 2 == 0 else PBf
                S.tt('pool', dst[:, sh:Wd], src[:, sh:Wd], src[:, sh - k:Wd - k], ALU.add)
                src = dst
            S.stt(MTP[:, 0:NW], src[:, 16:Wd], 1.0 / w, ue[:, 16:Wd], ALU.mult, ALU.subtract)
            if g == 0:
                tmp = STAT[:, 32:48]
                S.tt('dve', tmp, src[:, 16 + 112:16 + 128], CST[:, C_INVC + gi * 16:C_INVC + gi * 16 + 16], ALU.mult)
                S.tt('dve', MTP[:, 112:128], tmp, ue[:, 16 + 112:16 + 128], ALU.subtract)
            if has_s:
                S.cp('pool', SEXT[:, :, 0:15], SPOOLT[:, gi, :, :])
                ssrc = SEXT
                for j in range(gi + 1):
                    k = 1 << j
                    sh = (1 << (j + 1)) - 1
                    dst = SPA if j % 2 == 0 else SPB
                    S.tt('pool', dst[:, :, sh:23], ssrc[:, :, sh:23], ssrc[:, :, sh - k:23 - k], ALU.add)
                    ssrc = dst
                S.stt(MTP[:, NW:NT].rearrange("p (s t) -> p s t", t=8), ssrc[:, :, 15:23], 1.0 / w,
                      SEXT[:, :, 15:23], ALU.mult, ALU.subtract)
            for c in range(NCH):
                acc = pbank()[:, 0:CH]
                S.mm(acc, PWL[:, gi, :], MTP[:, c * CH:(c + 1) * CH])
                S.act(AOT[:, gi, c * CH:(c + 1) * CH], acc, AF.Copy, scale=PSC[:, l, gi:gi + 1])

    def stage_sconv_prep(l):
        S.dma('sp', SCONVT[:], sconvT[l])

    def stage_qkv(g, l, hb):
        tiles, npt, NW, has_s = group_info(g)
        for comp in range(3):
            Wt = wget(widx[(g, l, 'qkv', comp, hb)])
            if has_s:
                colbase = comp * 1024 + hb * 512
                acc = tm_proj(Wt, npt - 1, HT)
                S.cp('act', TMROW, acc)
                S.dma('sp', ncp[l, :, colbase:colbase + 512], TMROW[125:128, :])
                acc = tm_proj(Wt, TPG - 1, HT)
                S.cp('act', TMROW, acc)
                for t in range(3):
                    S.dma('sp', ncs[l, :, t, colbase:colbase + 512], TMROW[5 + t::8, :])
            def a1(hl):
                h = hb * 4 + hl
                ch = comp * 8 + h
                c0 = hl * 128
                ext = UEXT[ch % 2]
                S.cp('pool', ext[:, 0:3], CPRE[:, l, ch, :])
                for c in range(NCH):
                    acc = fm_proj(Wt, c0, HT, 8, c)
                    pr, sr = chunk_ranges(c, NW, has_s)
                    if pr:
                        S.cp('act', ext[:, 3 + pr[0]:3 + pr[1]], acc[:, pr[0] - c * CH:pr[1] - c * CH])
                    if sr:
                        S.cp('act', SCEXT[:, :, 3:11], acc[:, CH - 128:CH].rearrange("p (s t) -> p s t", t=8))
                S.cp('pool', CPRE[:, l, ch, :], ext[:, NW:NW + 3])
                if has_s:
                    CQ = CQ2[ch % 2]
                    S.cp('pool', SCEXT[:, :, 0:3], SCONVT[:, ch, :, :])
                    cqs = CQ[:, NW:NT].rearrange("p (s t) -> p s t", t=8)
                    S.ts('dve', cqs, SCEXT[:, :, 0:8], CW[:, l, ch, 0:1], None, ALU.mult)
                    for j in range(1, 4):
                        S.stt(cqs, SCEXT[:, :, j:j + 8], CW[:, l, ch, j:j + 1], cqs, ALU.mult, ALU.add)

            def a2(hl):
                h = hb * 4 + hl
                ch = comp * 8 + h
                ext = UEXT[ch % 2]
                CQ, SQ2, SQ, RN = CQ2[ch % 2], SQ2_2[ch % 2], SQ_4[hl], RN_4[hl]
                S.ts('dve', CQ[:, 0:NW], ext[:, 0:NW], CW[:, l, ch, 0:1], None, ALU.mult)
                for j in range(1, 4):
                    S.stt(CQ[:, 0:NW], ext[:, j:NW + j], CW[:, l, ch, j:j + 1], CQ[:, 0:NW], ALU.mult, ALU.add)
                if comp == 2:
                    S.act(VT4[:, hl, :], CQ, AF.Silu)
                else:
                    S.act(SQ, CQ, AF.Silu)
                    S.tt('pool', SQ2, SQ, SQ, ALU.mult)
                    for c in range(NCH):
                        acc = pbank()[:, 0:CH]
                        S.mm(acc, ones_f, SQ2[:, c * CH:(c + 1) * CH])
                        S.cp('act', RN[:, c * CH:(c + 1) * CH], acc)

            def phase_b(hl):
                SQ, RN = SQ_4[hl], RN_4[hl]
                S.act(RN, RN, AF.Ln, bias=eps_c, scale=1.0)
                S.act(RN, RN, AF.Exp, scale=-0.5)
                dst = QT4 if comp == 0 else KT4
                sc = (128.0 ** -0.5) if comp == 0 else 1.0
                S.stt(dst[:, hl, :], SQ, sc, RN, ALU.mult, ALU.mult)

            a1(0)
            for hl in range(4):
                if hl + 1 < 4:
                    a1(hl + 1)
                a2(hl)
            if comp != 2:
                for hl in range(4):
                    phase_b(hl)

    def stage_ba(g, l):
        WBA = WBA2[:, l, :, :]
        for ti in range(TPG):
            acc = pq()[:, 0:16]
            for kc in range(8):
                S.mm(acc, HT[:, kc, ti * 128:(ti + 1) * 128], WBA[:, kc, :], start=(kc == 0), stop=(kc == 7))
            S.act(BETA[:, ti, :], acc[:, 0:8], AF.Sigmoid)
            S.cp('act', TA[:], acc[:, 8:16])
            S.tt('dve', TA[:], TA[:], DTB[:, l, :], ALU.add)
            S.act(TA[:], TA[:], AF.Exp)
            S.act(TA[:], TA[:], AF.Ln, bias=one_c, scale=1.0)
            S.tt('dve', GG[:, ti, :], TA[:], NEGA[:, l, :], ALU.mult)
        S.ts('pool', NBETA[:], BETA[:], -1.0, None, ALU.mult)

    def stage_z(g, l, hb):
        Wt = wget(widx[(g, l, 'z', hb)])
        for ti in range(TPG):
            acc = tm_proj(Wt, ti, HT)
            S.act(ZS4[:, ti, :], acc, AF.Silu)
            zv = ZS4[:, ti, :].rearrange("p (h v) -> p h v", v=128)
            S.tt('pool', zv, zv, ONG[:, l, :].unsqueeze(1).broadcast_to([128, 4, 128]), ALU.mult)

    def stage_gdn(g, l, hb):
        tiles, npt, NW, has_s = group_info(g)
        for ti, (kind, pidx) in enumerate(tiles):
            isS = (kind == 'S')
            Umat = CST[:, C_US:C_US + 128] if isS else CST[:, C_UP:C_UP + 128]
            SLmat = CST[:, C_SLS:C_SLS + 128] if isS else CST[:, C_SLP:C_SLP + 128]
            SAMEmat = CST[:, C_SAMES:C_SAMES + 128] if isS else ones_f
            NEGmat = CST[:, C_NEGS:C_NEGS + 128] if isS else CST[:, C_NEGP:C_NEGP + 128]
            L = 3 if isS else 6
            tsl = slice(ti * 128, (ti + 1) * 128)
            gc_ps = pq()[:, 0:8]
            S.mm(gc_ps, Umat, GG[:, ti, :])
            gl_ps = pq()[:, 0:8]
            S.mm(gl_ps, SAMEmat, GG[:, ti, :])
            S.act(EGC[:], gc_ps, AF.Exp)
            S.cp('act', GCs[:], gc_ps)
            S.cp('act', EGD[:], gl_ps)
            S.act(EGL[:], gl_ps, AF.Exp)
            S.tt('dve', EGD[:], EGD[:], GCs[:], ALU.subtract)
            S.act(EGD[:], EGD[:], AF.Exp)
            if isS:
                S.tt('pool', GBS[:], GG[:, ti, :].to_broadcast([128, 8, 16]),
                     blk16.unsqueeze(1).broadcast_to([128, 8, 16]), ALU.mult)
                egp = pq()
                S.mm(egp, ones_f, GBS[:, :, :].rearrange("p a b -> p (a b)"))
                S.act(EGLS[:, :, :].rearrange("p a b -> p (a b)"), egp, AF.Exp)
            head_sets = [[0], [1], [2], [3]] if isS else [[0, 1, 2, 3]]
            for hls in head_sets:
                gdn_heads(g, l, hb, ti, isS, hls, Umat, SLmat, NEGmat, L, tsl)
        if g == len(GROUPS) - 1 and hb == 1:
            S.dma('sp', nsp[l].rearrange("h k v -> k h v"), S_ST[:, l, :, :])

    def gdn_heads(g, l, hb, ti, isS, hls, Umat, SLmat, NEGmat, L, tsl):
        H = [(hl, hb * 4 + hl) for hl in hls]
        lock4 = (len(hls) == 4)

        def pqx(hl):
            if not lock4:
                return pq()
            q = hq[hl]
            hq[hl] = (q + 1) % 4
            return PB[hl][:, q * 128:(q + 1) * 128]
        dps = {}
        for hl, h in H:
            S.ts('pool', Bm[hl], SLmat, GG[:, ti, h:h + 1], None, ALU.mult)
        pq_align()
        for hl, h in H:
            d = pqx(hl)
            S.mm(d, Bm[hl], Umat, start=True, stop=False)
            S.mm(d, ident, NEGmat, start=False, stop=True)
            dps[hl] = d
        for hl, h in H:
            S.act(DTS[hl], dps[hl], AF.Exp)
        for hl, h in H:
            S.tt('pool', DTM[hl], DTS[hl], ident, ALU.add)
        gks = {}
        pq_align()
        for hl, h in H:
            gk = pqx(hl)
            S.mm(gk, KT4[:, hl, tsl], KT4[:, hl, tsl])
            gks[hl] = gk
        for hl, h in H:
            S.stt(CA[hl][0], gks[hl], BETA[:, ti, h:h + 1], DTS[hl], ALU.mult, ALU.mult)
        pts = {}
        pq_align()
        pqb_align()
        for hl, h in H:
            p_ = pqx(hl) if CHAIN_F32 else pqb()
            if CHAIN_F32:
                S.trf(p_, CA[hl][0], ident_cd)
            else:
                S.tr(p_, CA[hl][0], ident_cd)
            pts[hl] = p_
        for hl, h in H:
            S.cp('act', CAT[hl][0], pts[hl])
            S.tt('pool', CR[hl][0], ident_cd, CA[hl][0], ALU.subtract)
        kts, vts = {}, {}
        pqb_align()
        for hl, h in H:
            kt_ = pqb()
            S.tr(kt_, KT4[:, hl, tsl], IDB[:])
            kts[hl] = kt_
        pqb_align()
        for hl, h in H:
            vt_ = pqb()
            S.tr(vt_, VT4[:, hl, tsl], IDB[:])
            vts[hl] = vt_
        for hl, h in H:
            S.ts('dve', KE[hl], kts[hl], EGC[:, h:h + 1], None, ALU.mult)
            S.ts('dve', KDEC[hl], kts[hl], EGD[:, h:h + 1], None, ALU.mult)
            S.cp('act', VTM[hl], vts[hl])
        for k in range(1, L + 1):
            a_ps, at_ps = {}, {}
            pq_align()
            if k < L:
                for hl, h in H:
                    a = pqx(hl)
                    S.mm(a, CAT[hl][(k - 1) % 2], CA[hl][(k - 1) % 2])
                    a_ps[hl] = a
                pq_align()
            for hl, h in H:
                at = pqx(hl)
                S.mm(at, CA[hl][(k - 1) % 2], CAT[hl][(k - 1) % 2])
                at_ps[hl] = at
            for hl, h in H:
                if k < L:
                    S.cp('act', CA[hl][k % 2], a_ps[hl])
                S.cp('dve', CAT[hl][k % 2], at_ps[hl])
            r_ps = {}
            pq_align()
            for hl, h in H:
                r = pqx(hl)
                S.mm(r, CAT[hl][k % 2], CR[hl][(k - 1) % 2])
                r_ps[hl] = r
            for hl, h in H:
                S.tt('dve', CR[hl][k % 2], r_ps[hl], CR[hl][(k - 1) % 2], ALU.add)
        XT = {hl: CR[hl][L % 2] for hl, h in H}
        wps, ups = {}, {}
        pq_align()
        for hl, h in H:
            w_ = pqx(hl)
            S.mm(w_, KE[hl], XT[hl])
            wps[hl] = w_
            u_ = pqx(hl)
            S.mm(u_, XT[hl], VTM[hl])
            ups[hl] = u_
        for hl, h in H:
            S.cp('act', WTt[hl], wps[hl])
            S.act(UB[hl], ups[hl], AF.Copy, scale=BETA[:, ti, h:h + 1])
        wss = {}
        if not isS:
            for hl, h in H:
                S.cp('pool', SBF[hl], S_ST[:, l, h, :])
            pq_align()
            for hl, h in H:
                ws = pqx(hl)
                S.mm(ws, WTt[hl], SBF[hl])
                wss[hl] = ws
        else:
            for hl, h in H:
                S.dma('sp', S0F, sssm[l, :, h].rearrange("s k v -> k s v"))
                S.dma('pool', S0B, sssm[l, :, h].rearrange("s k v -> k s v"))
                wsT = pqx(hl)
                for s in range(16):
                    S.mm(wsT[:, 8 * s:8 * s + 8], S0B[:, s, :], WTt[hl][:, 8 * s:8 * s + 8])
                S.cp('act', FMS[0], wsT)
                ws = pqx(hl)
                S.trf(ws, FMS[0], ident)
                wss[hl] = ws
        for hl, h in H:
            S.stt(VN[hl], wss[hl], NBETA[:, ti, h:h + 1], UB[hl], ALU.mult, ALU.add)
        aps = {}
        pq_align()
        for hl, h in H:
            a_ = pqx(hl)
            S.mm(a_, KT4[:, hl, tsl], QT4[:, hl, tsl])
            aps[hl] = a_
        for hl, h in H:
            S.tt('dve', ATt[hl], aps[hl], DTM[hl], ALU.mult)
        avs, qss = {}, {}
        pq_align()
        for hl, h in H:
            av = pqx(hl)
            S.mm(av, ATt[hl], VN[hl])
            avs[hl] = av
        pq_align()
        for hl, h in H:
            if not isS:
                qs = pqx(hl)
                S.mm(qs, QT4[:, hl, tsl], SBF[hl])
            else:
                qsT = pqx(hl)
                for s in range(16):
                    S.mm(qsT[:, 8 * s:8 * s + 8], S0B[:, s, :], QT4[:, hl, ti * 128 + 8 * s:ti * 128 + 8 * s + 8])
                S.cp('act', FMS[1], qsT)
                qs = pqx(hl)
                S.trf(qs, FMS[1], ident)
            qss[hl] = qs
        for hl, h in H:
            S.act(O1[hl], qss[hl], AF.Copy, scale=EGC[:, h:h + 1])
            S.tt('dve', OO[hl], avs[hl], O1[hl], ALU.add)
        for hl, h in H:
            S.act(JG[hl], OO[hl], AF.Square, accum_out=STAT[:, 16 + hl:17 + hl])
        for hl, h in H:
            rstd_from(STAT[:, 24 + hl:25 + hl], STAT[:, 16 + hl:17 + hl], 128)
        dts = {}
        pqb_align()
        for hl, h in H:
            S.stt(DD[hl], OO[hl], STAT[:, 24 + hl:25 + hl], ZS4[:, ti, hl * 128:(hl + 1) * 128], ALU.mult, ALU.mult)
            d_ = pqb()
            S.tr(d_, DD[hl], IDB[:])
            dts[hl] = d_
        for hl, h in H:
            S.cp('act', DOT[:, h, tsl], dts[hl])
        if not isS:
            dss = {}
            pq_align()
            for hl, h in H:
                ds = pqx(hl)
                S.mm(ds, KDEC[hl], VN[hl])
                dss[hl] = ds
            for hl, h in H:
                S.stt(S_ST[:, l, h, :], S_ST[:, l, h, :], EGL[:, h:h + 1], dss[hl], ALU.mult, ALU.add)
        else:
            for hl, h in H:
                S.tt('pool', VBLK, VN[hl].unsqueeze(1).broadcast_to([128, 16, 128]),
                     blk16.to_broadcast([128, 16, 128]), ALU.mult)
                S.tt('pool', S0F, S0F, EGLS[:, h, :].to_broadcast([128, 16, 128]), ALU.mult)
                for q4 in range(4):
                    bk = pbank()
                    S.mm(bk[:, :], KDEC[hl], VBLK[:, 4 * q4:4 * q4 + 4, :].rearrange("p a b -> p (a b)"))
                    sl = S0F[:, 4 * q4:4 * q4 + 4, :].rearrange("p a b -> p (a b)")
                    S.tt('dve', sl, bk[:, :], sl, ALU.add)
                S.dma('sp', nss[l, :, h].rearrange("s k v -> k s v"), S0F)

    def stage_merge(g, l):
        for q in range(4):
            Wg = wget(widx[(g, l, 'mg', q)], min(PREF, 4))
            Wb = wget(widx[(g, l, 'mb', q)], min(PREF, 3))
            for dl in range(2):
                dcn = q * 2 + dl
                for c in range(NCH):
                    i0 = (dl * NCH + c) % 2
                    gp = fm_proj(Wg, dl * 128, HT, 8, c)
                    gd = fm_proj(Wg, 256 + dl * 128, HT, 8, c)
                    brp = fm_proj(Wb, dl * 128, AOT, 4, c)
                    brd = fm_proj(Wb, 256 + dl * 128, DOT, 8, c)
                    S.act(SGT[2 * i0], gp, AF.Sigmoid)
                    S.act(SGT[2 * i0 + 1], gd, AF.Sigmoid)
                    S.tt('dve', T12[2 * i0], brp, SGT[2 * i0], ALU.mult)
                    S.tt('dve', T12[2 * i0 + 1], brd, SGT[2 * i0 + 1], ALU.mult)
                    S.tt('pool', MT[:, dcn, c * CH:(c + 1) * CH], T12[2 * i0], T12[2 * i0 + 1], ALU.add)

    def post_tile(l, which, ti):
        ssq = STAT[:, 48 + ti:49 + ti]
        rs = STAT[:, 56 + ti:57 + ti]
        S.act(JUNK, FF[:, ti, :], AF.Square, accum_out=ssq)
        rstd_from(rs, ssq, D)
        for half in range(2):
            cs = slice(half * 512, (half + 1) * 512)
            S.stt(TMPX[half], FF[:, ti, cs], rs, GPOSTb[:, cs], ALU.mult, ALU.mult)
            S.tt('pool', X[:, ti, cs], X[:, ti, cs], TMPX[half], ALU.add)

    def stage_out(g, l):
        S.dma('sp', GPOSTb, gpost_d[:, l, 0, :])
        for half in range(2):
            Wt = wget(widx[(g, l, 'wo', half)])
            for ti in range(TPG):
                acc = tm_proj(Wt, ti, MT)
                S.cp('act', FF[:, ti, half * 512:(half + 1) * 512], acc)
                if half == 1:
                    post_tile(l, 0, ti)

    def stage_ffn(g, l):
        norm_to_HT(l, 1)
        S.dma('sp', GPOSTb, gpost_d[:, l, 1, :])
        for fh in range(2):
            for f4 in range(4):
                Wt = wget(widx[(g, l, 'up', fh * 4 + f4)])
                for fl in range(4):
                    fi = f4 * 4 + fl
                    for c in range(NCH):
                        acc = fm_proj(Wt, fl * 128, HT, 8, c)
                        rl = RL[(fl * NCH + c) % 2]
                        S.act(rl, acc, AF.Relu)
                        S.tt('pool', UT[:, fi, c * CH:(c + 1) * CH], rl, rl, ALU.mult)
            for half in range(2):
                for kl in range(2):
                    Wt = wget(widx[(g, l, 'dn', half, fh * 2 + kl)])
                    for ti in range(TPG):
                        for kc in range(8):
                            S.mm(PB[ti][:, :], UT[:, kl * 8 + kc, ti * 128:(ti + 1) * 128], Wt[:, kc, :],
                                 start=(kl == 0 and kc == 0), stop=(kl == 1 and kc == 7))
                for ti in range(TPG):
                    dst = FF[:, ti, half * 512:(half + 1) * 512]
                    if fh == 0:
                        S.cp('act', dst, PB[ti][:, :])
                    else:
                        S.tt('dve', dst, PB[ti][:, :], dst, ALU.add)
                        if half == 1:
                            post_tile(l, 1, ti)

    dbg = DEBUG
    def on(name):
        return dbg is None or name in dbg['stages']
    if dbg is None:
        wissue(PREF)
    for g in range(len(GROUPS)):
        if dbg is not None and g not in dbg['groups']:
            continue
        tiles, npt, NW, has_s = group_info(g)
        load_x(g)
        for l in range(NL):
            if dbg is not None and l not in dbg['layers']:
                continue
            if on('norm'):
                norm_to_HT(l, 0)
            if on('pool'):
                stage_pool(g, l)
            if on('ba'):
                stage_ba(g, l)
            if has_s and on('qkv'):
                stage_sconv_prep(l)
            for hb in range(2):
                if on('qkv'):
                    stage_qkv(g, l, hb)
                if on('z'):
                    stage_z(g, l, hb)
                if on('gdn'):
                    stage_gdn(g, l, hb)
            if on('merge'):
                stage_merge(g, l)
            if on('out'):
                stage_out(g, l)
            if on('ffn'):
                stage_ffn(g, l)
        store_x(g)
    if dbg is None:
        assert wuse[0] == len(wlist)
    S.emit()
    return nc, S


_CACHE = {}


def kernel(x_prompt, x_sample, state_conv, state_ssm, state_pool, meta_tokens, g_pre_mix, w_in, conv_w, a_log,
           dt_bias, o_norm_g, pool_w, pool_scale, w_branch_pool, w_branch_delta, w_out, g_post_mix, g_pre_ffn,
           w_up, w_down, g_post_ffn):
    f = lambda a: np.ascontiguousarray(np.asarray(a, dtype=np.float32))
    if 'nc' not in _CACHE:
        _CACHE['nc'] = build_program()[0]
    nc = _CACHE['nc']
    x_prompt, x_sample, state_conv, state_ssm, state_pool = map(f, (x_prompt, x_sample, state_conv, state_ssm, state_pool))
    gpre = np.stack([f(g_pre_mix), f(g_pre_ffn)], axis=1)
    gpre = f(gpre.reshape(NL, 2, 8, 128).transpose(3, 0, 1, 2))
    cw = f(f(conv_w).reshape(NL, 4, 24, 128).transpose(3, 0, 2, 1))
    psc = f(f(pool_scale).reshape(NL, 4, 128).transpose(2, 0, 1))
    gpost = np.stack([f(g_post_mix), f(g_post_ffn)], axis=1)
    gpost = f(np.broadcast_to(gpost[None], (128, NL, 2, D)))
    ong = f(np.broadcast_to(f(o_norm_g)[None], (128, NL, 128)))
    alog = f(np.broadcast_to(f(a_log)[None], (128, NL, 8)))
    dtb = f(np.broadcast_to(f(dt_bias)[None], (128, NL, 8)))
    shared = {
        "meta": f(meta_tokens), "w_in": f(w_in), "pool_w": f(pool_w), "wbp": f(w_branch_pool),
        "wbd": f(w_branch_delta), "wo": f(w_out), "wup": f(w_up), "wdn": f(w_down),
        "consts": make_consts(), "gpre": gpre, "cw": cw, "pscale": psc, "gpost": gpost, "ong": ong,
        "alog": alog, "dtb": dtb,
    }
    in_maps = []
    for c in range(8):
        m = dict(shared)
        m["xp"] = x_prompt[c]
        m["xs"] = f(x_sample[16 * c:16 * c + 16].reshape(128, D))
        m["sconv"] = f(state_conv[:, 16 * c:16 * c + 16])
        m["sssm"] = f(state_ssm[:, 16 * c:16 * c + 16])
        m["spool"] = f(state_pool[:, 16 * c:16 * c + 16])
        m["spoolT"] = f(m["spool"].reshape(NL, 16, 15, 4, 128).transpose(0, 4, 3, 1, 2))
        m["sconvT"] = f(m["sconv"].reshape(NL, 16, 3, 24, 128).transpose(0, 4, 3, 1, 2))
        in_maps.append(m)
    res = run_bass_kernel_spmd(nc, in_maps, core_ids=list(range(8)))
    R = res.results
    y_prompt = np.stack([R[c]["y_p"] for c in range(8)], axis=0)
    y_sample = np.concatenate([R[c]["y_s"].reshape(16, 8, D) for c in range(8)], axis=0)
    ncp = np.stack([R[c]["ncp"] for c in range(8)], axis=1)
    nsp = np.stack([R[c]["nsp"] for c in range(8)], axis=1)
    npp = np.stack([R[c]["npp"] for c in range(8)], axis=1)
    ncs = np.concatenate([R[c]["ncs"] for c in range(8)], axis=1)
    nss = np.concatenate([R[c]["nss"] for c in range(8)], axis=1)
    nps = np.concatenate([R[c]["nps"] for c in range(8)], axis=1)
    return tuple(np.ascontiguousarray(a, dtype=np.float32) for a in (y_prompt, y_sample, ncp, nsp, npp, ncs, nss, nps))
```

```python
import numpy as np
import concourse.bass as bass
import concourse.mybir as mybir
from concourse.bass_utils import run_bass_kernel_spmd

F32 = mybir.dt.float32
BF16 = mybir.dt.bfloat16
AF = mybir.ActivationFunctionType
ALU = mybir.AluOpType

ENGS = ('pe', 'act', 'dve', 'pool', 'sp')
MAXOPS = None


class _Op:
    __slots__ = ('eng', 'fn', 'waits', 'signal', 'dma', 'idx')

    def __init__(self, eng, fn, dma=None):
        self.eng = eng
        self.fn = fn
        self.waits = []
        self.signal = False
        self.dma = dma
        self.idx = 0


class Sched:
    NDMA = 24

    def __init__(self, nc):
        self.nc = nc
        self.ops = {e: [] for e in ENGS}
        self.recs = {}
        self.water = {e: {} for e in ENGS}
        self.dma_tot = [0] * self.NDMA
        self.dma_last = [None] * self.NDMA
        self.dma_rr = 0
        self.dma_rr_sw = 0
        self.tensors = []
        self.same_eng_dist = 1 << 30

    def sbuf(self, name, shape, dtype):
        t = self.nc.alloc_sbuf_tensor(name, list(shape), dtype)
        return t

    def psum(self, name, shape, dtype):
        return self.nc.alloc_psum_tensor(name, list(shape), dtype)

    @staticmethod
    def _box(ap):
        t = ap.tensor
        shp = list(t.shape)
        row = 1
        for s in shp[1:]:
            row *= s
        dsz = mybir.dt.size(ap.dtype)
        pat = list(ap.ap)
        off = ap.offset
        p0 = off // row
        f0 = (off % row) * dsz
        pstep, pcnt = pat[0]
        if pstep == 0:
            np_ = 1
        else:
            np_ = (pcnt - 1) * (pstep // row) + 1
        ext = 0
        for st, cnt in pat[1:]:
            ext += (cnt - 1) * abs(st)
        if type(t).__name__ == 'PSumTensorHandle':
            return (p0, p0 + np_, 0, 1 << 20)
        return (p0, p0 + np_, f0, f0 + (ext + 1) * dsz)

    def _track(self, op, aps_r, aps_w):
        deps = set()
        for kind, aps in (('r', aps_r), ('w', aps_w)):
            for ap in aps:
                name = ap.tensor.name
                box = self._box(ap)
                if type(ap.tensor).__name__ == 'PSumTensorHandle':
                    kind = 'w'
                recs = self.recs.setdefault(name, {})
                dead = []
                for (b, k, key), tok in recs.items():
                    if b[0] < box[1] and box[0] < b[1] and b[2] < box[3] and box[2] < b[3]:
                        if kind == 'w' or k == 'w':
                            deps.add(tok)
                        if kind == 'w' and box[0] <= b[0] and b[1] <= box[1] and box[2] <= b[2] and b[3] <= box[3]:
                            dead.append((b, k, key))
                for d in dead:
                    del recs[d]
        return deps

    def _record(self, op, tok, aps_r, aps_w):
        for kind, aps in (('r', aps_r), ('w', aps_w)):
            for ap in aps:
                name = ap.tensor.name
                box = self._box(ap)
                if type(ap.tensor).__name__ == 'PSumTensorHandle':
                    kind = 'w'
                key = tok[0] if tok[0] != 'dma' else ('dma', tok[1])
                self.recs.setdefault(name, {})[(box, kind, key)] = tok

    def _add_waits(self, op, deps):
        eng = op.eng
        my_idx = len(self.ops[eng])
        for tok in deps:
            if tok[0] == 'dma':
                _, s, val = tok
                key = ('dma', s)
                if self.water[eng].get(key, 0) >= val:
                    continue
                self.water[eng][key] = val
                op.waits.append(tok)
            else:
                f, i = tok
                if f == eng:
                    if eng in ('pe', 'sp'):
                        continue
                    if my_idx - i > self.same_eng_dist:
                        continue
                if self.water[eng].get(f, -1) >= i:
                    continue
                self.water[eng][f] = i
                self.ops[f][i].signal = True
                op.waits.append(tok)

    @staticmethod
    def _is_onchip(ap):
        return type(ap.tensor).__name__ in ('SBTensorHandle', 'PSumTensorHandle')

    def op(self, eng, fn, r, w):
        self.nrec = getattr(self, 'nrec', 0) + 1
        if MAXOPS is not None and self.nrec > MAXOPS:
            return None
        o = _Op(eng, fn)
        r = [a for a in r if self._is_onchip(a)]
        w = [a for a in w if self._is_onchip(a)]
        deps = self._track(o, r, w)
        self._add_waits(o, deps)
        o.idx = len(self.ops[eng])
        self.ops[eng].append(o)
        self._record(o, (eng, o.idx), r, w)
        return o

    def dma(self, queue, out, in_):
        self.nrec = getattr(self, 'nrec', 0) + 1
        if MAXOPS is not None and self.nrec > MAXOPS:
            return None
        if queue == 'pool':
            s = 16 + self.dma_rr_sw
            self.dma_rr_sw = (self.dma_rr_sw + 1) % 8
        else:
            s = self.dma_rr
            self.dma_rr = (self.dma_rr + 1) % 16
        o = _Op(queue, None, dma=(s, out, in_))
        r = [in_] if self._is_onchip(in_) else []
        w = [out] if self._is_onchip(out) else []
        deps = self._track(o, r, w)
        if self.dma_last[s] is not None:
            deps.add(self.dma_last[s])
        self._add_waits(o, deps)
        self.dma_tot[s] += 16
        tok = ('dma', s, self.dma_tot[s])
        self.dma_last[s] = tok
        o.idx = len(self.ops[queue])
        self.ops[queue].append(o)
        self._record(o, tok, r, w)
        return tok

    def mm(self, out, lhsT, rhs, start=True, stop=True):
        return self.op('pe', lambda e: e.matmul(out, lhsT, rhs, start=start, stop=stop), [lhsT, rhs], [out])

    def tr(self, out, in_, ident):
        return self.op('pe', lambda e: e.transpose(out, in_, ident), [in_, ident], [out])

    def trf(self, out, in_, ident):
        return self.op('pe', lambda e: e.matmul(out, in_, ident, start=True, stop=True), [in_, ident], [out])

    def act(self, out, in_, func, bias=None, scale=None, accum_out=None, eng='act'):
        kw = {}
        r = [in_]
        w = [out]
        if bias is not None:
            kw['bias'] = bias
            if not isinstance(bias, (int, float)):
                r.append(bias)
        if scale is not None:
            kw['scale'] = scale
            if not isinstance(scale, (int, float)):
                r.append(scale)
        if accum_out is not None:
            kw['accum_out'] = accum_out
            w.append(accum_out)
        return self.op('act', lambda e: e.activation(out, in_, func, **kw), r, w)

    def tt(self, eng, out, in0, in1, op):
        return self.op(eng, lambda e: e.tensor_tensor(out, in0, in1, op), [in0, in1], [out])

    def ts(self, eng, out, in0, s1, s2, op0, op1=None, accum_out=None):
        r = [in0]
        if not isinstance(s1, (int, float)) and s1 is not None:
            r.append(s1)
        if not isinstance(s2, (int, float)) and s2 is not None:
            r.append(s2)
        w = [out]
        kw = {}
        if op1 is not None:
            kw['op1'] = op1
        if accum_out is not None:
            kw['accum_out'] = accum_out
            w.append(accum_out)
        return self.op(eng, lambda e: e.tensor_scalar(out, in0, s1, s2, op0, **kw), r, w)

    def stt(self, out, in0, scalar, in1, op0, op1, eng='dve'):
        r = [in0, in1]
        if not isinstance(scalar, (int, float)):
            r.append(scalar)
        return self.op(eng, lambda e: e.scalar_tensor_tensor(out, in0, scalar, in1, op0, op1), r, [out])

    def cp(self, eng, out, in_):
        if eng == 'act':
            return self.op('act', lambda e: e.copy(out, in_), [in_], [out])
        return self.op(eng, lambda e: e.tensor_copy(out, in_), [in_], [out])

    def memset(self, eng, ap, val):
        return self.op(eng, lambda e: e.memset(ap, val), [], [ap])

    def emit(self, final_wait=True):
        nc = self.nc
        engobj = {'pe': nc.tensor, 'act': nc.scalar, 'dve': nc.vector, 'pool': nc.gpsimd, 'sp': nc.sync}
        sems = {e: nc.alloc_semaphore("prog_" + e) for e in ENGS}
        dsems = [nc.alloc_semaphore("dma_%d" % i) for i in range(self.NDMA)]
        cnt = {}
        for e in ENGS:
            c = 0
            arr = []
            for o in self.ops[e]:
                if o.signal and o.dma is None:
                    c += 1
                arr.append(c)
            cnt[e] = arr
        blockattr = {'pe': 'tensor', 'act': 'scalar', 'dve': 'vector', 'pool': 'gpsimd', 'sp': 'sync'}
        with nc.Block() as block:
            for e in ENGS:
                ops = self.ops[e]

                def body(eng, e=e, ops=ops):
                    for o in ops:
                        for tok in o.waits:
                            if tok[0] == 'dma':
                                eng.wait_ge(dsems[tok[1]], tok[2])
                            else:
                                eng.wait_ge(sems[tok[0]], cnt[tok[0]][tok[1]])
                        if o.dma is not None:
                            s, out, in_ = o.dma
                            eng.dma_start(out=out, in_=in_).then_inc(dsems[s], 16)
                        else:
                            ins = o.fn(eng)
                            if o.signal:
                                ins.then_inc(sems[e], 1)
                    if e == 'sp' and final_wait:
                        for s in range(self.NDMA):
                            if self.dma_tot[s] > 0:
                                eng.wait_ge(dsems[s], self.dma_tot[s])
                getattr(block, blockattr[e])(body)


D = 1024
NH = 8
IN_W = 6672
DFF = 4096
NL = 2
TPG = 6
NT = TPG * 128
CH = 384
NCH = NT // CH
EPS = 1e-6
POOL_W = (2, 4, 8, 16)
C_ID, C_UP, C_US, C_SLP, C_SLS, C_SAMES, C_ONES, C_NEGP, C_NEGS, C_BLK, C_INVC = (
    0, 128, 256, 384, 512, 640, 768, 896, 1024, 1152, 1168)
NCONST = 1232
GROUPS = [[('P', i) for i in range(0, 6)], [('P', i) for i in range(6, 12)],
          [('P', i) for i in range(12, 17)] + [('S', 0)]]
CHAIN_F32 = True
DEBUG = None


def make_consts():
    c = np.zeros((128, NCONST), np.float32)
    p = np.arange(128)
    same = (p[:, None] // 8) == (p[None, :] // 8)
    c[:, C_ID:C_ID + 128] = np.eye(128)
    c[:, C_UP:C_UP + 128] = (p[:, None] <= p[None, :])
    c[:, C_US:C_US + 128] = (p[:, None] <= p[None, :]) & same
    c[:, C_SLP:C_SLP + 128] = (p[:, None] > p[None, :])
    c[:, C_SLS:C_SLS + 128] = (p[:, None] > p[None, :]) & same
    c[:, C_SAMES:C_SAMES + 128] = same
    c[:, C_ONES:C_ONES + 128] = 1.0
    c[:, C_NEGP:C_NEGP + 128] = np.where(p[:, None] < p[None, :], 0.0, -30000.0)
    c[:, C_NEGS:C_NEGS + 128] = np.where((p[:, None] < p[None, :]) & same, 0.0, -30000.0)
    c[:, C_BLK:C_BLK + 16] = (p[:, None] // 8) == np.arange(16)[None, :]
    for gi, w in enumerate(POOL_W):
        for pos in range(16):
            c[:, C_INVC + gi * 16 + pos] = 1.0 / min(pos + 1, w)
    return c


def build_program():
    nc = bass.Bass("TRN2", target_bir_lowering=False)

    def din(name, shape):
        return nc.dram_tensor(name, list(shape), F32, kind="ExternalInput").ap()

    def dout(name, shape):
        return nc.dram_tensor(name, list(shape), F32, kind="ExternalOutput").ap()

    xp = din("xp", [2048, D]); xs = din("xs", [128, D]); meta = din("meta", [16, D])
    sconv = din("sconv", [NL, 16, 3, 3072]); sssm = din("sssm", [NL, 16, NH, 128, 128])
    spool = din("spool", [NL, 16, 15, 512])
    spoolT = din("spoolT", [NL, 128, 4, 16, 15]); sconvT = din("sconvT", [NL, 128, 24, 16, 3])
    w_in = din("w_in", [NL, D, IN_W]); pool_w = din("pool_w", [NL, 4, 128, 128])
    wbp = din("wbp", [NL, 512, D]); wbd = din("wbd", [NL, D, D]); wo = din("wo", [NL, D, D])
    wup = din("wup", [NL, D, DFF]); wdn = din("wdn", [NL, DFF, D])
    consts = din("consts", [128, NCONST]); gpre_d = din("gpre", [128, NL, 2, 8])
    cw_d = din("cw", [128, NL, 24, 4]); pscale_d = din("pscale", [128, NL, 4])
    gpost_d = din("gpost", [128, NL, 2, D]); ong_d = din("ong", [128, NL, 128])
    alog_d = din("alog", [128, NL, 8]); dtb_d = din("dtb", [128, NL, 8])
    y_p = dout("y_p", [2048, D]); y_s = dout("y_s", [128, D])
    ncp = dout("ncp", [NL, 3, 3072]); nsp = dout("nsp", [NL, NH, 128, 128]); npp = dout("npp", [NL, 15, 512])
    ncs = dout("ncs", [NL, 16, 3, 3072]); nss = dout("nss", [NL, 16, NH, 128, 128]); nps = dout("nps", [NL, 16, 15, 512])

    S = Sched(nc)
    CD = F32 if CHAIN_F32 else BF16

    X = S.sbuf("X", [128, TPG, D], F32)
    HT = S.sbuf("HT", [128, 8, NT], BF16)
    BIG = S.sbuf("BIG", [128, 16 * NT], BF16)
    UT = BIG[:, :].rearrange("p (a b) -> p a b", b=NT)
    QT4 = BIG[:, 0:4 * NT].rearrange("p (a b) -> p a b", b=NT)
    KT4 = BIG[:, 4 * NT:8 * NT].rearrange("p (a b) -> p a b", b=NT)
    VT4 = BIG[:, 8 * NT:12 * NT].rearrange("p (a b) -> p a b", b=NT)
    ZS4 = BIG[:, 12 * NT:16 * NT].rearrange("p (t c) -> p t c", c=512)
    MT = BIG[:, 0:8 * NT].rearrange("p (a b) -> p a b", b=NT)
    DM = S.sbuf("DM", [128, 16 * NT], BF16)
    DOT = DM[:, 0:8 * NT].rearrange("p (a b) -> p a b", b=NT)
    AOT = DM[:, 8 * NT:12 * NT].rearrange("p (a b) -> p a b", b=NT)
    FF = DM[:, :].bitcast(F32).rearrange("p (t c) -> p t c", c=D)
    NSLOT = 5
    PREF = 4
    WR = [S.sbuf("WR%d" % i, [128, 8, 512], BF16) for i in range(NSLOT)]
    CST = S.sbuf("CST", [128, NCONST], F32)
    IDB = S.sbuf("IDB", [128, 128], BF16)
    GPRE = S.sbuf("GPRE", [128, NL, 2, 8], F32)
    CW = S.sbuf("CW", [128, NL, 24, 4], F32)
    PSC = S.sbuf("PSC", [128, NL, 4], F32)
    ONG = S.sbuf("ONG", [128, NL, 128], F32)
    NEGA = S.sbuf("NEGA", [128, NL, 8], F32)
    DTB = S.sbuf("DTB", [128, NL, 8], F32)
    EPSC = S.sbuf("EPSC", [128, 2], F32)
    S_ST = S.sbuf("S_ST", [128, NL, NH, 128], F32)
    CPRE = S.sbuf("CPRE", [128, NL, 24, 3], F32)
    PPRE = S.sbuf("PPRE", [128, NL, 4, 15], F32)
    PWL2 = S.sbuf("PWL2", [128, NL, 4, 128], BF16)
    WBA2 = S.sbuf("WBA2", [128, NL, 8, 16], BF16)
    STAT = S.sbuf("STAT", [128, 64], F32)
    BETA = S.sbuf("BETA", [128, TPG, 8], F32)
    NBETA = S.sbuf("NBETA", [128, TPG, 8], F32)
    GG = S.sbuf("GG", [128, TPG, 8], F32)
    TA = S.sbuf("TA", [128, 8], F32)
    EGC = S.sbuf("EGC", [128, 8], F32)
    EGD = S.sbuf("EGD", [128, 8], F32)
    EGL = S.sbuf("EGL", [128, 8], F32)
    GCs = S.sbuf("GCs", [128, 8], F32)
    GBS = S.sbuf("GBS", [128, 8, 16], F32)
    EGLS = S.sbuf("EGLS", [128, 8, 16], F32)
    SPOOLT = S.sbuf("SPOOLT", [128, 4, 16, 15], F32)
    SCONVT = S.sbuf("SCONVT", [128, 24, 16, 3], F32)

    SCR_BYTES = 52 * 1024
    SCR = S.sbuf("SCR", [128, SCR_BYTES // 2], BF16)

    class Arena:
        def __init__(self):
            self.off = 0

        def take(self, free, dtype):
            n = 1
            for f in free:
                n *= f
            nb = n * mybir.dt.size(dtype)
            nb_al = (nb + 31) // 32 * 32
            assert self.off + nb_al <= SCR_BYTES, ("arena overflow", self.off, nb_al)
            ap = SCR[:, self.off // 2:(self.off + nb) // 2]
            self.off += nb_al
            if dtype != BF16:
                ap = ap.bitcast(dtype)
            if len(free) == 2:
                ap = ap.rearrange("p (a b) -> p a b", b=free[1])
            elif len(free) == 3:
                ap = ap.rearrange("p (a b c) -> p a b c", b=free[1], c=free[2])
            return ap

    EXTW = 16 + NT
    A = Arena()
    JUNK = A.take([D], BF16)
    HB = [A.take([D], BF16) for _ in range(2)]
    offA0 = A.off
    UEXT = [A.take([EXTW], F32) for _ in range(2)]
    SCEXT = A.take([16, 11], F32)
    TMROW = A.take([512], F32)
    offA1 = A.off
    PA = A.take([EXTW], F32)
    PBf = A.take([EXTW], F32)
    MTP = A.take([NT], BF16)
    SEXT = A.take([16, 23], F32)
    SPA = A.take([16, 23], F32)
    SPB = A.take([16, 23], F32)
    A.off = offA1
    CQ2 = [A.take([NT], F32) for _ in range(2)]
    SQ2_2 = [A.take([NT], F32) for _ in range(2)]
    SQ_4 = [A.take([NT], F32) for _ in range(4)]
    RN_4 = [A.take([NT], F32) for _ in range(4)]
    A.off = offA0
    GPOSTb = A.take([D], F32)
    TMPX = [A.take([512], F32) for _ in range(2)]
    SGT = [A.take([CH], F32) for _ in range(4)]
    T12 = [A.take([CH], F32) for _ in range(4)]
    RL = [A.take([CH], F32) for _ in range(2)]
    G = Arena()
    NB = 4
    Bm = [G.take([128], F32) for _ in range(NB)]
    DTS = [G.take([128], F32) for _ in range(NB)]
    DTM = [G.take([128], F32) for _ in range(NB)]
    CA = [[G.take([128], CD) for _ in range(2)] for _ in range(NB)]
    CAT = [[G.take([128], CD) for _ in range(2)] for _ in range(NB)]
    CR = [[G.take([128], CD) for _ in range(2)] for _ in range(NB)]
    KE = [G.take([128], CD) for _ in range(NB)]
    KDEC = [G.take([128], BF16) for _ in range(NB)]
    VTM = [G.take([128], CD) for _ in range(NB)]
    WTt = [G.take([128], BF16) for _ in range(NB)]
    UB = [G.take([128], F32) for _ in range(NB)]
    VN = [G.take([128], BF16) for _ in range(NB)]
    ATt = [G.take([128], BF16) for _ in range(NB)]
    O1 = [G.take([128], F32) for _ in range(NB)]
    OO = [G.take([128], F32) for _ in range(NB)]
    DD = [G.take([128], BF16) for _ in range(NB)]
    SBF = [G.take([128], BF16) for _ in range(NB)]
    JG = [G.take([128], BF16) for _ in range(NB)]
    FMS = [G.take([128], F32) for _ in range(2)]
    S0F = G.take([16, 128], F32)
    S0B = G.take([16, 128], BF16)
    VBLK = G.take([16, 128], BF16)

    PB = [S.psum("PB%d" % i, [128, 512], F32) for i in range(6)]
    PH = [S.psum("PH%d" % i, [128, 1024], BF16) for i in range(2)]
    qctr = [0]

    def pq_align():
        qctr[0] = (qctr[0] + 3) // 4 * 4 % 8

    def pq():
        i = qctr[0]
        qctr[0] = (i + 1) % 8
        return PB[4 + i // 4][:, (i % 4) * 128:(i % 4 + 1) * 128]
    bctr = [0]

    def pbank():
        i = bctr[0]
        bctr[0] = (i + 1) % 6
        return PB[i]
    hctr = [0]
    hq = [0, 0, 0, 0]

    def pqb_align():
        hctr[0] = (hctr[0] + 3) // 4 * 4 % 16

    def pqb():
        i = hctr[0]
        hctr[0] = (i + 1) % 16
        slot = (i % 4) + 4 * ((i // 8) % 2)
        return PH[(i // 4) % 2][:, slot * 128:(slot + 1) * 128]

    ident = CST[:, C_ID:C_ID + 128]
    ident_cd = ident if CHAIN_F32 else IDB[:, :]
    ones_f = CST[:, C_ONES:C_ONES + 128]
    blk16 = CST[:, C_BLK:C_BLK + 16]
    eps_c = EPSC[:, 0:1]
    one_c = EPSC[:, 1:2]

    wlist = []

    def wsrc_k8(wt, l, c0, ncols=512):
        return wt[l, :, c0:c0 + ncols].rearrange("(dc p) c -> p dc c", p=128)

    wstate = {'issued': 0}

    def wdma(n):
        for (src, kc, coff, ncol) in wlist[n]:
            S.dma('pool', WR[n % NSLOT][:, 0:kc, coff:coff + ncol], src)

    def wissue(upto):
        while wstate['issued'] <= upto and wstate['issued'] < len(wlist):
            wdma(wstate['issued'])
            wstate['issued'] += 1

    wuse = [0]

    def wget(n, pref=None):
        pref = PREF if pref is None else pref
        if DEBUG is None:
            assert n == wuse[0], ("weight order mismatch", n, wuse[0])
        wuse[0] += 1
        if DEBUG is None:
            wissue(n + pref)
        else:
            wdma(n)
        return WR[n % NSLOT]

    widx = {}
    for g in range(len(GROUPS)):
        for l in range(NL):
            def add(key, src, kc=8, ncol=512):
                widx[(g, l) + key] = len(wlist)
                wlist.append([(src, kc, 0, ncol)])

            def add2(key, parts):
                widx[(g, l) + key] = len(wlist)
                wlist.append(parts)
            add(('u',), wsrc_k8(w_in, l, 0))
            for hb in range(2):
                for comp in range(3):
                    add(('qkv', comp, hb), wsrc_k8(w_in, l, 512 + comp * 1024 + hb * 512))
                add(('z', hb), wsrc_k8(w_in, l, 3600 + hb * 512))
            for q in range(4):
                add2(('mg', q), [(wsrc_k8(w_in, l, 4624 + q * 256, 256), 8, 0, 256),
                                 (wsrc_k8(w_in, l, 5648 + q * 256, 256), 8, 256, 256)])
                add2(('mb', q), [(wbp[l, :, q * 256:(q + 1) * 256].rearrange("(dc p) c -> p dc c", p=128), 4, 0, 256),
                                 (wsrc_k8(wbd, l, q * 256, 256), 8, 256, 256)])
            for half in range(2):
                add(('wo', half), wsrc_k8(wo, l, half * 512))
            for fh in range(2):
                for f4 in range(4):
                    add(('up', fh * 4 + f4), wsrc_k8(wup, l, (fh * 4 + f4) * 512))
                for half in range(2):
                    for kl in range(2):
                        kcg = fh * 2 + kl
                        add(('dn', half, kcg),
                            wdn[l, kcg * 1024:(kcg + 1) * 1024, half * 512:(half + 1) * 512].rearrange("(dc p) c -> p dc c", p=128))

    S.dma('sp', CST[:], consts)
    for l_ in range(NL):
        S.dma('pool', PWL2[:, l_, :, :], pool_w[l_].rearrange("g c d -> c g d"))
        S.dma('pool', WBA2[:, l_, :, :], w_in[l_, :, 3584:3600].rearrange("(dc p) c -> p dc c", p=128))
    S.dma('sp', GPRE[:], gpre_d)
    S.dma('sp', CW[:], cw_d)
    S.dma('sp', PSC[:], pscale_d)
    S.dma('sp', ONG[:], ong_d)
    S.dma('sp', NEGA[:], alog_d)
    S.dma('sp', DTB[:], dtb_d)
    S.cp('dve', IDB[:], ident)
    S.memset('dve', EPSC[:, 0:1], EPS)
    S.memset('dve', EPSC[:, 1:2], 1.0)
    S.act(NEGA[:], NEGA[:], AF.Exp)
    S.ts('pool', NEGA[:], NEGA[:], -1.0, None, ALU.mult)
    S.memset('pool', S_ST[:], 0.0)
    S.memset('pool', CPRE[:], 0.0)
    S.memset('pool', PPRE[:], 0.0)

    def rstd_from(out, ssq, n):
        S.act(out, ssq, AF.Ln, bias=eps_c, scale=1.0 / n)
        S.act(out, out, AF.Exp, scale=-0.5)

    def group_info(g):
        tiles = GROUPS[g]
        npt = sum(1 for t in tiles if t[0] == 'P')
        has_s = any(t[0] == 'S' for t in tiles)
        return tiles, npt, npt * 128, has_s

    def chunk_ranges(c, NW, has_s):
        a, b = c * CH, min((c + 1) * CH, NW)
        pr = (a, b) if b > a else None
        sr = has_s and (c + 1) * CH == NT
        return pr, sr

    def load_x(g):
        tiles, npt, NW, has_s = group_info(g)
        ti = 0
        if tiles[0] == ('P', 0):
            S.memset('pool', X[:, 0, :], 0.0)
            S.dma('sp', X[112:128, 0, :], meta)
            ti = 1
        for tj in range(ti, npt):
            pt = tiles[tj][1]
            S.dma('sp', X[:, tj, :], xp[(pt - 1) * 128:pt * 128, :])
        if has_s:
            S.dma('sp', X[:, TPG - 1, :], xs)

    def store_x(g):
        tiles, npt, NW, has_s = group_info(g)
        ti = 1 if tiles[0] == ('P', 0) else 0
        for tj in range(ti, npt):
            pt = tiles[tj][1]
            S.dma('sp', y_p[(pt - 1) * 128:pt * 128, :], X[:, tj, :])
        if has_s:
            S.dma('sp', y_s, X[:, TPG - 1, :])

    def norm_to_HT(l, which):
        for ti in range(TPG):
            ssq = STAT[:, ti:ti + 1]
            rs = STAT[:, 8 + ti:9 + ti]
            S.act(JUNK, X[:, ti, :], AF.Square, accum_out=ssq)
            rstd_from(rs, ssq, D)
            hb = HB[ti % 2]
            S.ts('dve', hb, X[:, ti, :], rs, None, ALU.mult)
            ph = PH[ti % 2]
            for dc in range(8):
                S.tr(ph[:, dc * 128:(dc + 1) * 128], hb[:, dc * 128:(dc + 1) * 128], IDB[:])
            S.tt('dve', HT[:, :, ti * 128:(ti + 1) * 128],
                 ph[:, :].rearrange("p (a b) -> p a b", b=128),
                 GPRE[:, l, which, :].to_broadcast([128, 8, 128]), ALU.mult)

    def fm_proj(wtile, c0, rhs_buf, nk, cidx):
        acc = pbank()[:, 0:CH]
        for kc in range(nk):
            S.mm(acc, wtile[:, kc, c0:c0 + 128], rhs_buf[:, kc, cidx * CH:(cidx + 1) * CH],
                 start=(kc == 0), stop=(kc == nk - 1))
        return acc

    def tm_proj(wtile, ti, src_buf, nk=8, ncol=512):
        acc = pbank()[:, 0:ncol]
        for kc in range(nk):
            S.mm(acc, src_buf[:, kc, ti * 128:(ti + 1) * 128], wtile[:, kc, 0:ncol],
                 start=(kc == 0), stop=(kc == nk - 1))
        return acc

    def stage_pool(g, l):
        tiles, npt, NW, has_s = group_info(g)
        W0 = wget(widx[(g, l, 'u')])
        PWL = PWL2[:, l, :, :]
        if has_s:
            S.dma('sp', SPOOLT[:], spoolT[l])
            S.dma('sp', nps[l, :, 0:7, :], spool[l, :, 8:15, :])
            acc = tm_proj(W0, npt - 1, HT)
            S.cp('act', TMROW, acc)
            S.dma('sp', npp[l], TMROW[113:128, :])
            acc = tm_proj(W0, TPG - 1, HT)
            S.cp('act', TMROW, acc)
            for t in range(8):
                S.dma('sp', nps[l, :, 7 + t, :], TMROW[t::8, :])
        for gi in range(4):
            w = POOL_W[gi]
            ue = UEXT[gi % 2]
            S.cp('pool', ue[:, 1:16], PPRE[:, l, gi, :])
            for c in range(NCH):
                acc = fm_proj(W0, gi * 128, HT, 8, c)
                pr, sr = chunk_ranges(c, NW, has_s)
                if pr:
                    S.cp('act', ue[:, 16 + pr[0]:16 + pr[1]], acc[:, pr[0] - c * CH:pr[1] - c * CH])
                if sr:
                    S.cp('act', SEXT[:, :, 15:23], acc[:, CH - 128:CH].rearrange("p (s t) -> p s t", t=8))
            Wd = 16 + NW
            S.cp('pool', PPRE[:, l, gi, :], ue[:, Wd - 15:Wd])
            src = ue
            for j in range(gi + 1):
                k = 1 << j
                sh = (1 << (j + 1))
                dst = PA if j % 2 == 0 else PBf
                S.tt('pool', dst[:, sh:Wd], src[:, sh:Wd], src[:, sh - k:Wd - k], ALU.add)
                src = dst
            S.stt(MTP[:, 0:NW], src[:, 16:Wd], 1.0 / w, ue[:, 16:Wd], ALU.mult, ALU.subtract)
            if g == 0:
                tmp = STAT[:, 32:48]
                S.tt('dve', tmp, src[:, 16 + 112:16 + 128], CST[:, C_INVC + gi * 16:C_INVC + gi * 16 + 16], ALU.mult)
                S.tt('dve', MTP[:, 112:128], tmp, ue[:, 16 + 112:16 + 128], ALU.subtract)
            if has_s:
                S.cp('pool', SEXT[:, :, 0:15], SPOOLT[:, gi, :, :])
                ssrc = SEXT
                for j in range(gi + 1):
                    k = 1 << j
                    sh = (1 << (j + 1)) - 1
                    dst = SPA if j % 2 == 0 else SPB
                    S.tt('pool', dst[:, :, sh:23], ssrc[:, :, sh:23], ssrc[:, :, sh - k:23 - k], ALU.add)
                    ssrc = dst
                S.stt(MTP[:, NW:NT].rearrange("p (s t) -> p s t", t=8), ssrc[:, :, 15:23], 1.0 / w,
                      SEXT[:, :, 15:23], ALU.mult, ALU.subtract)
            for c in range(NCH):
                acc = pbank()[:, 0:CH]
                S.mm(acc, PWL[:, gi, :], MTP[:, c * CH:(c + 1) * CH])
                S.act(AOT[:, gi, c * CH:(c + 1) * CH], acc, AF.Copy, scale=PSC[:, l, gi:gi + 1])

    def stage_sconv_prep(l):
        S.dma('sp', SCONVT[:], sconvT[l])

    def stage_qkv(g, l, hb):
        tiles, npt, NW, has_s = group_info(g)
        for comp in range(3):
            Wt = wget(widx[(g, l, 'qkv', comp, hb)])
            if has_s:
                colbase = comp * 1024 + hb * 512
                acc = tm_proj(Wt, npt - 1, HT)
                S.cp('act', TMROW, acc)
                S.dma('sp', ncp[l, :, colbase:colbase + 512], TMROW[125:128, :])
                acc = tm_proj(Wt, TPG - 1, HT)
                S.cp('act', TMROW, acc)
                for t in range(3):
                    S.dma('sp', ncs[l, :, t, colbase:colbase + 512], TMROW[5 + t::8, :])
            def a1(hl):
                h = hb * 4 + hl
                ch = comp * 8 + h
                c0 = hl * 128
                ext = UEXT[ch % 2]
                S.cp('pool', ext[:, 0:3], CPRE[:, l, ch, :])
                for c in range(NCH):
                    acc = fm_proj(Wt, c0, HT, 8, c)
                    pr, sr = chunk_ranges(c, NW, has_s)
                    if pr:
                        S.cp('act', ext[:, 3 + pr[0]:3 + pr[1]], acc[:, pr[0] - c * CH:pr[1] - c * CH])
                    if sr:
                        S.cp('act', SCEXT[:, :, 3:11], acc[:, CH - 128:CH].rearrange("p (s t) -> p s t", t=8))
                S.cp('pool', CPRE[:, l, ch, :], ext[:, NW:NW + 3])
                if has_s:
                    CQ = CQ2[ch % 2]
                    S.cp('pool', SCEXT[:, :, 0:3], SCONVT[:, ch, :, :])
                    cqs = CQ[:, NW:NT].rearrange("p (s t) -> p s t", t=8)
                    S.ts('dve', cqs, SCEXT[:, :, 0:8], CW[:, l, ch, 0:1], None, ALU.mult)
                    for j in range(1, 4):
                        S.stt(cqs, SCEXT[:, :, j:j + 8], CW[:, l, ch, j:j + 1], cqs, ALU.mult, ALU.add)

            def a2(hl):
                h = hb * 4 + hl
                ch = comp * 8 + h
                ext = UEXT[ch % 2]
                CQ, SQ2, SQ, RN = CQ2[ch % 2], SQ2_2[ch % 2], SQ_4[hl], RN_4[hl]
                S.ts('dve', CQ[:, 0:NW], ext[:, 0:NW], CW[:, l, ch, 0:1], None, ALU.mult)
                for j in range(1, 4):
                    S.stt(CQ[:, 0:NW], ext[:, j:NW + j], CW[:, l, ch, j:j + 1], CQ[:, 0:NW], ALU.mult, ALU.add)
                if comp == 2:
                    S.act(VT4[:, hl, :], CQ, AF.Silu)
                else:
                    S.act(SQ, CQ, AF.Silu)
                    S.tt('pool', SQ2, SQ, SQ, ALU.mult)
                    for c in range(NCH):
                        acc = pbank()[:, 0:CH]
                        S.mm(acc, ones_f, SQ2[:, c * CH:(c + 1) * CH])
                        S.cp('act', RN[:, c * CH:(c + 1) * CH], acc)

            def phase_b(hl):
                SQ, RN = SQ_4[hl], RN_4[hl]
                S.act(RN, RN, AF.Ln, bias=eps_c, scale=1.0)
                S.act(RN, RN, AF.Exp, scale=-0.5)
                dst = QT4 if comp == 0 else KT4
                sc = (128.0 ** -0.5) if comp == 0 else 1.0
                S.stt(dst[:, hl, :], SQ, sc, RN, ALU.mult, ALU.mult)

            a1(0)
            for hl in range(4):
                if hl + 1 < 4:
                    a1(hl + 1)
                a2(hl)
            if comp != 2:
                for hl in range(4):
                    phase_b(hl)

    def stage_ba(g, l):
        WBA = WBA2[:, l, :, :]
        for ti in range(TPG):
            acc = pq()[:, 0:16]
            for kc in range(8):
                S.mm(acc, HT[:, kc, ti * 128:(ti + 1) * 128], WBA[:, kc, :], start=(kc == 0), stop=(kc == 7))
            S.act(BETA[:, ti, :], acc[:, 0:8], AF.Sigmoid)
            S.cp('act', TA[:], acc[:, 8:16])
            S.tt('dve', TA[:], TA[:], DTB[:, l, :], ALU.add)
            S.act(TA[:], TA[:], AF.Exp)
            S.act(TA[:], TA[:], AF.Ln, bias=one_c, scale=1.0)
            S.tt('dve', GG[:, ti, :], TA[:], NEGA[:, l, :], ALU.mult)
        S.ts('pool', NBETA[:], BETA[:], -1.0, None, ALU.mult)

    def stage_z(g, l, hb):
        Wt = wget(widx[(g, l, 'z', hb)])
        for ti in range(TPG):
            acc = tm_proj(Wt, ti, HT)
            S.act(ZS4[:, ti, :], acc, AF.Silu)
            zv = ZS4[:, ti, :].rearrange("p (h v) -> p h v", v=128)
            S.tt('pool', zv, zv, ONG[:, l, :].unsqueeze(1).broadcast_to([128, 4, 128]), ALU.mult)

    def stage_gdn(g, l, hb):
        tiles, npt, NW, has_s = group_info(g)
        for ti, (kind, pidx) in enumerate(tiles):
            isS = (kind == 'S')
            Umat = CST[:, C_US:C_US + 128] if isS else CST[:, C_UP:C_UP + 128]
            SLmat = CST[:, C_SLS:C_SLS + 128] if isS else CST[:, C_SLP:C_SLP + 128]
            SAMEmat = CST[:, C_SAMES:C_SAMES + 128] if isS else ones_f
            NEGmat = CST[:, C_NEGS:C_NEGS + 128] if isS else CST[:, C_NEGP:C_NEGP + 128]
            L = 3 if isS else 6
            tsl = slice(ti * 128, (ti + 1) * 128)
            gc_ps = pq()[:, 0:8]
            S.mm(gc_ps, Umat, GG[:, ti, :])
            gl_ps = pq()[:, 0:8]
            S.mm(gl_ps, SAMEmat, GG[:, ti, :])
            S.act(EGC[:], gc_ps, AF.Exp)
            S.cp('act', GCs[:], gc_ps)
            S.cp('act', EGD[:], gl_ps)
            S.act(EGL[:], gl_ps, AF.Exp)
            S.tt('dve', EGD[:], EGD[:], GCs[:], ALU.subtract)
            S.act(EGD[:], EGD[:], AF.Exp)
            if isS:
                S.tt('pool', GBS[:], GG[:, ti, :].to_broadcast([128, 8, 16]),
                     blk16.unsqueeze(1).broadcast_to([128, 8, 16]), ALU.mult)
                egp = pq()
                S.mm(egp, ones_f, GBS[:, :, :].rearrange("p a b -> p (a b)"))
                S.act(EGLS[:, :, :].rearrange("p a b -> p (a b)"), egp, AF.Exp)
            head_sets = [[0], [1], [2], [3]] if isS else [[0, 1, 2, 3]]
            for hls in head_sets:
                gdn_heads(g, l, hb, ti, isS, hls, Umat, SLmat, NEGmat, L, tsl)
        if g == len(GROUPS) - 1 and hb == 1:
            S.dma('sp', nsp[l].rearrange("h k v -> k h v"), S_ST[:, l, :, :])

    def gdn_heads(g, l, hb, ti, isS, hls, Umat, SLmat, NEGmat, L, tsl):
        H = [(hl, hb * 4 + hl) for hl in hls]
        lock4 = (len(hls) == 4)

        def pqx(hl):
            if not lock4:
                return pq()
            q = hq[hl]
            hq[hl] = (q + 1) % 4
            return PB[hl][:, q * 128:(q + 1) * 128]
        dps = {}
        for hl, h in H:
            S.ts('pool', Bm[hl], SLmat, GG[:, ti, h:h + 1], None, ALU.mult)
        pq_align()
        for hl, h in H:
            d = pqx(hl)
            S.mm(d, Bm[hl], Umat, start=True, stop=False)
            S.mm(d, ident, NEGmat, start=False, stop=True)
            dps[hl] = d
        for hl, h in H:
            S.act(DTS[hl], dps[hl], AF.Exp)
        for hl, h in H:
            S.tt('pool', DTM[hl], DTS[hl], ident, ALU.add)
        gks = {}
        pq_align()
        for hl, h in H:
            gk = pqx(hl)
            S.mm(gk, KT4[:, hl, tsl], KT4[:, hl, tsl])
            gks[hl] = gk
        for hl, h in H:
            S.stt(CA[hl][0], gks[hl], BETA[:, ti, h:h + 1], DTS[hl], ALU.mult, ALU.mult)
        pts = {}
        pq_align()
        pqb_align()
        for hl, h in H:
            p_ = pqx(hl) if CHAIN_F32 else pqb()
            if CHAIN_F32:
                S.trf(p_, CA[hl][0], ident_cd)
            else:
                S.tr(p_, CA[hl][0], ident_cd)
            pts[hl] = p_
        for hl, h in H:
            S.cp('act', CAT[hl][0], pts[hl])
            S.tt('pool', CR[hl][0], ident_cd, CA[hl][0], ALU.subtract)
        kts, vts = {}, {}
        pqb_align()
        for hl, h in H:
            kt_ = pqb()
            S.tr(kt_, KT4[:, hl, tsl], IDB[:])
            kts[hl] = kt_
        pqb_align()
        for hl, h in H:
            vt_ = pqb()
            S.tr(vt_, VT4[:, hl, tsl], IDB[:])
            vts[hl] = vt_
        for hl, h in H:
            S.ts('dve', KE[hl], kts[hl], EGC[:, h:h + 1], None, ALU.mult)
            S.ts('dve', KDEC[hl], kts[hl], EGD[:, h:h + 1], None, ALU.mult)
            S.cp('act', VTM[hl], vts[hl])
        for k in range(1, L + 1):
            a_ps, at_ps = {}, {}
            pq_align()
            if k < L:
                for hl, h in H:
                    a = pqx(hl)
                    S.mm(a, CAT[hl][(k - 1) % 2], CA[hl][(k - 1) % 2])
                    a_ps[hl] = a
                pq_align()
            for hl, h in H:
                at = pqx(hl)
                S.mm(at, CA[hl][(k - 1) % 2], CAT[hl][(k - 1) % 2])
                at_ps[hl] = at
            for hl, h in H:
                if k < L:
                    S.cp('act', CA[hl][k % 2], a_ps[hl])
                S.cp('dve', CAT[hl][k % 2], at_ps[hl])
            r_ps = {}
            pq_align()
            for hl, h in H:
                r = pqx(hl)
                S.mm(r, CAT[hl][k % 2], CR[hl][(k - 1) % 2])
                r_ps[hl] = r
            for hl, h in H:
                S.tt('dve', CR[hl][k % 2], r_ps[hl], CR[hl][(k - 1) % 2], ALU.add)
        XT = {hl: CR[hl][L % 2] for hl, h in H}
        wps, ups = {}, {}
        pq_align()
        for hl, h in H:
            w_ = pqx(hl)
            S.mm(w_, KE[hl], XT[hl])
            wps[hl] = w_
            u_ = pqx(hl)
            S.mm(u_, XT[hl], VTM[hl])
            ups[hl] = u_
        for hl, h in H:
            S.cp('act', WTt[hl], wps[hl])
            S.act(UB[hl], ups[hl], AF.Copy, scale=BETA[:, ti, h:h + 1])
        wss = {}
        if not isS:
            for hl, h in H:
                S.cp('pool', SBF[hl], S_ST[:, l, h, :])
            pq_align()
            for hl, h in H:
                ws = pqx(hl)
                S.mm(ws, WTt[hl], SBF[hl])
                wss[hl] = ws
        else:
            for hl, h in H:
                S.dma('sp', S0F, sssm[l, :, h].rearrange("s k v -> k s v"))
                S.dma('pool', S0B, sssm[l, :, h].rearrange("s k v -> k s v"))
                wsT = pqx(hl)
                for s in range(16):
                    S.mm(wsT[:, 8 * s:8 * s + 8], S0B[:, s, :], WTt[hl][:, 8 * s:8 * s + 8])
                S.cp('act', FMS[0], wsT)
                ws = pqx(hl)
                S.trf(ws, FMS[0], ident)
                wss[hl] = ws
        for hl, h in H:
            S.stt(VN[hl], wss[hl], NBETA[:, ti, h:h + 1], UB[hl], ALU.mult, ALU.add)
        aps = {}
        pq_align()
        for hl, h in H:
            a_ = pqx(hl)
            S.mm(a_, KT4[:, hl, tsl], QT4[:, hl, tsl])
            aps[hl] = a_
        for hl, h in H:
            S.tt('dve', ATt[hl], aps[hl], DTM[hl], ALU.mult)
        avs, qss = {}, {}
        pq_align()
        for hl, h in H:
            av = pqx(hl)
            S.mm(av, ATt[hl], VN[hl])
            avs[hl] = av
        pq_align()
        for hl, h in H:
            if not isS:
                qs = pqx(hl)
                S.mm(qs, QT4[:, hl, tsl], SBF[hl])
            else:
                qsT = pqx(hl)
                for s in range(16):
                    S.mm(qsT[:, 8 * s:8 * s + 8], S0B[:, s, :], QT4[:, hl, ti * 128 + 8 * s:ti * 128 + 8 * s + 8])
                S.cp('act', FMS[1], qsT)
                qs = pqx(hl)
                S.trf(qs, FMS[1], ident)
            qss[hl] = qs
        for hl, h in H:
            S.act(O1[hl], qss[hl], AF.Copy, scale=EGC[:, h:h + 1])
            S.tt('dve', OO[hl], avs[hl], O1[hl], ALU.add)
        for hl, h in H:
            S.act(JG[hl], OO[hl], AF.Square, accum_out=STAT[:, 16 + hl:17 + hl])
        for hl, h in H:
            rstd_from(STAT[:, 24 + hl:25 + hl], STAT[:, 16 + hl:17 + hl], 128)
        dts = {}
        pqb_align()
        for hl, h in H:
            S.stt(DD[hl], OO[hl], STAT[:, 24 + hl:25 + hl], ZS4[:, ti, hl * 128:(hl + 1) * 128], ALU.mult, ALU.mult)
            d_ = pqb()
            S.tr(d_, DD[hl], IDB[:])
            dts[hl] = d_
        for hl, h in H:
            S.cp('act', DOT[:, h, tsl], dts[hl])
        if not isS:
            dss = {}
            pq_align()
            for hl, h in H:
                ds = pqx(hl)
                S.mm(ds, KDEC[hl], VN[hl])
                dss[hl] = ds
            for hl, h in H:
                S.stt(S_ST[:, l, h, :], S_ST[:, l, h, :], EGL[:, h:h + 1], dss[hl], ALU.mult, ALU.add)
        else:
            for hl, h in H:
                S.tt('pool', VBLK, VN[hl].unsqueeze(1).broadcast_to([128, 16, 128]),
                     blk16.to_broadcast([128, 16, 128]), ALU.mult)
                S.tt('pool', S0F, S0F, EGLS[:, h, :].to_broadcast([128, 16, 128]), ALU.mult)
                for q4 in range(4):
                    bk = pbank()
                    S.mm(bk[:, :], KDEC[hl], VBLK[:, 4 * q4:4 * q4 + 4, :].rearrange("p a b -> p (a b)"))
                    sl = S0F[:, 4 * q4:4 * q4 + 4, :].rearrange("p a b -> p (a b)")
                    S.tt('dve', sl, bk[:, :], sl, ALU.add)
                S.dma('sp', nss[l, :, h].rearrange("s k v -> k s v"), S0F)

    def stage_merge(g, l):
        for q in range(4):
            Wg = wget(widx[(g, l, 'mg', q)], min(PREF, 4))
            Wb = wget(widx[(g, l, 'mb', q)], min(PREF, 3))
            for dl in range(2):
                dcn = q * 2 + dl
                for c in range(NCH):
                    i0 = (dl * NCH + c) % 2
                    gp = fm_proj(Wg, dl * 128, HT, 8, c)
                    gd = fm_proj(Wg, 256 + dl * 128, HT, 8, c)
                    brp = fm_proj(Wb, dl * 128, AOT, 4, c)
                    brd = fm_proj(Wb, 256 + dl * 128, DOT, 8, c)
                    S.act(SGT[2 * i0], gp, AF.Sigmoid)
                    S.act(SGT[2 * i0 + 1], gd, AF.Sigmoid)
                    S.tt('dve', T12[2 * i0], brp, SGT[2 * i0], ALU.mult)
                    S.tt('dve', T12[2 * i0 + 1], brd, SGT[2 * i0 + 1], ALU.mult)
                    S.tt('pool', MT[:, dcn, c * CH:(c + 1) * CH], T12[2 * i0], T12[2 * i0 + 1], ALU.add)

    def post_tile(l, which, ti):
        ssq = STAT[:, 48 + ti:49 + ti]
        rs = STAT[:, 56 + ti:57 + ti]
        S.act(JUNK, FF[:, ti, :], AF.Square, accum_out=ssq)
        rstd_from(rs, ssq, D)
        for half in range(2):
            cs = slice(half * 512, (half + 1) * 512)
            S.stt(TMPX[half], FF[:, ti, cs], rs, GPOSTb[:, cs], ALU.mult, ALU.mult)
            S.tt('pool', X[:, ti, cs], X[:, ti, cs], TMPX[half], ALU.add)

    def stage_out(g, l):
        S.dma('sp', GPOSTb, gpost_d[:, l, 0, :])
        for half in range(2):
            Wt = wget(widx[(g, l, 'wo', half)])
            for ti in range(TPG):
                acc = tm_proj(Wt, ti, MT)
                S.cp('act', FF[:, ti, half * 512:(half + 1) * 512], acc)
                if half == 1:
                    post_tile(l, 0, ti)

    def stage_ffn(g, l):
        norm_to_HT(l, 1)
        S.dma('sp', GPOSTb, gpost_d[:, l, 1, :])
        for fh in range(2):
            for f4 in range(4):
                Wt = wget(widx[(g, l, 'up', fh * 4 + f4)])
                for fl in range(4):
                    fi = f4 * 4 + fl
                    for c in range(NCH):
                        acc = fm_proj(Wt, fl * 128, HT, 8, c)
                        rl = RL[(fl * NCH + c) % 2]
                        S.act(rl, acc, AF.Relu)
                        S.tt('pool', UT[:, fi, c * CH:(c + 1) * CH], rl, rl, ALU.mult)
            for half in range(2):
                for kl in range(2):
                    Wt = wget(widx[(g, l, 'dn', half, fh * 2 + kl)])
                    for ti in range(TPG):
                        for kc in range(8):
                            S.mm(PB[ti][:, :], UT[:, kl * 8 + kc, ti * 128:(ti + 1) * 128], Wt[:, kc, :],
                                 start=(kl == 0 and kc == 0), stop=(kl == 1 and kc == 7))
                for ti in range(TPG):
                    dst = FF[:, ti, half * 512:(half + 1) * 512]
                    if fh == 0:
                        S.cp('act', dst, PB[ti][:, :])
                    else:
                        S.tt('dve', dst, PB[ti][:, :], dst, ALU.add)
                        if half == 1:
                            post_tile(l, 1, ti)

    dbg = DEBUG
    def on(name):
        return dbg is None or name in dbg['stages']
    if dbg is None:
        wissue(PREF)
    for g in range(len(GROUPS)):
        if dbg is not None and g not in dbg['groups']:
            continue
        tiles, npt, NW, has_s = group_info(g)
        load_x(g)
        for l in range(NL):
            if dbg is not None and l not in dbg['layers']:
                continue
            if on('norm'):
                norm_to_HT(l, 0)
            if on('pool'):
                stage_pool(g, l)
            if on('ba'):
                stage_ba(g, l)
            if has_s and on('qkv'):
                stage_sconv_prep(l)
            for hb in range(2):
                if on('qkv'):
                    stage_qkv(g, l, hb)
                if on('z'):
                    stage_z(g, l, hb)
                if on('gdn'):
                    stage_gdn(g, l, hb)
            if on('merge'):
                stage_merge(g, l)
            if on('out'):
                stage_out(g, l)
            if on('ffn'):
                stage_ffn(g, l)
        store_x(g)
    if dbg is None:
        assert wuse[0] == len(wlist)
    S.emit()
    return nc, S


_CACHE = {}


def kernel(x_prompt, x_sample, state_conv, state_ssm, state_pool, meta_tokens, g_pre_mix, w_in, conv_w, a_log,
           dt_bias, o_norm_g, pool_w, pool_scale, w_branch_pool, w_branch_delta, w_out, g_post_mix, g_pre_ffn,
           w_up, w_down, g_post_ffn):
    f = lambda a: np.ascontiguousarray(np.asarray(a, dtype=np.float32))
    if 'nc' not in _CACHE:
        _CACHE['nc'] = build_program()[0]
    nc = _CACHE['nc']
    x_prompt, x_sample, state_conv, state_ssm, state_pool = map(f, (x_prompt, x_sample, state_conv, state_ssm, state_pool))
    gpre = np.stack([f(g_pre_mix), f(g_pre_ffn)], axis=1)
    gpre = f(gpre.reshape(NL, 2, 8, 128).transpose(3, 0, 1, 2))
    cw = f(f(conv_w).reshape(NL, 4, 24, 128).transpose(3, 0, 2, 1))
    psc = f(f(pool_scale).reshape(NL, 4, 128).transpose(2, 0, 1))
    gpost = np.stack([f(g_post_mix), f(g_post_ffn)], axis=1)
    gpost = f(np.broadcast_to(gpost[None], (128, NL, 2, D)))
    ong = f(np.broadcast_to(f(o_norm_g)[None], (128, NL, 128)))
    alog = f(np.broadcast_to(f(a_log)[None], (128, NL, 8)))
    dtb = f(np.broadcast_to(f(dt_bias)[None], (128, NL, 8)))
    shared = {
        "meta": f(meta_tokens), "w_in": f(w_in), "pool_w": f(pool_w), "wbp": f(w_branch_pool),
        "wbd": f(w_branch_delta), "wo": f(w_out), "wup": f(w_up), "wdn": f(w_down),
        "consts": make_consts(), "gpre": gpre, "cw": cw, "pscale": psc, "gpost": gpost, "ong": ong,
        "alog": alog, "dtb": dtb,
    }
    in_maps = []
    for c in range(8):
        m = dict(shared)
        m["xp"] = x_prompt[c]
        m["xs"] = f(x_sample[16 * c:16 * c + 16].reshape(128, D))
        m["sconv"] = f(state_conv[:, 16 * c:16 * c + 16])
        m["sssm"] = f(state_ssm[:, 16 * c:16 * c + 16])
        m["spool"] = f(state_pool[:, 16 * c:16 * c + 16])
        m["spoolT"] = f(m["spool"].reshape(NL, 16, 15, 4, 128).transpose(0, 4, 3, 1, 2))
        m["sconvT"] = f(m["sconv"].reshape(NL, 16, 3, 24, 128).transpose(0, 4, 3, 1, 2))
        in_maps.append(m)
    res = run_bass_kernel_spmd(nc, in_maps, core_ids=list(range(8)))
    R = res.results
    y_prompt = np.stack([R[c]["y_p"] for c in range(8)], axis=0)
    y_sample = np.concatenate([R[c]["y_s"].reshape(16, 8, D) for c in range(8)], axis=0)
    ncp = np.stack([R[c]["ncp"] for c in range(8)], axis=1)
    nsp = np.stack([R[c]["nsp"] for c in range(8)], axis=1)
    npp = np.stack([R[c]["npp"] for c in range(8)], axis=1)
    ncs = np.concatenate([R[c]["ncs"] for c in range(8)], axis=1)
    nss = np.concatenate([R[c]["nss"] for c in range(8)], axis=1)
    nps = np.concatenate([R[c]["nps"] for c in range(8)], axis=1)
    return tuple(np.ascontiguousarray(a, dtype=np.float32) for a in (y_prompt, y_sample, ncp, nsp, npp, ncs, nss, nps))
```
